# Optimizing a Trainium2 kernel written in Bass

```python
import jax, jax.numpy as jnp
from jax import lax
import numpy as np

D_MODEL = 1024
BATCH = 2
SEQ = 8192
DEPTH = 2

GRID_W = 64
CTX_LEN = 256
N_MIXERS = 2
GLA_HEADS = 4
GLA_DK = D_MODEL // 2
GLA_DV = D_MODEL
GLA_DK_HEAD = GLA_DK // GLA_HEADS
GLA_DV_HEAD = GLA_DV // GLA_HEADS
GLA_LOW_RANK = 16
GLA_GATE_NORM = 16.0
GLA_CHUNK = 64
ROPE_BASE = 10000.0
NA_HEADS = 16
NA_HEAD_DIM = D_MODEL // NA_HEADS
NA_KH = 8
NA_KW = 16
D_FF = 2816
CONV_W = 3
EPS = 1e-6

kernel_name = "hybrid_gla_natten_convffn_prefix_dit"


def rms_norm(x, g):
    xf = x.astype(jnp.float32)
    y = xf * lax.rsqrt(jnp.mean(xf * xf, axis=-1, keepdims=True) + EPS)
    return (y * g.astype(jnp.float32)).astype(x.dtype)


def modulate(x, g, shift, scale):
    return rms_norm(x, g) * (1 + scale) + shift


def rope_2d(x, rows, cols):
    half = x.shape[-1] // 2
    inv = 1.0 / (ROPE_BASE ** (jnp.arange(0, half, 2, dtype=jnp.float32) / half))

    def rot(xa, pos):
        ang = pos.astype(jnp.float32)[:, None] * inv[None, :]
        cos = jnp.cos(ang)[:, None, :].astype(x.dtype)
        sin = jnp.sin(ang)[:, None, :].astype(x.dtype)
        x1, x2 = jnp.split(xa, 2, axis=-1)
        return jnp.concatenate([x1 * cos - x2 * sin, x1 * sin + x2 * cos], axis=-1)

    return jnp.concatenate([rot(x[..., :half], rows), rot(x[..., half:], cols)], axis=-1)


def gla_chunked(q, k, v, g, s0):
    B, H, T, dk = q.shape
    dv = v.shape[-1]
    n = T // GLA_CHUNK

    def to_chunks(t):
        return jnp.moveaxis(t.reshape(B, H, n, GLA_CHUNK, t.shape[-1]), 2, 0)

    mask = jnp.tril(jnp.ones((GLA_CHUNK, GLA_CHUNK), dtype=bool))[:, :, None]

    def step(s, xs):
        qc, kc, vc, gc = xs
        b = jnp.cumsum(gc, axis=2)
        b_last = b[:, :, -1:, :]
        o_inter = jnp.einsum('bhid,bhde->bhie', qc * jnp.exp(b), s)
        diff = jnp.where(mask, b[:, :, :, None, :] - b[:, :, None, :, :], -jnp.inf)
        att = jnp.einsum('bhid,bhjd,bhijd->bhij', qc, kc, jnp.exp(diff))
        o = o_inter + jnp.einsum('bhij,bhje->bhie', att, vc)
        s_new = jnp.exp(b_last)[:, :, 0, :, None] * s + jnp.einsum('bhjd,bhje->bhde', kc * jnp.exp(b_last - b), vc)
        return s_new, o

    s_final, o = lax.scan(step, s0, (to_chunks(q), to_chunks(k), to_chunks(v), to_chunks(g)))
    o = jnp.moveaxis(o, 0, 2).reshape(B, H, T, dv)
    return o, s_final


def gla_mixer(h_ctx, h_lat, w_in, a_w1, a_w2, a_b, norm_g, w_out, with_ctx_out):
    B, N, _ = h_lat.shape
    L = h_ctx.shape[1]
    T = L + N
    h = jnp.concatenate([h_ctx, h_lat], axis=1)
    proj = h @ w_in
    q, k, v, r = jnp.split(proj, [GLA_DK, 2 * GLA_DK, 2 * GLA_DK + GLA_DV], axis=-1)
    q = q.reshape(B, T, GLA_HEADS, GLA_DK_HEAD)
    k = k.reshape(B, T, GLA_HEADS, GLA_DK_HEAD)
    v = v.reshape(B, T, GLA_HEADS, GLA_DV_HEAD)
    pos = jnp.arange(N)
    rows, cols = pos // GRID_W, pos % GRID_W
    q = jnp.concatenate([q[:, :L], rope_2d(q[:, L:], rows, cols)], axis=1) * (GLA_DK_HEAD ** -0.5)
    k = jnp.concatenate([k[:, :L], rope_2d(k[:, L:], rows, cols)], axis=1)
    to_bhtd = lambda t: jnp.transpose(t, (0, 2, 1, 3))
    q, k, v = to_bhtd(q), to_bhtd(k), to_bhtd(v)
    s0 = jnp.zeros((B, GLA_HEADS, GLA_DK_HEAD, GLA_DV_HEAD), dtype=q.dtype)
    flip = lambda t: jnp.flip(t, axis=2)

    o_sum = None
    for d in range(2):
        g = jax.nn.log_sigmoid((h @ a_w1[d]) @ a_w2[d] + a_b[d]) / GLA_GATE_NORM
        g = to_bhtd(g.reshape(B, T, GLA_HEADS, GLA_DK_HEAD))
        qd, kd, vd, gd = q, k, v, g
        if d == 1:
            qd, kd, vd, gd = flip(q), flip(k), flip(v), flip(g)
            o_c, s_c = gla_chunked(qd[:, :, N:], kd[:, :, N:], vd[:, :, N:], gd[:, :, N:], s0)
            o_l, _ = gla_chunked(qd[:, :, :N], kd[:, :, :N], vd[:, :, :N], gd[:, :, :N], s_c)
            o_d = jnp.concatenate([flip(o_c), flip(o_l)], axis=2)
        else:
            o_c, s_c = gla_chunked(qd[:, :, :L], kd[:, :, :L], vd[:, :, :L], gd[:, :, :L], s0)
            o_l, _ = gla_chunked(qd[:, :, L:], kd[:, :, L:], vd[:, :, L:], gd[:, :, L:], s_c)
            o_d = jnp.concatenate([o_c, o_l], axis=2)
        o_sum = o_d if o_sum is None else o_sum + o_d

    o = jnp.transpose(o_sum, (0, 2, 1, 3))
    r = r.reshape(B, T, GLA_HEADS, GLA_DV_HEAD)
    if not with_ctx_out:
        o, r = o[:, L:], r[:, L:]
    y = (rms_norm(o, norm_g) * jax.nn.silu(r)).reshape(B, -1, GLA_DV) @ w_out
    if with_ctx_out:
        return y[:, :L], y[:, L:]
    return None, y


def na_mixer(h_ctx, h_lat, w_qkv, rpb, w_out, with_ctx_out):
    B, N, _ = h_lat.shape
    L = h_ctx.shape[1]
    rows_n = N // GRID_W
    kh = min(NA_KH, rows_n)
    q_l, k_l, v_l = jnp.split(h_lat @ w_qkv, 3, axis=-1)
    q_c, k_c, v_c = jnp.split(h_ctx @ w_qkv, 3, axis=-1)
    sc = NA_HEAD_DIM ** -0.5
    q_g = (q_l * sc).reshape(B, rows_n, GRID_W, NA_HEADS, NA_HEAD_DIM)
    k_g = k_l.reshape(B, rows_n, GRID_W, NA_HEADS, NA_HEAD_DIM)
    v_g = v_l.reshape(B, rows_n, GRID_W, NA_HEADS, NA_HEAD_DIM)
    k_c = k_c.reshape(B, L, NA_HEADS, NA_HEAD_DIM)
    v_c = v_c.reshape(B, L, NA_HEADS, NA_HEAD_DIM)

    cpos = np.arange(GRID_W)
    col_start = np.clip(cpos - NA_KW // 2, 0, GRID_W - NA_KW)
    col_idx = col_start[:, None] + np.arange(NA_KW)[None, :]
    dc_idx = col_idx - cpos[:, None] + (NA_KW - 1)
    rpb_c = rpb[:, :, dc_idx]

    def row_block(r):
        rs = jnp.clip(r - kh // 2, 0, rows_n - kh)
        q_r = lax.dynamic_index_in_dim(q_g, r, axis=1, keepdims=False)
        k_rows = lax.dynamic_slice_in_dim(k_g, rs, kh, axis=1)
        v_rows = lax.dynamic_slice_in_dim(v_g, rs, kh, axis=1)
        k_win = k_rows[:, :, col_idx]
        v_win = v_rows[:, :, col_idx]
        s_nb = jnp.einsum('bchd,bicjhd->bhcij', q_r, k_win).reshape(B, NA_HEADS, GRID_W, kh * NA_KW)
        dr_idx = rs + jnp.arange(kh) - r + (NA_KH - 1)
        bias = jnp.take(rpb_c, dr_idx, axis=1)
        bias = jnp.transpose(bias, (0, 2, 1, 3)).reshape(NA_HEADS, GRID_W, kh * NA_KW)
        s_nb = s_nb + bias[None]
        s_cx = jnp.einsum('bchd,blhd->bhcl', q_r, k_c)
        p = jax.nn.softmax(jnp.concatenate([s_nb, s_cx], axis=-1).astype(jnp.float32), axis=-1).astype(v_l.dtype)
        p_nb = p[..., :kh * NA_KW].reshape(B, NA_HEADS, GRID_W, kh, NA_KW)
        p_cx = p[..., kh * NA_KW:]
        return jnp.einsum('bhcij,bicjhd->bchd', p_nb, v_win) + jnp.einsum('bhcl,blhd->bchd', p_cx, v_c)

    o = lax.map(row_block, jnp.arange(rows_n))
    o_lat = jnp.transpose(o, (1, 0, 2, 3, 4)).reshape(B, N, D_MODEL) @ w_out
    if not with_ctx_out:
        return None, o_lat
    qc = (q_c * sc).reshape(B, L, NA_HEADS, NA_HEAD_DIM)
    s = jnp.einsum('blhd,bmhd->bhlm', qc, k_c)
    p = jax.nn.softmax(s.astype(jnp.float32), axis=-1).astype(v_c.dtype)
    o_ctx = jnp.einsum('bhlm,bmhd->blhd', p, v_c).reshape(B, L, D_MODEL) @ w_out
    return o_ctx, o_lat


def conv_ffn(h, w_in, conv_w, conv_b, w_out):
    a, val = jnp.split(h @ w_in, 2, axis=-1)
    ap = jnp.pad(a, ((0, 0), (1, 1), (0, 0)))
    a = ap[:, :-2] * conv_w[0] + ap[:, 1:-1] * conv_w[1] + ap[:, 2:] * conv_w[2] + conv_b
    return (jax.nn.silu(a) * val) @ w_out


def setup_inputs(seed: int = 0) -> dict:
    key = jax.random.key(seed)
    ks = jax.random.split(key, 24)
    n_gla = (DEPTH + N_MIXERS - 1) // N_MIXERS
    n_na = DEPTH // N_MIXERS
    D = D_MODEL

    def nrm(k, shape, scale):
        return jax.random.normal(k, shape, jnp.float32) * scale

    return {
        "x": nrm(ks[0], (BATCH, SEQ, D), 1.0),
        "c": nrm(ks[1], (BATCH, D), 1.0),
        "ctx": nrm(ks[2], (BATCH, CTX_LEN, D), 1.0),
        "c_ctx": nrm(ks[3], (D,), 1.0),
        "ada_w": nrm(ks[4], (DEPTH, D, 6 * D), 0.5 * D ** -0.5),
        "ada_b": nrm(ks[5], (DEPTH, 6 * D), 0.02),
        "norm1_g": 1.0 + nrm(ks[6], (DEPTH, D), 0.02),
        "norm2_g": 1.0 + nrm(ks[7], (DEPTH, D), 0.02),
        "ffn_w_in": nrm(ks[8], (DEPTH, D, 2 * D_FF), D ** -0.5),
        "ffn_conv_w": nrm(ks[9], (DEPTH, CONV_W, D_FF), CONV_W ** -0.5),
        "ffn_conv_b": nrm(ks[10], (DEPTH, D_FF), 0.02),
        "ffn_w_out": nrm(ks[11], (DEPTH, D_FF, D), D_FF ** -0.5),
        "gla_w_in": nrm(ks[12], (n_gla, D, 2 * GLA_DK + 2 * GLA_DV), D ** -0.5),
        "gla_a_w1": nrm(ks[13], (n_gla, 2, D, GLA_LOW_RANK), D ** -0.5),
        "gla_a_w2": nrm(ks[14], (n_gla, 2, GLA_LOW_RANK, GLA_DK), GLA_LOW_RANK ** -0.5),
        "gla_a_b": nrm(ks[15], (n_gla, 2, GLA_DK), 0.1),
        "gla_norm_g": 1.0 + nrm(ks[16], (n_gla, GLA_DV_HEAD), 0.02),
        "gla_w_out": nrm(ks[17], (n_gla, GLA_DV, D), GLA_DV ** -0.5),
        "na_w_qkv": nrm(ks[18], (n_na, D, 3 * D), D ** -0.5),
        "na_rpb": nrm(ks[19], (n_na, NA_HEADS, 2 * NA_KH - 1, 2 * NA_KW - 1), 0.1),
        "na_w_out": nrm(ks[20], (n_na, D, D), D ** -0.5),
        "final_g": 1.0 + nrm(ks[21], (D,), 0.02),
    }


def reference(x, c, ctx, c_ctx, ada_w, ada_b, norm1_g, norm2_g, ffn_w_in, ffn_conv_w, ffn_conv_b, ffn_w_out,
              gla_w_in, gla_a_w1, gla_a_w2, gla_a_b, gla_norm_g, gla_w_out, na_w_qkv, na_rpb, na_w_out, final_g):
    x_lat, x_ctx = x, ctx
    sc_lat = jax.nn.silu(c)
    sc_ctx = jax.nn.silu(c_ctx)
    for i in range(DEPTH):
        last = i == DEPTH - 1
        j = i // N_MIXERS
        mod_l = (sc_lat @ ada_w[i] + ada_b[i])[:, None, :]
        mod_c = sc_ctx @ ada_w[i] + ada_b[i]
        sh1_l, s1_l, g1_l, sh2_l, s2_l, g2_l = jnp.split(mod_l, 6, axis=-1)
        sh1_c, s1_c, g1_c, sh2_c, s2_c, g2_c = jnp.split(mod_c, 6, axis=-1)

        h_l = modulate(x_lat, norm1_g[i], sh1_l, s1_l)
        h_c = modulate(x_ctx, norm1_g[i], sh1_c, s1_c)
        if i % N_MIXERS == 0:
            o_c, o_l = gla_mixer(h_c, h_l, gla_w_in[j], gla_a_w1[j], gla_a_w2[j], gla_a_b[j],
                                 gla_norm_g[j], gla_w_out[j], not last)
        else:
            o_c, o_l = na_mixer(h_c, h_l, na_w_qkv[j], na_rpb[j], na_w_out[j], not last)
        x_lat = x_lat + g1_l * o_l

        h_l = modulate(x_lat, norm2_g[i], sh2_l, s2_l)
        x_lat = x_lat + g2_l * conv_ffn(h_l, ffn_w_in[i], ffn_conv_w[i], ffn_conv_b[i], ffn_w_out[i])

        if not last:
            x_ctx = x_ctx + g1_c * o_c
            h_c = modulate(x_ctx, norm2_g[i], sh2_c, s2_c)
            x_ctx = x_ctx + g2_c * conv_ffn(h_c, ffn_w_in[i], ffn_conv_w[i], ffn_conv_b[i], ffn_w_out[i])
    return rms_norm(x_lat, final_g)
```

```python
import numpy as np
import concourse.bass as bass
import concourse.mybir as mybir
from concourse.bass_utils import run_bass_kernel_spmd

F32 = mybir.dt.float32
BF16 = mybir.dt.bfloat16
AF = mybir.ActivationFunctionType
ALU = mybir.AluOpType
AX = mybir.AxisListType

SAME_ENGINE_SYNC = True
N_DMA_SEMS = 40

D = 1024
NROW = 41
TEXT = NROW * 64
TCTX = 256
ES = [0, 27, 59, 87]
OWN = [0, 5, 5, 9]
EPS = 1e-6
DFF = 2816
NFC = 22


class Res:
    __slots__ = ("name", "w", "rs")

    def __init__(self, name=""):
        self.name = name
        self.w = None
        self.rs = []


class Prog:
    def __init__(self):
        self.nc = bass.Bass("TRN2", target_bir_lowering=False)
        nc = self.nc
        self.eng = {"pe": nc.tensor, "act": nc.scalar, "dve": nc.vector,
                    "pool": nc.gpsimd, "sp": nc.sync}
        self.sem = {e: nc.alloc_semaphore("s_" + e) for e in self.eng}
        self.cnt = {e: 0 for e in self.eng}
        self.dsem = [nc.alloc_semaphore("d%d" % i) for i in range(N_DMA_SEMS)]
        self.dcnt = [0] * N_DMA_SEMS
        self.dnext = 0
        self.seen = {}
        self.n_inst = 0
        self.n_wait = 0
        self.out_tickets = []
        self.stack = []

    def sb(self, name, shape, dt):
        g = self.nc.sbuf_tensor("sb_" + name, list(shape), dt)
        t = g.__enter__()
        self.stack.append(g)
        return t.ap() if hasattr(t, "ap") and callable(t.ap) else t

    def mark(self):
        return len(self.stack)

    def free_to(self, k):
        self.barrier()
        while len(self.stack) > k:
            self.stack.pop().__exit__(None, None, None)

    def barrier(self):
        for e in self.eng:
            for o in self.eng:
                if o != e and self.cnt[o] > 0:
                    self._wait(e, (o, self.cnt[o]))
            for i in range(N_DMA_SEMS):
                if self.dcnt[i] > 0:
                    self._wait(e, (i, self.dcnt[i]))

    def ps(self, name, shape, dt=F32):
        return self.nc.alloc_psum_tensor("ps_" + name, list(shape), dt).ap()

    def dram(self, name, shape, dt, kind="Internal"):
        return self.nc.dram_tensor(name, list(shape), dt, kind=kind).ap()

    def _wait(self, e, ticket, raw=True):
        if ticket is None:
            return
        key, val = ticket
        if isinstance(key, str):
            if key == e and (e == "pe" or not SAME_ENGINE_SYNC or not raw):
                return
            sem = self.sem[key]
        else:
            sem = self.dsem[key]
        k = (e, key)
        if self.seen.get(k, 0) >= val:
            return
        self.seen[k] = val
        self.eng[e].wait_ge(sem, val)
        self.n_wait += 1

    def _deps(self, e, reads, writes, is_dma=False):
        for r in reads:
            self._wait(e, r.w)
        for r in writes:
            self._wait(e, r.w, raw=is_dma)
            for t in r.rs:
                self._wait(e, t, raw=is_dma)

    def _commit(self, ticket, reads, writes):
        for r in reads:
            r.rs.append(ticket)
            if len(r.rs) > 48:
                best = {}
                for k, v in r.rs:
                    if best.get(k, 0) < v:
                        best[k] = v
                r.rs = list(best.items())
        for r in writes:
            r.w = ticket
            r.rs = []

    def op(self, e, fn, reads=(), writes=(), inc=True):
        self._deps(e, reads, writes)
        ins = fn()
        if inc:
            self.cnt[e] += 1
            ins.then_inc(self.sem[e], 1)
            t = (e, self.cnt[e])
        else:
            t = (e, self.cnt[e] + 1)
        self._commit(t, reads, writes)
        self.n_inst += 1
        return t

    def dma(self, q, out, in_, reads=(), writes=(), is_output=False, **kw):
        self._deps(q, reads, writes, is_dma=True)
        i = self.dnext
        self.dnext = (self.dnext + 1) % N_DMA_SEMS
        if self.dcnt[i] > 0:
            self._wait(q, (i, self.dcnt[i]))
        self.dcnt[i] += 16
        self.eng[q].dma_start(out=out, in_=in_, **kw).then_inc(self.dsem[i], 16)
        t = (i, self.dcnt[i])
        self._commit(t, reads, writes)
        if is_output:
            self.out_tickets.append(t)
        self.n_inst += 1
        return t

    def finish(self):
        for i in range(N_DMA_SEMS):
            if self.dcnt[i] > 0:
                self._wait("sp", (i, self.dcnt[i]))
        return self.nc


class T:
    __slots__ = ("ap", "r")

    def __init__(self, ap, name=""):
        self.ap = ap
        self.r = Res(name)


def build(stop="all"):
    p = Prog()
    nc = p.nc
    V, A, G_, PE = nc.vector, nc.scalar, nc.gpsimd, nc.tensor

    def din(name, shape):
        return p.dram(name, shape, F32, kind="ExternalInput")

    xfull = din("xfull", [8192, D])
    xext = din("xext", [TEXT, D])
    xctx = din("xctx", [TCTX, D])
    cT_d = din("cT", [128, 16])
    sel_d = din("sel", [128, 8])
    ropef = din("ropef", [8192, 128])
    ropee = din("ropee", [TEXT, 128])
    consts_d = din("consts", [128, 640])
    ada_w = din("ada_w", [2, D, 6 * D])
    ada_b = din("ada_b", [2, 6 * D])
    norm1_g = din("norm1_g", [2, D])
    norm2_g = din("norm2_g", [2, D])
    ffn_w_in = din("ffn_w_in", [2, D, 2 * DFF])
    ffn_cw = din("ffn_cw", [2, 128, NFC * 4])
    ffn_w_out = din("ffn_w_out", [2, DFF, D])
    gla_w_in = din("gla_w_in", [D, 3 * D])
    gla_a1 = din("gla_a1", [D, 32])
    gla_a2 = din("gla_a2", [2, 33, 512])
    gla_ng = din("gla_ng", [1, D])
    gla_w_out = din("gla_w_out", [D, D])
    na_w_qkv = din("na_w_qkv", [D, 3 * D])
    na_T = din("na_T", [64, 16 * 15 * 64])
    na_w_out = din("na_w_out", [D, D])
    final_g = din("final_g", [1, D])
    out_d = p.dram("out", [TEXT, D], F32, kind="ExternalOutput")

    modd = p.dram("modd", [2, 2, 6 * D], F32)
    r_modd = Res("modd")
    OF = p.dram("OF", [TEXT + TCTX, D], F32)
    Rd = p.dram("Rd", [TEXT + TCTX, D], F32)
    Rd2 = p.dram("Rd2", [TEXT + TCTX, D], F32)
    NT_E = 21
    r_OF = [Res() for _ in range(NT_E + 2)]
    r_Rd = [Res() for _ in range(NT_E + 2)]
    r_Rd2 = [Res() for _ in range(NT_E + 2)]

    consts = T(p.sb("consts", [128, 640], F32))
    p.dma("sp", consts.ap[:], consts_d[:], writes=[consts.r])
    identf = consts.ap[:, 0:128]
    M_incl = [consts.ap[:, 128:256], consts.ap[:, 384:512]]
    M_excl = [consts.ap[:, 256:384], consts.ap[:, 512:640]]
    ident = T(p.sb("ident", [128, 128], BF16))
    p.op("dve", lambda: V.tensor_copy(out=ident.ap[:], in_=identf), reads=[consts.r], writes=[ident.r])
    ones = T(p.sb("ones", [128, 1], F32))
    p.op("dve", lambda: V.memset(ones.ap[:], 1.0), writes=[ones.r])
    sel = T(p.sb("sel", [128, 8], F32))
    p.dma("sp", sel.ap[:], sel_d[:], writes=[sel.r])

    pA = [T(p.ps("pA%d" % i, [128, 512])) for i in range(2)]
    pV = T(p.ps("pV", [128, 1024]))
    pO = T(p.ps("pO", [128, 1024]))
    pT = T(p.ps("pT", [128, 1024], BF16))
    pS = T(p.ps("pS", [128, 512]))
    pa_i = [0]

    def next_pA():
        pa_i[0] ^= 1
        return pA[pa_i[0]]

    modtmp = T(p.sb("modtmp", [128, D], F32))

    xt = [T(p.sb("xt%d" % i, [128, D], F32)) for i in range(2)]
    junk = T(p.sb("junk", [128, D], BF16))
    h32 = T(p.sb("h32", [128, D], F32))
    hb = T(p.sb("hb", [128, D], BF16))
    st = [T(p.sb("st%d" % i, [128, 8], F32)) for i in range(2)]
    st_i = [0]
    qsc = T(p.sb("qsc", [128, 1], F32))
    p.op("dve", lambda: V.memset(qsc.ap[:], float(np.log(128.0 ** -0.5))), writes=[qsc.r])
    eps_c = T(p.sb("eps_c", [128, 1], F32))
    p.op("dve", lambda: V.memset(eps_c.ap[:], EPS), writes=[eps_c.r])
    one_c = T(p.sb("one_c", [128, 1], F32))
    p.op("dve", lambda: V.memset(one_c.ap[:], 1.0), writes=[one_c.r])
    mkM = p.mark()
    cT = T(p.sb("cTs", [128, 16], F32))
    p.dma("sp", cT.ap[:], cT_d[:], writes=[cT.r])
    scT = T(p.sb("scT", [128, 16], F32))
    p.op("act", lambda: A.activation(out=scT.ap[:], in_=cT.ap[:], func=AF.Silu), reads=[cT.r], writes=[scT.r])
    adab = [T(p.sb("adab%d" % i, [2, 512], F32)) for i in range(2)]
    adaw = [T(p.sb("adaw%d" % i, [128, 8, 512], F32)) for i in range(6)]
    modsb = [T(p.sb("modsb%d" % i, [2, 512], F32)) for i in range(2)]
    it = 0
    for l in range(2):
        for n in range(12):
            wt = adaw[it % 6]
            ms = modsb[it % 2]
            ab = adab[it % 2]
            wq = ("sp", "act", "pool")[it % 3]
            it += 1
            for w in range(2):
                p.dma("sp", ab.ap[w:w + 1, :], ada_b[l:l + 1, n * 512:(n + 1) * 512], writes=[ab.r])
            p.dma(wq, wt.ap[:], ada_w[l, :, n * 512:(n + 1) * 512].rearrange("(k p) n -> p k n", p=128),
                  writes=[wt.r])
            pa = next_pA()
            for k in range(8):
                p.op("pe", lambda k=k, pa=pa, wt=wt: PE.matmul(pa.ap[0:2, :], lhsT=scT.ap[:, 2 * k:2 * k + 2], rhs=wt.ap[:, k, :],
                                                         start=(k == 0), stop=(k == 7)),
                     reads=[scT.r, wt.r], writes=[pa.r], inc=(k == 7))
            p.op("dve", lambda pa=pa, ms=ms, ab=ab: V.tensor_tensor(out=ms.ap[:], in0=pa.ap[0:2, :],
                                                                 in1=ab.ap[:, :], op=ALU.add),
                 reads=[pa.r, ab.r], writes=[ms.r])
            p.dma("sp", modd[l, :, n * 512:(n + 1) * 512], ms.ap[:], reads=[ms.r], writes=[r_modd])

    p.free_to(mkM)

    def bload(dst_ap, row_ap, n, writes, reads=()):
        p.dma("sp", dst_ap, row_ap.to_broadcast([128, n]), writes=writes, reads=reads)

    modbuf = {}

    def load_mod(tag, l, w, sub, normg_ap):
        if tag not in modbuf:
            modbuf[tag] = (T(p.sb("G_" + tag, [128, D], F32)), T(p.sb("SH_" + tag, [128, D], F32)),
                           T(p.sb("GT_" + tag, [128, D], F32)))
        Gt, SHt, GTt = modbuf[tag]
        base = sub * 3 * D
        tmp = T(p.sb("ngtmp_" + tag + str(l) + str(sub), [128, D], F32)) if False else modtmp
        bload(SHt.ap[:], modd[l, w:w + 1, base:base + D], D, writes=[SHt.r], reads=[r_modd])
        bload(Gt.ap[:], modd[l, w:w + 1, base + D:base + 2 * D], D, writes=[Gt.r], reads=[r_modd])
        bload(GTt.ap[:], modd[l, w:w + 1, base + 2 * D:base + 3 * D], D, writes=[GTt.r], reads=[r_modd])
        bload(tmp.ap[:], normg_ap, D, writes=[tmp.r])
        p.op("dve", lambda: V.scalar_tensor_tensor(out=Gt.ap[:], in0=Gt.ap[:], scalar=1.0, in1=tmp.ap[:],
                                                   op0=ALU.add, op1=ALU.mult),
             reads=[Gt.r, tmp.r], writes=[Gt.r])
        return Gt, SHt, GTt


    def norm_A1(xtile, n, Gt, SHt, hbuf=None):
        st_i[0] ^= 1
        s = st[st_i[0]]
        xa = xtile.ap
        p.op("act", lambda: A.activation(out=junk.ap[:n, :], in_=xa[:n, :], func=AF.Square, accum_out=s.ap[:n, 0:1]),
             reads=[xtile.r], writes=[junk.r, s.r])
        p.op("act", lambda: A.activation(out=s.ap[:n, 1:2], in_=s.ap[:n, 0:1], func=AF.Ln, scale=1.0 / D, bias=eps_c.ap[:n, 0:1]),
             reads=[s.r, eps_c.r], writes=[s.r])
        p.op("act", lambda: A.activation(out=s.ap[:n, 2:3], in_=s.ap[:n, 1:2], func=AF.Exp, scale=-0.5),
             reads=[s.r], writes=[s.r])
        p.op("dve", lambda: V.scalar_tensor_tensor(out=h32.ap[:n, :], in0=xa[:n, :], scalar=s.ap[:n, 2:3],
                                                   in1=Gt.ap[:n, :], op0=ALU.mult, op1=ALU.mult),
             reads=[xtile.r, s.r, Gt.r], writes=[h32.r])
        hb_ = hb if hbuf is None else hbuf
        p.op("pool", lambda: G_.tensor_tensor(out=hb_.ap[:n, :], in0=h32.ap[:n, :], in1=SHt.ap[:n, :], op=ALU.add),
             reads=[h32.r, SHt.r], writes=[hb_.r])

    def norm_A2(n, hT_ap, hT_r, col0=0, sub=None, hbuf=None):
        hb_ = hb if hbuf is None else hbuf
        for k in range(8):
            p.op("pe", lambda k=k: PE.transpose(out=pT.ap[:, k * 128:k * 128 + n], in_=hb_.ap[:n, k * 128:(k + 1) * 128],
                                                identity=ident.ap[:n, :n]),
                 reads=[hb_.r, ident.r], writes=[pT.r], inc=(k == 7))
        s0, sn = (0, n) if sub is None else sub
        p.op("act", lambda: A.copy(out=hT_ap[:, :, col0:col0 + sn],
                                   in_=pT.ap[:, :].rearrange("p (k t) -> p k t", k=8)[:, :, s0:s0 + sn]),
             reads=[pT.r], writes=[hT_r])

    def norm_mod_T(xtile, n, Gt, SHt, hT_ap, hT_r, col0=0, sub=None, xap=None):
        norm_A1(xtile, n, Gt, SHt)
        norm_A2(n, hT_ap, hT_r, col0, sub)

    mkG = p.mark()
    Wg = T(p.sb("Wg", [128, 8, 3 * D], BF16))
    for g in range(6):
        p.dma("pool", Wg.ap[:, :, g * 512:(g + 1) * 512],
              gla_w_in[:, g * 512:(g + 1) * 512].rearrange("(k p) n -> p k n", p=128), writes=[Wg.r])
    A1 = T(p.sb("A1", [128, 8, 32], BF16))
    p.dma("pool", A1.ap[:], gla_a1.rearrange("(k p) n -> p k n", p=128), writes=[A1.r])
    A2 = T(p.sb("A2", [33, 2, 512], F32))
    p.dma("sp", A2.ap[:], gla_a2.rearrange("d r n -> r d n"), writes=[A2.r])
    Wo = T(p.sb("Wo", [128, 8, D], BF16))
    p.dma("pool", Wo.ap[:], gla_w_out.rearrange("(k p) n -> p k n", p=128), writes=[Wo.r])
    NG = T(p.sb("NG", [128, D], F32))
    bload(NG.ap[:], gla_ng[0:1, :], D, writes=[NG.r])

    G1l, SH1l, GT1l = load_mod("a", 0, 0, 0, norm1_g[0:1, :])
    G1c, SH1c, GT1c = load_mod("b", 0, 1, 0, norm1_g[0:1, :])

    hT = T(p.sb("hT", [128, 8, 128], BF16))
    U = T(p.sb("U", [33, 128], F32))
    p.op("dve", lambda: V.memset(U.ap[:], 0.0), writes=[U.r])
    p.op("dve", lambda: V.memset(U.ap[32:33, :], 1.0), writes=[U.r])
    e1 = T(p.sb("e1", [128, 512], F32))
    sp_ = T(p.sb("sp", [128, 512], F32))
    q32 = T(p.sb("q32", [128, 512], F32))
    k32 = T(p.sb("k32", [128, 512], F32))
    qr = T(p.sb("qr", [128, 512], F32))
    kr = T(p.sb("kr", [128, 512], F32))
    tA = T(p.sb("tA", [128, 256], F32))
    tB = T(p.sb("tB", [128, 256], F32))
    tC = T(p.sb("tC", [128, 256], F32))
    tD = T(p.sb("tD", [128, 256], F32))
    rope_t = [T(p.sb("ropet%d" % i, [128, 128], F32)) for i in range(2)]
    Eq = T(p.sb("Eq", [128, 512], F32))
    Ek = T(p.sb("Ek", [128, 512], F32))
    Dh = T(p.sb("Dh", [128, 512], F32))
    dec = T(p.sb("dec", [128, 4], F32))
    qt = T(p.sb("qt", [128, 512], BF16))
    kt = T(p.sb("kt", [128, 512], BF16))
    kh = T(p.sb("kh", [128, 512], BF16))
    qT = T(p.sb("qT", [128, 4, 128], BF16))
    kT = T(p.sb("kT", [128, 4, 128], BF16))
    vsb = T(p.sb("vsb", [128, D], BF16))
    attm = [T(p.sb("attm%d" % i, [128, 128], BF16)) for i in range(2)]
    S32 = [T(p.sb("S32_%d" % h, [128, 256], F32)) for h in range(4)]
    Sbf = [T(p.sb("Sbf_%d" % h, [128, 256], BF16)) for h in range(4)]
    acc = [[T(p.sb("acc%d_%d" % (d, h), [128, 256], F32)) for h in range(4)] for d in range(2)]
    oft = T(p.sb("oft", [128, D], F32))
    osum = T(p.sb("osum", [128, D], F32))
    sr = T(p.sb("sr", [128, D], F32))
    og = T(p.sb("og", [128, D], BF16))
    ogT = T(p.sb("ogT", [128, 8, 128], BF16))
    ytmp = T(p.sb("ytmp", [128, D], F32))
    xo = T(p.sb("xo", [128, D], F32))

    def load_x(buf, src_ap, n, reads=()):
        p.dma("sp", buf.ap[:n, :], src_ap, writes=[buf.r], reads=reads)

    def gla_tile(xtile, n, d, full, rope_src, Gt, SHt, of_ap=None, of_r=None, GTt=None, dst_ap=None, dst_r=None, first=True, nxt=None):
        if first:
            norm_mod_T(xtile, n, Gt, SHt, hT.ap, hT.r)
        rt = None
        if rope_src is not None:
            rope_t.reverse()
            rt = rope_t[0]
            p.dma("sp", rt.ap[:n, :], rope_src, writes=[rt.r])

        def proj(pa_ap, pa_r, c0, nc_):
            for k in range(8):
                p.op("pe", lambda k=k: PE.matmul(pa_ap[:n, 0:nc_], lhsT=hT.ap[:, k, :n], rhs=Wg.ap[:, k, c0:c0 + nc_],
                                                 start=(k == 0), stop=(k == 7)),
                     reads=[hT.r, Wg.r], writes=[pa_r], inc=(k == 7))

        pu = next_pA()
        for k in range(8):
            p.op("pe", lambda k=k: PE.matmul(pu.ap[0:16, :n], lhsT=A1.ap[:, k, d * 16:(d + 1) * 16], rhs=hT.ap[:, k, :n],
                                             start=(k == 0), stop=(k == 7)), reads=[A1.r, hT.r], writes=[pu.r], inc=(k == 7))
        p.op("act", lambda: A.copy(out=U.ap[0:16, :n], in_=pu.ap[0:16, :n]), reads=[pu.r], writes=[U.r])
        pz = next_pA()
        p.op("pe", lambda: PE.matmul(pz.ap[:n, :], lhsT=U.ap[:, :n], rhs=A2.ap[:, d, :], start=True, stop=True),
             reads=[U.r, A2.r], writes=[pz.r])
        p.op("act", lambda: A.activation(out=e1.ap[:n, :], in_=pz.ap[:n, :], func=AF.Exp, scale=-1.0),
             reads=[pz.r], writes=[e1.r])
        p.op("act", lambda: A.activation(out=sp_.ap[:n, :], in_=e1.ap[:n, :], func=AF.Ln, bias=one_c.ap[:n, 0:1]),
             reads=[e1.r, one_c.r], writes=[sp_.r])
        if nxt is not None:
            norm_A1(nxt[0], nxt[1], Gt, SHt)
        pk = next_pA()
        proj(pk.ap, pk.r, 512, 512)
        p.op("act", lambda: A.copy(out=k32.ap[:n, :], in_=pk.ap[:n, :]), reads=[pk.r], writes=[k32.r])

        def rope(src, dst):
            sv = src.ap[:n, :].rearrange("p (h c f e) -> p h c f e", h=4, c=2, f=2)
            dv = dst.ap[:n, :].rearrange("p (h c f e) -> p h c f e", h=4, c=2, f=2)
            x1, x2 = sv[:, :, :, 0, :], sv[:, :, :, 1, :]
            cs = rt.ap[:n, 0:64].rearrange("p (c e) -> p c e", c=2).unsqueeze(1).to_broadcast([n, 4, 2, 32])
            sn = rt.ap[:n, 64:128].rearrange("p (c e) -> p c e", c=2).unsqueeze(1).to_broadcast([n, 4, 2, 32])
            v4 = lambda t: t.ap[:n, :].rearrange("p (h c e) -> p h c e", h=4, c=2)
            p.op("pool", lambda: G_.tensor_tensor(out=v4(tA), in0=x1, in1=cs, op=ALU.mult), reads=[src.r, rt.r], writes=[tA.r])
            p.op("pool", lambda: G_.tensor_tensor(out=v4(tB), in0=x2, in1=sn, op=ALU.mult), reads=[src.r, rt.r], writes=[tB.r])
            p.op("dve", lambda: V.tensor_tensor(out=v4(tC), in0=x1, in1=sn, op=ALU.mult), reads=[src.r, rt.r], writes=[tC.r])
            p.op("dve", lambda: V.tensor_tensor(out=v4(tD), in0=x2, in1=cs, op=ALU.mult), reads=[src.r, rt.r], writes=[tD.r])
            p.op("pool", lambda: G_.tensor_tensor(out=dv[:, :, :, 0, :], in0=v4(tA), in1=v4(tB), op=ALU.subtract),
                 reads=[tA.r, tB.r], writes=[dst.r])
            p.op("dve", lambda: V.tensor_tensor(out=dv[:, :, :, 1, :], in0=v4(tC), in1=v4(tD), op=ALU.add),
                 reads=[tC.r, tD.r], writes=[dst.r])

        if rt is not None:
            rope(k32, kr)
            krr = kr
        else:
            krr = k32
        pc = next_pA()
        p.op("pe", lambda: PE.matmul(pc.ap[:n, :], lhsT=M_excl[d][:n, :n], rhs=sp_.ap[:n, :], start=True, stop=True),
             reads=[consts.r, sp_.r], writes=[pc.r])
        p.op("act", lambda: A.activation(out=Dh.ap[:n, :], in_=pc.ap[:n, :], func=AF.Exp, scale=-1.0 / 16),
             reads=[pc.r], writes=[Dh.r])
        p.op("dve", lambda: V.tensor_tensor(out=kh.ap[:n, :], in0=krr.ap[:n, :], in1=Dh.ap[:n, :], op=ALU.mult),
             reads=[krr.r, Dh.r], writes=[kh.r])
        pd = next_pA()
        for h in range(4):
            p.op("pe", lambda h=h: PE.matmul(pd.ap[:, h:h + 1], lhsT=sp_.ap[:n, h * 128:(h + 1) * 128], rhs=ones.ap[:n, 0:1],
                                             start=True, stop=True), reads=[sp_.r, ones.r], writes=[pd.r], inc=(h == 3))
        p.op("act", lambda: A.activation(out=dec.ap[:, :], in_=pd.ap[:, 0:4], func=AF.Exp, scale=-1.0 / 16),
             reads=[pd.r], writes=[dec.r])
        if full:
            pc2 = next_pA()
            p.op("pe", lambda: PE.matmul(pc2.ap[:n, :], lhsT=M_incl[d][:n, :n], rhs=sp_.ap[:n, :], start=True, stop=True),
                 reads=[consts.r, sp_.r], writes=[pc2.r])
            p.op("act", lambda: A.activation(out=Eq.ap[:n, :], in_=pc2.ap[:n, :], func=AF.Exp, scale=-1.0 / 16, bias=qsc.ap[:n, 0:1]),
                 reads=[pc2.r, qsc.r], writes=[Eq.r])
            p.op("act", lambda: A.activation(out=Ek.ap[:n, :], in_=pc2.ap[:n, :], func=AF.Exp, scale=1.0 / 16),
                 reads=[pc2.r], writes=[Ek.r])
            pq = next_pA()
            proj(pq.ap, pq.r, 0, 512)
            p.op("act", lambda: A.copy(out=q32.ap[:n, :], in_=pq.ap[:n, :]), reads=[pq.r], writes=[q32.r])
            if rt is not None:
                rope(q32, qr)
                qrr = qr
            else:
                qrr = q32
            p.op("dve", lambda: V.tensor_tensor(out=qt.ap[:n, :], in0=qrr.ap[:n, :], in1=Eq.ap[:n, :], op=ALU.mult),
                 reads=[qrr.r, Eq.r], writes=[qt.r])
            p.op("pool", lambda: G_.tensor_tensor(out=kt.ap[:n, :], in0=krr.ap[:n, :], in1=Ek.ap[:n, :], op=ALU.mult),
                 reads=[krr.r, Ek.r], writes=[kt.r])
            for g in range(2):
                proj(pV.ap[:, g * 512:(g + 1) * 512], pV.r, 1024 + g * 512, 512)
            p.op("act", lambda: A.copy(out=vsb.ap[:n, :], in_=pV.ap[:n, :]), reads=[pV.r], writes=[vsb.r])
            if full and d == 1:
                for g in range(2):
                    proj(pV.ap[:, g * 512:(g + 1) * 512], pV.r, 2048 + g * 512, 512)
                p.op("act", lambda: A.activation(out=sr.ap[:n, :], in_=pV.ap[:n, :], func=AF.Silu), reads=[pV.r], writes=[sr.r])
                p.op("pool", lambda: G_.tensor_tensor(out=sr.ap[:n, :], in0=sr.ap[:n, :], in1=NG.ap[:n, :], op=ALU.mult),
                     reads=[sr.r, NG.r], writes=[sr.r])
            for h in range(4):
                p.op("pe", lambda h=h: PE.transpose(out=pT.ap[:, h * 128:h * 128 + n], in_=qt.ap[:n, h * 128:(h + 1) * 128],
                                                    identity=ident.ap[:n, :n]), reads=[qt.r, ident.r], writes=[pT.r], inc=False)
            for h in range(4):
                p.op("pe", lambda h=h: PE.transpose(out=pT.ap[:, 512 + h * 128:512 + h * 128 + n], in_=kt.ap[:n, h * 128:(h + 1) * 128],
                                                    identity=ident.ap[:n, :n]), reads=[kt.r, ident.r], writes=[pT.r], inc=(h == 3))
            pTv = pT.ap[:, :].rearrange("p (a h t) -> p a h t", a=2, h=4)
            p.op("act", lambda: A.copy(out=qT.ap[:, :, :n], in_=pTv[:, 0, :, :n]), reads=[pT.r], writes=[qT.r])
            p.op("act", lambda: A.copy(out=kT.ap[:, :, :n], in_=pTv[:, 1, :, :n]), reads=[pT.r], writes=[kT.r])
        if not full:
            for g in range(2):
                proj(pV.ap[:, g * 512:(g + 1) * 512], pV.r, 1024 + g * 512, 512)
            p.op("act", lambda: A.copy(out=vsb.ap[:n, :], in_=pV.ap[:n, :]), reads=[pV.r], writes=[vsb.r])
            if full and d == 1:
                for g in range(2):
                    proj(pV.ap[:, g * 512:(g + 1) * 512], pV.r, 2048 + g * 512, 512)
                p.op("act", lambda: A.activation(out=sr.ap[:n, :], in_=pV.ap[:n, :], func=AF.Silu), reads=[pV.r], writes=[sr.r])
                p.op("pool", lambda: G_.tensor_tensor(out=sr.ap[:n, :], in0=sr.ap[:n, :], in1=NG.ap[:n, :], op=ALU.mult),
                     reads=[sr.r, NG.r], writes=[sr.r])
        for h in range(4):
            hs = slice(h * 256, (h + 1) * 256)
            if full:
                am = attm[h % 2]
                psa = pS.ap[:, 256 + (h % 2) * 128:256 + (h % 2) * 128 + 128]
                p.op("pe", lambda h=h, psa=psa: PE.matmul(psa[:n, :n], lhsT=kT.ap[:, h, :n], rhs=qT.ap[:, h, :n], start=True, stop=True),
                     reads=[kT.r, qT.r], writes=[pS.r])
                p.op("dve", lambda am=am, psa=psa: V.tensor_tensor(out=am.ap[:n, :n], in0=psa[:n, :n], in1=M_incl[d][:n, :n], op=ALU.mult),
                     reads=[pS.r, consts.r], writes=[am.r])
                p.op("pe", lambda h=h, hs=hs: PE.matmul(pO.ap[:n, hs], lhsT=qT.ap[:, h, :n], rhs=Sbf[h].ap[:, :], start=True, stop=False),
                     reads=[qT.r, Sbf[h].r], writes=[pO.r], inc=False)
                p.op("pe", lambda h=h, hs=hs, am=am: PE.matmul(pO.ap[:n, hs], lhsT=am.ap[:n, :n], rhs=vsb.ap[:n, hs], start=False, stop=True),
                     reads=[am.r, vsb.r], writes=[pO.r])
            sn_bufs = [pA[0], pA[1]] if full else [pA[0], pA[1], pS]
            snb = sn_bufs[h % len(sn_bufs)]
            p.op("pe", lambda h=h, hs=hs: PE.matmul(snb.ap[:, 0:256], lhsT=kh.ap[:n, h * 128:(h + 1) * 128], rhs=vsb.ap[:n, hs], start=True, stop=True),
                 reads=[kh.r, vsb.r], writes=[snb.r])
            p.op("dve", lambda h=h: V.scalar_tensor_tensor(out=S32[h].ap[:, :], in0=S32[h].ap[:, :], scalar=dec.ap[:, h:h + 1],
                                                           in1=snb.ap[:, 0:256], op0=ALU.mult, op1=ALU.add),
                 reads=[S32[h].r, dec.r, snb.r], writes=[S32[h].r])
            if h == 1 and nxt is not None:
                norm_A2(nxt[1], hT.ap, hT.r)
            if full:
                p.op("act", lambda h=h: A.copy(out=Sbf[h].ap[:, :], in_=S32[h].ap[:, :]), reads=[S32[h].r], writes=[Sbf[h].r])
        if not full:
            return
        if d == 0:
            p.op("act", lambda: A.copy(out=oft.ap[:n, :], in_=pO.ap[:n, :]), reads=[pO.r], writes=[oft.r])
            p.dma("sp", of_ap, oft.ap[:n, :], reads=[oft.r], writes=[of_r])
            return
        p.dma("sp", oft.ap[:n, :], of_ap, reads=[of_r], writes=[oft.r])
        p.op("dve", lambda: V.tensor_tensor(out=osum.ap[:n, :], in0=pO.ap[:n, :], in1=oft.ap[:n, :], op=ALU.add),
             reads=[pO.r, oft.r], writes=[osum.r])
        st_i[0] ^= 1
        s = st[st_i[0]]
        for h in range(4):
            p.op("act", lambda h=h: A.activation(out=junk.ap[:n, 0:256], in_=osum.ap[:n, h * 256:(h + 1) * 256], func=AF.Square,
                                                 accum_out=s.ap[:n, h:h + 1]), reads=[osum.r], writes=[junk.r, s.r])
        p.op("act", lambda: A.activation(out=s.ap[:n, 0:4], in_=s.ap[:n, 0:4], func=AF.Ln, scale=1.0 / 256, bias=eps_c.ap[:n, 0:1]),
             reads=[s.r, eps_c.r], writes=[s.r])
        p.op("act", lambda: A.activation(out=s.ap[:n, 4:8], in_=s.ap[:n, 0:4], func=AF.Exp, scale=-0.5),
             reads=[s.r], writes=[s.r])
        for h in range(4):
            hs = slice(h * 256, (h + 1) * 256)
            p.op("dve", lambda h=h, hs=hs: V.scalar_tensor_tensor(out=og.ap[:n, hs], in0=osum.ap[:n, hs], scalar=s.ap[:n, 4 + h:5 + h],
                                                                  in1=sr.ap[:n, hs], op0=ALU.mult, op1=ALU.mult),
                 reads=[osum.r, s.r, sr.r], writes=[og.r])
        for k in range(8):
            p.op("pe", lambda k=k: PE.transpose(out=pT.ap[:, k * 128:k * 128 + n], in_=og.ap[:n, k * 128:(k + 1) * 128],
                                                identity=ident.ap[:n, :n]), reads=[og.r, ident.r], writes=[pT.r], inc=(k == 7))
        p.op("act", lambda: A.copy(out=ogT.ap[:, :, :n], in_=pT.ap[:, :].rearrange("p (k t) -> p k t", k=8)[:, :, :n]),
             reads=[pT.r], writes=[ogT.r])
        for g in range(2):
            for k in range(8):
                p.op("pe", lambda k=k, g=g: PE.matmul(pO.ap[:n, g * 512:(g + 1) * 512], lhsT=ogT.ap[:, k, :n],
                                                      rhs=Wo.ap[:, k, g * 512:(g + 1) * 512], start=(k == 0), stop=(k == 7)),
                     reads=[ogT.r, Wo.r], writes=[pO.r], inc=(k == 7 and g == 1))
        p.op("dve", lambda: V.tensor_tensor(out=ytmp.ap[:n, :], in0=pO.ap[:n, :], in1=GTt.ap[:n, :], op=ALU.mult),
             reads=[pO.r, GTt.r], writes=[ytmp.r])
        p.op("pool", lambda: G_.tensor_tensor(out=xo.ap[:n, :], in0=ytmp.ap[:n, :], in1=xtile.ap[:n, :], op=ALU.add),
             reads=[ytmp.r, xtile.r], writes=[xo.r])
        p.dma("sp", dst_ap, xo.ap[:n, :], reads=[xo.r], writes=[dst_r])

    def zero_state():
        for h in range(4):
            p.op("dve", lambda h=h: V.memset(S32[h].ap[:], 0.0), writes=[S32[h].r])
            p.op("pool", lambda h=h: G_.memset(Sbf[h].ap[:], 0.0), writes=[Sbf[h].r])

    def take_snap(d, i):
        sc = sel.ap[:, 4 * d + i:4 * d + i + 1]
        for h in range(4):
            if i == (0 if d == 0 else 3):
                p.op("dve", lambda h=h: V.tensor_scalar(out=acc[d][h].ap[:], in0=S32[h].ap[:], scalar1=sc, scalar2=None, op0=ALU.mult),
                     reads=[S32[h].r, sel.r], writes=[acc[d][h].r])
            else:
                p.op("dve", lambda h=h: V.scalar_tensor_tensor(out=acc[d][h].ap[:], in0=S32[h].ap[:], scalar=sc, in1=acc[d][h].ap[:],
                                                                 op0=ALU.mult, op1=ALU.add),
                     reads=[S32[h].r, sel.r, acc[d][h].r], writes=[acc[d][h].r])

    def select_state(d):
        for h in range(4):
            p.op("dve", lambda h=h: V.tensor_copy(out=S32[h].ap[:], in_=acc[d][h].ap[:]), reads=[acc[d][h].r], writes=[S32[h].r])
            p.op("act", lambda h=h: A.copy(out=Sbf[h].ap[:], in_=acc[d][h].ap[:]), reads=[acc[d][h].r], writes=[Sbf[h].r])

    def rows_to_tiles(r0, r1, descending=False):
        tiles = []
        if not descending:
            r = r0
            while r < r1:
                nr = 2 if r + 2 <= r1 else 1
                tiles.append((r * 64, nr * 64))
                r += nr
        else:
            r = r1
            while r > r0:
                nr = 2 if r - 2 >= r0 else 1
                tiles.append(((r - nr) * 64, nr * 64))
                r -= nr
        return tiles

    xi = [0]

    def sweep(tiles, fn, rlist=None, pipelined=False):
        bufs = []
        rd = (lambda i: [rlist[i]]) if rlist is not None else (lambda i: [])
        for idx, (src, n, extra) in enumerate(tiles):
            if idx == 0:
                xi[0] ^= 1
                b = xt[xi[0]]
                load_x(b, src, n, rd(0))
                bufs.append(b)
            if idx + 1 < len(tiles):
                xi[0] ^= 1
                b2 = xt[xi[0]]
                load_x(b2, tiles[idx + 1][0], tiles[idx + 1][1], rd(idx + 1))
                bufs.append(b2)
            nx = (bufs[idx + 1], tiles[idx + 1][1]) if idx + 1 < len(tiles) else None
            if pipelined:
                fn(bufs[idx], n, extra, idx == 0, nx)
            else:
                fn(bufs[idx], n, extra)

    ctx_tiles = [(xctx[i * 128:(i + 1) * 128, :], 128, i) for i in range(2)]
    ext_tiles = [(xext[i * 128:min((i + 1) * 128, TEXT), :], min(128, TEXT - i * 128), i) for i in range(NT_E)]

    zero_state()
    sweep(ctx_tiles, lambda b, n, i, f, nx: gla_tile(b, n, 0, True, None, G1c, SH1c,
                                              of_ap=OF[TEXT + i * 128:TEXT + i * 128 + n, :], of_r=r_OF[NT_E + i], first=f, nxt=nx), pipelined=True)
    take_snap(0, 0)
    for si, (r0, r1) in enumerate([(0, 27), (27, 59), (59, 87)]):
        tl = [(xfull[t0:t0 + n, :], n, t0) for (t0, n) in rows_to_tiles(r0, r1)]
        sweep(tl, lambda b, n, t0, f, nx: gla_tile(b, n, 0, False, ropef[t0:t0 + n, :], G1l, SH1l, first=f, nxt=nx), pipelined=True)
        take_snap(0, si + 1)
    zero_state()
    sweep(ctx_tiles[::-1], lambda b, n, i, f, nx: gla_tile(b, n, 1, True, None, G1c, SH1c,
                                                    of_ap=OF[TEXT + i * 128:TEXT + i * 128 + n, :], of_r=r_OF[NT_E + i], GTt=GT1c,
                                                    dst_ap=Rd[TEXT + i * 128:TEXT + i * 128 + n, :], dst_r=r_Rd[NT_E + i], first=f, nxt=nx), pipelined=True)
    take_snap(1, 3)
    for si, (r0, r1) in zip([2, 1, 0], [(100, 128), (68, 100), (41, 68)]):
        tl = [(xfull[t0:t0 + n, :], n, t0) for (t0, n) in rows_to_tiles(r0, r1, descending=True)]
        sweep(tl, lambda b, n, t0, f, nx: gla_tile(b, n, 1, False, ropef[t0:t0 + n, :], G1l, SH1l, first=f, nxt=nx), pipelined=True)
        take_snap(1, si)
    select_state(0)
    sweep(ext_tiles, lambda b, n, i, f, nx: gla_tile(b, n, 0, True, ropee[i * 128:i * 128 + n, :], G1l, SH1l,
                                              of_ap=OF[i * 128:i * 128 + n, :], of_r=r_OF[i], first=f, nxt=nx), pipelined=True)
    select_state(1)
    sweep(ext_tiles[::-1], lambda b, n, i, f, nx: gla_tile(b, n, 1, True, ropee[i * 128:i * 128 + n, :], G1l, SH1l,
                                                    of_ap=OF[i * 128:i * 128 + n, :], of_r=r_OF[i], GTt=GT1l,
                                                    dst_ap=Rd[i * 128:i * 128 + n, :], dst_r=r_Rd[i], first=f, nxt=nx), pipelined=True)

    def split_groups(tb):
        u = tb // 64
        ng = -(-u // 7)
        base, rem = divmod(u, ng)
        out, t = [], 0
        for i in range(ng):
            sz = (base + (1 if i < rem else 0)) * 64
            out.append((t, sz))
            t += sz
        return out

    SLICES = [(0, 4), (4, 4), (8, 4), (12, 4), (16, 3), (19, 3)]

    def ffn_phase(l, src, r_src, dst, r_dst, with_ctx, final):
        mk = p.mark()
        tagp = "f%d" % l
        G2l, SH2l, GT2l = load_mod(tagp + "a", l, 0, 1, norm2_g[l:l + 1, :])
        if with_ctx:
            G2c, SH2c, GT2c = load_mod(tagp + "b", l, 1, 1, norm2_g[l:l + 1, :])
        CW = T(p.sb(tagp + "CW", [128, NFC * 4], F32))
        p.dma("sp", CW.ap[:], ffn_cw[l], writes=[CW.r])
        if final:
            FG = T(p.sb(tagp + "FG", [128, D], F32))
            bload(FG.ap[:], final_g[0:1, :], D, writes=[FG.r])
        Wa = [T(p.sb(tagp + "Wa%d" % i, [128, 8, 512], BF16)) for i in range(2)]
        Wv = [T(p.sb(tagp + "Wv%d" % i, [128, 8, 512], BF16)) for i in range(2)]
        Wo_ = [T(p.sb(tagp + "Wo%d" % i, [128, 4, D], BF16)) for i in range(2)]
        NTB = 11
        Rb = T(p.sb(tagp + "Rb", [128, NTB, D], F32))
        r_Rb = [Res() for _ in range(NTB)]
        TBMAX = NTB * 128
        H2T = T(p.sb(tagp + "H2T", [128, 8, TBMAX + 2], BF16))
        gT = T(p.sb(tagp + "gT", [128, 4, TBMAX], BF16))
        if with_ctx:
            Rc = T(p.sb(tagp + "Rc", [128, 2, D], F32))
            r_Rc = [Res() for _ in range(2)]
            H2Tc = T(p.sb(tagp + "H2Tc", [128, 8, TCTX + 2], BF16))
            gTc = T(p.sb(tagp + "gTc", [128, 4, TCTX], BF16))
        c1 = [T(p.sb(tagp + "c1_%d" % i, [128, 448], F32)) for i in range(2)]
        c2 = [T(p.sb(tagp + "c2_%d" % i, [128, 448], F32)) for i in range(2)]
        ytm = T(p.sb(tagp + "ytm", [128, D], F32))
        hb2 = T(p.sb(tagp + "hb2", [128, D], BF16))
        pAV = [(pA[0], pA[1]), (T(pV.ap[:, 0:512]), T(pV.ap[:, 512:1024]))]
        pT32 = T(pT.ap.bitcast(F32))
        pT32.r = pT.r
        ybufs = [(T(pO.ap[:, 0:512]), T(pO.ap[:, 512:1024])), (pS, pT32)]
        ycnt = [0]
        wi = [0]

        def load_slice(si):
            f0, nf = SLICES[si]
            b = wi[0] % 2
            wi[0] += 1
            p.dma("pool", Wa[b].ap[:, :, 0:nf * 128], ffn_w_in[l, :, f0 * 128:(f0 + nf) * 128].rearrange("(k p) n -> p k n", p=128),
                  writes=[Wa[b].r])
            p.dma("pool", Wv[b].ap[:, :, 0:nf * 128],
                  ffn_w_in[l, :, DFF + f0 * 128:DFF + (f0 + nf) * 128].rearrange("(k p) n -> p k n", p=128), writes=[Wv[b].r])
            p.dma("pool", Wo_[b].ap[:, 0:nf, :], ffn_w_out[l, f0 * 128:(f0 + nf) * 128, :].rearrange("(c p) n -> p c n", p=128),
                  writes=[Wo_[b].r])
            return b

        blocks = [(0, 11), (11, 21)]
        cnt = [0]
        for bi, (tl0, tl1) in enumerate(blocks):
            tok0 = tl0 * 128
            tb = min(tl1 * 128, TEXT) - tok0
            do_ctx = with_ctx and bi == 0
            nxt = load_slice(0)
            ents = []
            n_pre = 0
            if tl0 == 0:
                p.op("pool", lambda: G_.memset(H2T.ap[:, :, 0:1], 0.0), writes=[H2T.r])
            else:
                xi[0] ^= 1
                hb_ = xt[xi[0]]
                p.dma("sp", hb_.ap[:, :], src[(tl0 - 1) * 128:tl0 * 128, :], reads=[r_src[tl0 - 1]], writes=[hb_.r])
                ents.append((hb_, 128, G2l, SH2l, H2T, 0, (127, 1)))
                n_pre = 1
            for ti in range(tl0, tl1):
                n = min(128, TEXT - ti * 128)
                sl = ti - tl0
                p.dma("sp", Rb.ap[:n, sl, :], src[ti * 128:ti * 128 + n, :], reads=[r_src[ti]], writes=[r_Rb[sl]])
                xtile = T(Rb.ap[:, sl, :])
                xtile.r = r_Rb[sl]
                ents.append((xtile, n, G2l, SH2l, H2T, 1 + sl * 128, None))
            n_tiles_e = len(ents)
            if tl1 == NT_E:
                p.op("pool", lambda: G_.memset(H2T.ap[:, :, 1 + tb:2 + tb], 0.0), writes=[H2T.r])
            else:
                xi[0] ^= 1
                hb_ = xt[xi[0]]
                nn = min(128, TEXT - tl1 * 128)
                p.dma("sp", hb_.ap[:nn, :], src[tl1 * 128:tl1 * 128 + nn, :], reads=[r_src[tl1]], writes=[hb_.r])
                ents.append((hb_, nn, G2l, SH2l, H2T, 1 + tb, (0, 1)))
            n_lat_e = len(ents)
            if do_ctx:
                for ci in range(2):
                    p.dma("sp", Rc.ap[:, ci, :], src[TEXT + ci * 128:TEXT + (ci + 1) * 128, :], reads=[r_src[NT_E + ci]], writes=[r_Rc[ci]])
                    xtile = T(Rc.ap[:, ci, :])
                    xtile.r = r_Rc[ci]
                    ents.append((xtile, 128, G2c, SH2c, H2Tc, 1 + ci * 128, None))
                p.op("pool", lambda: G_.memset(H2Tc.ap[:, :, 0:1], 0.0), writes=[H2Tc.r])
                p.op("pool", lambda: G_.memset(H2Tc.ap[:, :, TCTX + 1:TCTX + 2], 0.0), writes=[H2Tc.r])
            hbs = [hb, hb2]
            e0 = ents[0]
            norm_A1(e0[0], e0[1], e0[2], e0[3], hbuf=hbs[0])
            eptr = [0]
            h2t_res = [Res() for _ in ents]

            def emit_upto(kmax):
                while eptr[0] < kmax:
                    ei = eptr[0]
                    e = ents[ei]
                    if ei + 1 < len(ents):
                        e1 = ents[ei + 1]
                        norm_A1(e1[0], e1[1], e1[2], e1[3], hbuf=hbs[(ei + 1) % 2])
                    norm_A2(e[1], e[4].ap, h2t_res[ei], col0=e[5], sub=e[6], hbuf=hbs[ei % 2])
                    eptr[0] += 1

            lat_groups = split_groups(tb)
            groups = []
            for gi, (g0, ng) in enumerate(lat_groups):
                if gi == len(lat_groups) - 1:
                    need = n_lat_e
                else:
                    need = n_pre + min((g0 + ng) // 128, tl1 - tl0 - 1) + 1
                groups.append((H2T, gT, g0, ng, need))
            if do_ctx:
                groups.append((H2Tc, gTc, 0, TCTX, len(ents)))
            def item(b, fl, fc, Hs, gs, g0, ng, need):
                hres = [Hs.r] + h2t_res[:need]
                cwv = CW.ap[:, fc * 4:fc * 4 + 4]
                pa_, pv_ = pAV[cnt[0] % 2]
                cc1 = c1[cnt[0] % 2]
                cc2 = c2[cnt[0] % 2]
                cnt[0] += 1
                for k in range(8):
                    p.op("pe", lambda k=k, pa_=pa_, Hs=Hs, g0=g0, ng=ng, b=b, fl=fl: PE.matmul(
                        pa_.ap[:, 0:ng + 2], lhsT=Wa[b].ap[:, k, fl * 128:(fl + 1) * 128], rhs=Hs.ap[:, k, g0:g0 + ng + 2],
                        start=(k == 0), stop=(k == 7)), reads=[Wa[b].r] + hres, writes=[pa_.r], inc=(k == 7))
                for k in range(8):
                    p.op("pe", lambda k=k, pv_=pv_, Hs=Hs, g0=g0, ng=ng, b=b, fl=fl: PE.matmul(
                        pv_.ap[:, 0:ng], lhsT=Wv[b].ap[:, k, fl * 128:(fl + 1) * 128], rhs=Hs.ap[:, k, g0 + 1:g0 + 1 + ng],
                        start=(k == 0), stop=(k == 7)), reads=[Wv[b].r] + hres, writes=[pv_.r], inc=(k == 7))
                p.op("act", lambda pa_=pa_, cc1=cc1, ng=ng, cwv=cwv: A.activation(
                    out=cc1.ap[:, 0:ng], in_=pa_.ap[:, 1:ng + 1], func=AF.Identity, scale=cwv[:, 1:2], bias=cwv[:, 3:4]),
                    reads=[pa_.r, CW.r], writes=[cc1.r])
                p.op("dve", lambda pa_=pa_, cc1=cc1, cc2=cc2, ng=ng, cwv=cwv: V.scalar_tensor_tensor(
                    out=cc2.ap[:, 0:ng], in0=pa_.ap[:, 0:ng], scalar=cwv[:, 0:1], in1=cc1.ap[:, 0:ng], op0=ALU.mult, op1=ALU.add),
                    reads=[pa_.r, CW.r, cc1.r], writes=[cc2.r])
                p.op("dve", lambda pa_=pa_, cc1=cc1, cc2=cc2, ng=ng, cwv=cwv: V.scalar_tensor_tensor(
                    out=cc1.ap[:, 0:ng], in0=pa_.ap[:, 2:ng + 2], scalar=cwv[:, 2:3], in1=cc2.ap[:, 0:ng], op0=ALU.mult, op1=ALU.add),
                    reads=[pa_.r, CW.r, cc2.r], writes=[cc1.r])
                p.op("act", lambda cc1=cc1, cc2=cc2, ng=ng: A.activation(out=cc2.ap[:, 0:ng], in_=cc1.ap[:, 0:ng], func=AF.Silu),
                     reads=[cc1.r], writes=[cc2.r])
                p.op("dve", lambda pv_=pv_, cc2=cc2, gs=gs, g0=g0, ng=ng, fl=fl: V.tensor_tensor(
                    out=gs.ap[:, fl, g0:g0 + ng], in0=cc2.ap[:, 0:ng], in1=pv_.ap[:, 0:ng], op=ALU.mult),
                    reads=[cc2.r, pv_.r], writes=[gs.r])

            for si, (f0, nf) in enumerate(SLICES):
                b = nxt
                if si + 1 < len(SLICES):
                    nxt = load_slice(si + 1)
                if si == 0:
                    for (Hs, gs, g0, ng, need) in groups:
                        emit_upto(need)
                        for fl in range(nf):
                            item(b, fl, f0 + fl, Hs, gs, g0, ng, need)
                else:
                    for fl in range(nf):
                        for (Hs, gs, g0, ng, need) in groups:
                            item(b, fl, f0 + fl, Hs, gs, g0, ng, need)
                tiles = [(gT, Rb, r_Rb[ti - tl0], ti - tl0, (ti - tl0) * 128, min(128, TEXT - ti * 128), GT2l) for ti in range(tl0, tl1)]
                if do_ctx:
                    tiles += [(gTc, Rc, r_Rc[ci], ci, ci * 128, 128, GT2c) for ci in range(2)]
                for (gs, Rt, rr, sl, c0, n, GTt) in tiles:
                    yb = ybufs[ycnt[0] % 2]
                    ycnt[0] += 1
                    for g in range(2):
                        yt = yb[g]
                        for fl in range(nf):
                            p.op("pe", lambda: PE.matmul(
                                yt.ap[:n, 0:512], lhsT=gs.ap[:, fl, c0:c0 + n], rhs=Wo_[b].ap[:, fl, g * 512:(g + 1) * 512],
                                start=(fl == 0), stop=(fl == nf - 1)), reads=[gs.r, Wo_[b].r], writes=[yt.r], inc=(fl == nf - 1))
                    for g in range(2):
                        yt = yb[g]
                        p.op("dve", lambda: V.tensor_tensor(out=ytm.ap[:n, g * 512:(g + 1) * 512], in0=yt.ap[:n, 0:512],
                                                            in1=GTt.ap[:n, g * 512:(g + 1) * 512], op=ALU.mult),
                             reads=[yt.r, GTt.r], writes=[ytm.r])
                    p.op("dve", lambda Rt=Rt, sl=sl, n=n: V.tensor_tensor(out=Rt.ap[:n, sl, :], in0=Rt.ap[:n, sl, :], in1=ytm.ap[:n, :], op=ALU.add),
                         reads=[ytm.r, rr], writes=[rr])
            for ti in range(tl0, tl1):
                n = min(128, TEXT - ti * 128)
                sl = ti - tl0
                if not final:
                    p.dma("sp", dst[ti * 128:ti * 128 + n, :], Rb.ap[:n, sl, :], reads=[r_Rb[sl]], writes=[r_dst[ti]])
                else:
                    st_i[0] ^= 1
                    s_ = st[st_i[0]]
                    p.op("act", lambda s_=s_, n=n, sl=sl: A.activation(out=junk.ap[:n, :], in_=Rb.ap[:n, sl, :], func=AF.Square, accum_out=s_.ap[:n, 0:1]),
                         reads=[r_Rb[sl]], writes=[junk.r, s_.r])
                    p.op("act", lambda s_=s_, n=n: A.activation(out=s_.ap[:n, 1:2], in_=s_.ap[:n, 0:1], func=AF.Ln, scale=1.0 / D, bias=eps_c.ap[:n, 0:1]),
                         reads=[s_.r, eps_c.r], writes=[s_.r])
                    p.op("act", lambda s_=s_, n=n: A.activation(out=s_.ap[:n, 2:3], in_=s_.ap[:n, 1:2], func=AF.Exp, scale=-0.5),
                         reads=[s_.r], writes=[s_.r])
                    p.op("dve", lambda s_=s_, n=n, sl=sl: V.scalar_tensor_tensor(out=ytm.ap[:n, :], in0=Rb.ap[:n, sl, :], scalar=s_.ap[:n, 2:3],
                                                                           in1=FG.ap[:n, :], op0=ALU.mult, op1=ALU.mult),
                         reads=[r_Rb[sl], s_.r, FG.r], writes=[ytm.r])
                    p.dma("sp", out_d[ti * 128:ti * 128 + n, :], ytm.ap[:n, :], reads=[ytm.r], is_output=True)
            if do_ctx:
                for ci in range(2):
                    p.dma("sp", dst[TEXT + ci * 128:TEXT + (ci + 1) * 128, :], Rc.ap[:, ci, :], reads=[r_Rc[ci]], writes=[r_dst[NT_E + ci]])
        p.free_to(mk)

    def na_phase(src, r_src, dst, r_dst):
        l = 1
        for hh in range(2):
            mk = p.mark()
            tg = "n%d" % hh
            G1, SH1, GT1 = load_mod(tg + "a", l, 0, 0, norm1_g[l:l + 1, :])
            G1c_, SH1c_, _ = load_mod(tg + "b", l, 1, 0, norm1_g[l:l + 1, :])
            Wq = T(p.sb(tg + "Wq", [128, 8, 512], BF16))
            Wk = T(p.sb(tg + "Wk", [128, 8, 512], BF16))
            Wv2 = T(p.sb(tg + "Wv", [128, 8, 512], BF16))
            for (Wt, c0) in ((Wq, hh * 512), (Wk, D + hh * 512), (Wv2, 2 * D + hh * 512)):
                p.dma("pool", Wt.ap[:], na_w_qkv[:, c0:c0 + 512].rearrange("(k p) n -> p k n", p=128), writes=[Wt.r])
            WoN = T(p.sb(tg + "WoN", [128, 4, D], BF16))
            p.dma("pool", WoN.ap[:], na_w_out[hh * 512:(hh + 1) * 512, :].rearrange("(c p) n -> p c n", p=128), writes=[WoN.r])
            Tt = T(p.sb(tg + "Tt", [128, 4 * 960], BF16))
            naT4 = na_T.rearrange("c (pr hp x) -> c pr hp x", hp=2, x=960)
            for hp_ in range(2):
                p.dma("pool", Tt.ap[hp_ * 64:(hp_ + 1) * 64, :].rearrange("c (pr x) -> c pr x", x=960),
                      naT4[:, hh * 4:(hh + 1) * 4, hp_, :], writes=[Tt.r])
            QT = T(p.sb(tg + "QT", [128, 4, TEXT], BF16))
            KT = T(p.sb(tg + "KT", [128, 4, TEXT], BF16))
            Vt = T(p.sb(tg + "Vt", [128, NT_E, 512], BF16))
            KcT = T(p.sb(tg + "KcT", [128, 4, TCTX], BF16))
            Vc = T(p.sb(tg + "Vc", [128, 2, 512], BF16))
            hT1 = T(p.sb(tg + "hT1", [128, 8, 128], BF16))
            Pb = [T(p.sb(tg + "Pb%d" % i, [128, 896], BF16)) for i in range(2)]
            PTs = [T(p.sb(tg + "PTs%d" % i, [128, 896], BF16)) for i in range(2)]
            BD = [T(p.sb(tg + "BD%d" % i, [128, 128], BF16)) for i in range(2)]
            OT = T(p.sb(tg + "OT", [128, 4, 64], BF16))
            sm = [T(p.sb(tg + "sm%d" % i, [128, 8], F32)) for i in range(2)]
            xrow = [T(p.sb(tg + "xrow%d" % i, [64, D], F32)) for i in range(2)]
            yrow = T(p.sb(tg + "yrow", [64, D], F32))
            for i in range(2):
                p.op("pool", lambda i=i: G_.memset(Pb[i].ap[:], 0.0), writes=[Pb[i].r])
                p.op("pool", lambda i=i: G_.memset(BD[i].ap[:], 0.0), writes=[BD[i].r])
            pT2 = [T(pT.ap[:, 0:896]), T(pA[1].ap.bitcast(BF16)[:, 0:896])]
            pT2[0].r = pT.r
            pT2[1].r = pA[1].r
            Sps = [pV, pO]

            qtok = T(p.sb(tg + "qtok", [128, 512], BF16))
            ktok = T(p.sb(tg + "ktok", [128, 512], BF16))

            def kv_proj(n, tok0, QT_, KT_, Vdst_ap, Vdst_r, kcols):
                if QT_ is not None:
                    pq = pA[0]
                    for k in range(8):
                        p.op("pe", lambda k=k: PE.matmul(pq.ap[:n, :], lhsT=hT1.ap[:, k, :n], rhs=Wq.ap[:, k, :], start=(k == 0), stop=(k == 7)),
                             reads=[hT1.r, Wq.r], writes=[pq.r], inc=(k == 7))
                    p.op("act", lambda: A.mul(out=qtok.ap[:n, :], in_=pq.ap[:n, :], mul=0.125), reads=[pq.r], writes=[qtok.r])
                pk = pA[1]
                for k in range(8):
                    p.op("pe", lambda k=k: PE.matmul(pk.ap[:n, :], lhsT=hT1.ap[:, k, :n], rhs=Wk.ap[:, k, :], start=(k == 0), stop=(k == 7)),
                         reads=[hT1.r, Wk.r], writes=[pk.r], inc=(k == 7))
                p.op("act", lambda: A.copy(out=ktok.ap[:n, :], in_=pk.ap[:n, :]), reads=[pk.r], writes=[ktok.r])
                for k in range(8):
                    p.op("pe", lambda k=k: PE.matmul(pS.ap[:n, :], lhsT=hT1.ap[:, k, :n], rhs=Wv2.ap[:, k, :], start=(k == 0), stop=(k == 7)),
                         reads=[hT1.r, Wv2.r], writes=[pS.r], inc=(k == 7))
                p.op("dve", lambda: V.tensor_copy(out=Vdst_ap, in_=pS.ap[:n, :]), reads=[pS.r], writes=[Vdst_r])
                if QT_ is not None:
                    for pr in range(4):
                        p.op("pe", lambda pr=pr: PE.transpose(out=pT.ap[:, pr * 128:pr * 128 + n], in_=qtok.ap[:n, pr * 128:(pr + 1) * 128],
                                                              identity=ident.ap[:n, :n]), reads=[qtok.r, ident.r], writes=[pT.r], inc=False)
                for pr in range(4):
                    p.op("pe", lambda pr=pr: PE.transpose(out=pT.ap[:, 512 + pr * 128:512 + pr * 128 + n], in_=ktok.ap[:n, pr * 128:(pr + 1) * 128],
                                                          identity=ident.ap[:n, :n]), reads=[ktok.r, ident.r], writes=[pT.r], inc=(pr == 3))
                pTv = pT.ap[:, :].rearrange("p (a h t) -> p a h t", a=2, h=4)
                if QT_ is not None:
                    p.op("act", lambda: A.copy(out=QT_.ap[:, :, tok0:tok0 + n], in_=pTv[:, 0, :, :n]), reads=[pT.r], writes=[QT_.r])
                p.op("act", lambda: A.copy(out=KT_.ap[:, :, kcols:kcols + n], in_=pTv[:, 1, :, :n]), reads=[pT.r], writes=[KT_.r])

            def ctx_fn(b, n, ci, first, nx):
                if first:
                    norm_mod_T(b, n, G1c_, SH1c_, hT1.ap, hT1.r)
                if nx is not None:
                    norm_A1(nx[0], nx[1], G1c_, SH1c_)
                kv_proj(n, 0, None, KcT, Vc.ap[:n, ci, :], Vc.r, ci * 128)
                if nx is not None:
                    norm_A2(nx[1], hT1.ap, hT1.r)
            sweep([(src[TEXT + ci * 128:TEXT + (ci + 1) * 128, :], 128, ci) for ci in range(2)], ctx_fn,
                  rlist=[r_src[NT_E + ci] for ci in range(2)], pipelined=True)

            def ext_fn(b, n, ti, first, nx):
                if first:
                    norm_mod_T(b, n, G1, SH1, hT1.ap, hT1.r)
                if nx is not None:
                    norm_A1(nx[0], nx[1], G1, SH1)
                kv_proj(n, ti * 128, QT, KT, Vt.ap[:n, ti, :], Vt.r, ti * 128)
                if nx is not None:
                    norm_A2(nx[1], hT1.ap, hT1.r)
            sweep([(src[ti * 128:ti * 128 + min(128, TEXT - ti * 128), :], min(128, TEXT - ti * 128), ti) for ti in range(NT_E)], ext_fn,
                  rlist=[r_src[ti] for ti in range(NT_E)], pipelined=True)

            items = [(r, pr) for r in range(NROW) for pr in range(4)]

            def row_info(r):
                rs = min(max(r - 4, 0), NROW - 8)
                return rs, rs - r + 7

            def stage1(i):
                r, pr = items[i]
                rs, dr0 = row_info(r)
                S_, pb, smt, bd = Sps[i % 2], Pb[i % 2], sm[i % 2], BD[i % 2]
                if pr == 0:
                    xr = xrow[r % 2]
                    row_src = src if hh == 0 else dst
                    row_res = r_src[r // 2] if hh == 0 else r_dst[r // 2]
                    p.dma("sp", xr.ap[:, :], row_src[r * 64:(r + 1) * 64, :], reads=[row_res], writes=[xr.r])
                for hp in range(2):
                    p.op("pool", lambda: G_.tensor_copy(out=bd.ap[hp * 64:(hp + 1) * 64, hp * 64:(hp + 1) * 64],
                                                        in_=QT.ap[hp * 64:(hp + 1) * 64, pr, r * 64:(r + 1) * 64]),
                         reads=[QT.r], writes=[bd.r])
                p.op("pe", lambda: PE.matmul(S_.ap[:, 0:512], lhsT=bd.ap[:, :], rhs=KT.ap[:, pr, rs * 64:rs * 64 + 512],
                                             start=True, stop=False), reads=[bd.r, KT.r], writes=[S_.r], inc=False)
                tb0 = (pr * 15 + dr0) * 64
                p.op("pe", lambda: PE.matmul(S_.ap[:, 0:512], lhsT=ident.ap[:, :], rhs=Tt.ap[:, tb0:tb0 + 512],
                                             start=False, stop=True), reads=[ident.r, Tt.r], writes=[S_.r], inc=False)
                p.op("pe", lambda: PE.matmul(S_.ap[:, 512:768], lhsT=bd.ap[:, :], rhs=KcT.ap[:, pr, :],
                                             start=True, stop=True), reads=[bd.r, KcT.r], writes=[S_.r])
                p.op("dve", lambda: V.reduce_max(out=smt.ap[:, 0:1], in_=S_.ap[:, 0:768], axis=AX.X), reads=[S_.r], writes=[smt.r])
                p.op("dve", lambda: V.tensor_scalar(out=smt.ap[:, 1:2], in0=smt.ap[:, 0:1], scalar1=-1.0, scalar2=None, op0=ALU.mult),
                     reads=[smt.r], writes=[smt.r])
                p.op("act", lambda: A.activation(out=pb.ap[:, 64:576], in_=S_.ap[:, 0:512], func=AF.Exp, bias=smt.ap[:, 1:2],
                                                 accum_out=smt.ap[:, 2:3]), reads=[S_.r, smt.r], writes=[pb.r, smt.r])
                p.op("act", lambda: A.activation(out=pb.ap[:, 640:896], in_=S_.ap[:, 512:768], func=AF.Exp, bias=smt.ap[:, 1:2],
                                                 accum_out=smt.ap[:, 3:4]), reads=[S_.r, smt.r], writes=[pb.r, smt.r])

            ktab = {}
            pend = []

            def stage2pre(i):
                pb, smt = Pb[i % 2], sm[i % 2]
                p.op("dve", lambda: V.tensor_tensor(out=smt.ap[:, 4:5], in0=smt.ap[:, 2:3], in1=smt.ap[:, 3:4], op=ALU.add),
                     reads=[smt.r], writes=[smt.r])
                p.op("dve", lambda: V.reciprocal(out=smt.ap[:, 5:6], in_=smt.ap[:, 4:5]), reads=[smt.r], writes=[smt.r])
                p.op("dve", lambda: V.tensor_scalar(out=pb.ap[:, 64:896], in0=pb.ap[:, 64:896], scalar1=smt.ap[:, 5:6], scalar2=None, op0=ALU.mult),
                     reads=[pb.r, smt.r], writes=[pb.r])

            def stage2a(i):
                r, pr = items[i]
                rs, dr0 = row_info(r)
                pb, smt, pts, ptp = Pb[i % 2], sm[i % 2], PTs[i % 2], pT2[i % 2]
                kts = []
                if rs % 2 == 0:
                    for t in range(4):
                        kts.append((64 + 128 * t, 128, ("v", rs // 2 + t)))
                else:
                    for t in range(5):
                        vt = (rs - 1) // 2 + t
                        kk = 128 if vt * 128 + 128 <= TEXT else 64
                        kts.append((128 * t, kk, ("v", vt)))
                kts.append((640, 128, ("c", 0)))
                kts.append((768, 128, ("c", 1)))
                nt = len(kts)
                for t, (c0, kk, _) in enumerate(kts):
                    p.op("pe", lambda: PE.transpose(out=ptp.ap[:kk, t * 128:(t + 1) * 128], in_=pb.ap[:, c0:c0 + kk],
                                                    identity=ident.ap[:, :]),
                         reads=[pb.r, ident.r], writes=[ptp.r], inc=(t == nt - 1))
                p.op("dve", lambda: V.tensor_copy(out=pts.ap[:, 0:nt * 128], in_=ptp.ap[:, 0:nt * 128]), reads=[ptp.r], writes=[pts.r])
                ktab[i] = kts

            def stage2b(i):
                r, pr = items[i]
                pts = PTs[i % 2]
                kts = ktab.pop(i)
                nt = len(kts)
                if pend:
                    out_proj_row(pend.pop())
                po = pS.ap[:, pr * 128:(pr + 1) * 128]
                for t, (c0, kk, (kind, vi)) in enumerate(kts):
                    vl = Vt.ap[:kk, vi, pr * 128:(pr + 1) * 128] if kind == "v" else Vc.ap[:kk, vi, pr * 128:(pr + 1) * 128]
                    p.op("pe", lambda: PE.matmul(po, lhsT=vl, rhs=pts.ap[:kk, t * 128:(t + 1) * 128],
                                                 start=(t == 0), stop=(t == nt - 1)),
                         reads=[pts.r, Vt.r, Vc.r], writes=[pS.r], inc=(t == nt - 1))
                p.op("act", lambda: A.copy(out=OT.ap[0:64, pr, :], in_=pS.ap[0:64, pr * 128:pr * 128 + 64]), reads=[pS.r], writes=[OT.r])
                p.op("act", lambda: A.copy(out=OT.ap[64:128, pr, :], in_=pS.ap[64:128, pr * 128 + 64:pr * 128 + 128]), reads=[pS.r], writes=[OT.r])
                if pr == 3:
                    pend.append(r)

            def out_proj_row(r):
                xr = xrow[r % 2]
                for g in range(2):
                    for kk in range(4):
                        p.op("pe", lambda: PE.matmul(pA[g].ap[0:64, :], lhsT=OT.ap[:, kk, :], rhs=WoN.ap[:, kk, g * 512:(g + 1) * 512],
                                                     start=(kk == 0), stop=(kk == 3)), reads=[OT.r, WoN.r], writes=[pA[g].r], inc=(kk == 3))
                    p.op("dve", lambda: V.tensor_tensor(out=yrow.ap[:, g * 512:(g + 1) * 512], in0=pA[g].ap[0:64, :],
                                                        in1=GT1.ap[0:64, g * 512:(g + 1) * 512], op=ALU.mult),
                         reads=[pA[g].r, GT1.r], writes=[yrow.r])
                p.op("pool", lambda: G_.tensor_tensor(out=xr.ap[:, :], in0=xr.ap[:, :], in1=yrow.ap[:, :], op=ALU.add),
                     reads=[xr.r, yrow.r], writes=[xr.r])
                p.dma("sp", dst[r * 64:(r + 1) * 64, :], xr.ap[:, :], reads=[xr.r], writes=[r_dst[r // 2]])

            stage1(0)
            for i in range(len(items)):
                stage2pre(i)
                if i + 1 < len(items):
                    stage1(i + 1)
                if i >= 1:
                    stage2b(i - 1)
                stage2a(i)
            stage2b(len(items) - 1)
            out_proj_row(pend.pop())
            p.free_to(mk)

    def dump(src, r_src):
        for i in range(NT_E):
            n = min(128, TEXT - i * 128)
            b = xt[i % 2]
            p.dma("sp", b.ap[:n, :], src[i * 128:i * 128 + n, :], reads=[r_src[i]], writes=[b.r])
            p.dma("sp", out_d[i * 128:i * 128 + n, :], b.ap[:n, :], reads=[b.r], is_output=True)
        print("instructions", p.n_inst, "waits", p.n_wait)
        return p.finish()

    if stop == "gla":
        return dump(Rd, r_Rd)
    p.free_to(mkG)
    ffn_phase(0, Rd, r_Rd, Rd2, r_Rd2, True, False)
    if stop == "ffn0":
        return dump(Rd2, r_Rd2)
    if stop == "ffn0c":
        for i in range(2):
            b = xt[i % 2]
            p.dma("sp", b.ap[:, :], Rd2[TEXT + i * 128:TEXT + (i + 1) * 128, :], reads=[r_Rd2[NT_E + i]], writes=[b.r])
            p.dma("sp", out_d[i * 128:(i + 1) * 128, :], b.ap[:, :], reads=[b.r], is_output=True)
        return p.finish()

    na_phase(Rd2, r_Rd2, Rd, r_Rd)
    if stop == "na":
        return dump(Rd, r_Rd)
    ffn_phase(1, Rd, r_Rd, None, None, False, True)
    print("instructions", p.n_inst, "waits", p.n_wait)
    return p.finish()


def host_inputs(inputs):
    f = np.float32
    x = np.asarray(inputs["x"], f)
    ctx = np.asarray(inputs["ctx"], f)
    c = np.asarray(inputs["c"], f)
    c_ctx = np.asarray(inputs["c_ctx"], f)
    consts = np.zeros((128, 640), f)
    jj, ii = np.meshgrid(np.arange(128), np.arange(128), indexing="ij")
    consts[:, 0:128] = (jj == ii)
    consts[:, 128:256] = (jj <= ii)
    consts[:, 256:384] = (jj > ii)
    consts[:, 384:512] = (jj >= ii)
    consts[:, 512:640] = (jj < ii)
    half = 64
    inv = (1.0 / (10000.0 ** (np.arange(0, half, 2, dtype=f) / f(half)))).astype(f)
    pos = np.arange(8192)
    rows = (pos // 64).astype(f)
    cols = (pos % 64).astype(f)
    ar = (rows[:, None] * inv[None, :]).astype(f)
    ac = (cols[:, None] * inv[None, :]).astype(f)
    ropef = np.concatenate([np.cos(ar), np.cos(ac), np.sin(ar), np.sin(ac)], axis=1).astype(f)
    cw = np.zeros((2, 128, NFC * 4), f)
    for l in range(2):
        w4 = np.concatenate([np.asarray(inputs["ffn_conv_w"][l], f), np.asarray(inputs["ffn_conv_b"][l], f)[None]], axis=0)
        cw[l] = w4.reshape(4, NFC, 128).transpose(2, 1, 0).reshape(128, NFC * 4)
    a1 = np.concatenate([np.asarray(inputs["gla_a_w1"][0, 0], f), np.asarray(inputs["gla_a_w1"][0, 1], f)], axis=1)
    a2 = np.zeros((2, 33, 512), f)
    for d in range(2):
        a2[d, 0:16] = inputs["gla_a_w2"][0, d]
        a2[d, 32] = inputs["gla_a_b"][0, d]
    ng = np.tile(np.asarray(inputs["gla_norm_g"][0], f), 4)[None, :]
    rpb = np.asarray(inputs["na_rpb"][0], f)
    cpos = np.arange(64)
    cstart = np.clip(cpos - 8, 0, 48)
    kc = np.arange(64)
    inwin = (kc[None, :] >= cstart[:, None]) & (kc[None, :] < cstart[:, None] + 16)
    dc = np.clip(kc[None, :] - cpos[:, None] + 15, 0, 30)
    Tt = rpb[:, :, dc]
    Tt = np.where(inwin[None, None], Tt, f(-30000.0)).astype(f)
    Tt = np.ascontiguousarray(Tt.transpose(2, 0, 1, 3)).reshape(64, 16 * 15 * 64)
    shared = dict(
        consts=consts, ropef=ropef,
        ada_w=np.asarray(inputs["ada_w"], f), ada_b=np.asarray(inputs["ada_b"], f),
        norm1_g=np.asarray(inputs["norm1_g"], f), norm2_g=np.asarray(inputs["norm2_g"], f),
        ffn_w_in=np.asarray(inputs["ffn_w_in"], f), ffn_cw=cw, ffn_w_out=np.asarray(inputs["ffn_w_out"], f),
        gla_w_in=np.asarray(inputs["gla_w_in"][0], f), gla_a1=a1, gla_a2=a2, gla_ng=ng,
        gla_w_out=np.asarray(inputs["gla_w_out"][0], f),
        na_w_qkv=np.asarray(inputs["na_w_qkv"][0], f), na_T=Tt, na_w_out=np.asarray(inputs["na_w_out"][0], f),
        final_g=np.asarray(inputs["final_g"], f)[None, :],
    )
    maps = []
    for core in range(8):
        b, j = core // 4, core % 4
        t0 = ES[j] * 64
        cT = np.zeros((128, 16), f)
        cT[:, 0::2] = c[b].reshape(8, 128).T
        cT[:, 1::2] = c_ctx.reshape(8, 128).T
        sel = np.zeros((128, 8), f)
        sel[:, j] = 1.0
        sel[:, 4 + j] = 1.0
        m = dict(shared)
        m.update(xfull=x[b], xext=np.ascontiguousarray(x[b, t0:t0 + TEXT]), xctx=ctx[b], cT=cT, sel=sel,
                 ropee=np.ascontiguousarray(ropef[t0:t0 + TEXT]))
        maps.append(m)
    return maps


_NC_CACHE = {}


def run(inputs, stop="all"):
    if stop not in _NC_CACHE:
        _NC_CACHE[stop] = build(stop)
    nc = _NC_CACHE[stop]
    maps = host_inputs(inputs)
    res = run_bass_kernel_spmd(nc, maps, core_ids=list(range(8)))
    return [r["out"] for r in res.results]


def kernel(**inputs):
    outs = run(inputs, "all")
    full = np.zeros((2, 8192, D), np.float32)
    for core in range(8):
        b, j = core // 4, core % 4
        full[b, j * 2048:(j + 1) * 2048] = outs[core][OWN[j] * 64:OWN[j] * 64 + 2048]
    return full
```

```python
import numpy as np
import concourse.bass as bass
import concourse.mybir as mybir
from concourse.bass_utils import run_bass_kernel_spmd

F32 = mybir.dt.float32
BF16 = mybir.dt.bfloat16
AF = mybir.ActivationFunctionType
ALU = mybir.AluOpType
AX = mybir.AxisListType
F32R = mybir.dt.float32r

SAME_ENGINE_SYNC = True
N_DMA_SEMS = 40

D = 1024
NROW = 41
TEXT = NROW * 64
TCTX = 256
ES = [0, 27, 59, 87]
OWN = [0, 5, 5, 9]
EPS = 1e-6
DFF = 2816
NFC = 22


class Res:
    __slots__ = ("name", "w", "rs")

    def __init__(self, name=""):
        self.name = name
        self.w = None
        self.rs = []


class Prog:
    def __init__(self):
        self.nc = bass.Bass("TRN2", target_bir_lowering=False)
        nc = self.nc
        self.eng = {"pe": nc.tensor, "act": nc.scalar, "dve": nc.vector,
                    "pool": nc.gpsimd, "sp": nc.sync}
        self.sem = {e: nc.alloc_semaphore("s_" + e) for e in self.eng}
        self.cnt = {e: 0 for e in self.eng}
        self.dsem = [nc.alloc_semaphore("d%d" % i) for i in range(N_DMA_SEMS)]
        self.dcnt = [0] * N_DMA_SEMS
        self.dnext = 0
        self.seen = {}
        self.n_inst = 0
        self.n_wait = 0
        self.out_tickets = []
        self.stack = []

    def sb(self, name, shape, dt):
        g = self.nc.sbuf_tensor("sb_" + name, list(shape), dt)
        t = g.__enter__()
        self.stack.append(g)
        return t.ap() if hasattr(t, "ap") and callable(t.ap) else t

    def mark(self):
        return len(self.stack)

    def free_to(self, k):
        self.barrier()
        while len(self.stack) > k:
            self.stack.pop().__exit__(None, None, None)

    def barrier(self):
        for e in self.eng:
            for o in self.eng:
                if o != e and self.cnt[o] > 0:
                    self._wait(e, (o, self.cnt[o]))
            for i in range(N_DMA_SEMS):
                if self.dcnt[i] > 0:
                    self._wait(e, (i, self.dcnt[i]))

    def ps(self, name, shape, dt=F32):
        return self.nc.alloc_psum_tensor("ps_" + name, list(shape), dt).ap()

    def dram(self, name, shape, dt, kind="Internal"):
        return self.nc.dram_tensor(name, list(shape), dt, kind=kind).ap()

    def _wait(self, e, ticket, raw=True):
        if ticket is None:
            return
        key, val = ticket
        if isinstance(key, str):
            if key == e and (e == "pe" or not SAME_ENGINE_SYNC or not raw):
                return
            sem = self.sem[key]
        else:
            sem = self.dsem[key]
        k = (e, key)
        if self.seen.get(k, 0) >= val:
            return
        self.seen[k] = val
        self.eng[e].wait_ge(sem, val)
        self.n_wait += 1

    def _deps(self, e, reads, writes, is_dma=False):
        for r in reads:
            self._wait(e, r.w)
        for r in writes:
            self._wait(e, r.w, raw=is_dma)
            for t in r.rs:
                self._wait(e, t, raw=is_dma)

    def _commit(self, ticket, reads, writes):
        for r in reads:
            r.rs.append(ticket)
            if len(r.rs) > 48:
                best = {}
                for k, v in r.rs:
                    if best.get(k, 0) < v:
                        best[k] = v
                r.rs = list(best.items())
        for r in writes:
            r.w = ticket
            r.rs = []

    def op(self, e, fn, reads=(), writes=(), inc=True):
        self._deps(e, reads, writes)
        ins = fn()
        if inc:
            self.cnt[e] += 1
            ins.then_inc(self.sem[e], 1)
            t = (e, self.cnt[e])
        else:
            t = (e, self.cnt[e] + 1)
        self._commit(t, reads, writes)
        self.n_inst += 1
        return t

    def dma(self, q, out, in_, reads=(), writes=(), is_output=False, **kw):
        self._deps(q, reads, writes, is_dma=True)
        i = self.dnext
        self.dnext = (self.dnext + 1) % N_DMA_SEMS
        if self.dcnt[i] > 0:
            self._wait(q, (i, self.dcnt[i]))
        self.dcnt[i] += 16
        self.eng[q].dma_start(out=out, in_=in_, **kw).then_inc(self.dsem[i], 16)
        t = (i, self.dcnt[i])
        self._commit(t, reads, writes)
        if is_output:
            self.out_tickets.append(t)
        self.n_inst += 1
        return t

    def finish(self):
        for i in range(N_DMA_SEMS):
            if self.dcnt[i] > 0:
                self._wait("sp", (i, self.dcnt[i]))
        return self.nc


class T:
    __slots__ = ("ap", "r")

    def __init__(self, ap, name=""):
        self.ap = ap
        self.r = Res(name)


def build(stop="all"):
    p = Prog()
    nc = p.nc
    V, A, G_, PE = nc.vector, nc.scalar, nc.gpsimd, nc.tensor

    def din(name, shape):
        return p.dram(name, shape, F32, kind="ExternalInput")

    xfull = din("xfull", [8192, D])
    xext = din("xext", [TEXT, D])
    xctx = din("xctx", [TCTX, D])
    cT_d = din("cT", [128, 16])
    sel_d = din("sel", [128, 8])
    ropef = din("ropef", [8192, 128])
    ropee = din("ropee", [TEXT, 128])
    consts_d = din("consts", [128, 640])
    ada_w = din("ada_w", [2, D, 6 * D])
    ada_b = din("ada_b", [2, 6 * D])
    norm1_g = din("norm1_g", [2, D])
    norm2_g = din("norm2_g", [2, D])
    ffn_w_in = din("ffn_w_in", [2, D, 2 * DFF])
    ffn_cw = din("ffn_cw", [2, 128, NFC * 4])
    ffn_w_out = din("ffn_w_out", [2, DFF, D])
    gla_w_in = din("gla_w_in", [D, 3 * D])
    gla_a1 = din("gla_a1", [D, 32])
    gla_a2 = din("gla_a2", [2, 33, 512])
    gla_ng = din("gla_ng", [1, D])
    gla_w_out = din("gla_w_out", [D, D])
    na_w_qkv = din("na_w_qkv", [D, 3 * D])
    na_T = din("na_T", [64, 16 * 15 * 64])
    na_w_out = din("na_w_out", [D, D])
    final_g = din("final_g", [1, D])
    out_d = p.dram("out", [TEXT, D], F32, kind="ExternalOutput")

    modd = p.dram("modd", [2, 2, 6 * D], F32)
    r_modd = Res("modd")
    OF = p.dram("OF", [TEXT + TCTX, D], F32)
    Rd = p.dram("Rd", [TEXT + TCTX, D], F32)
    Rd2 = p.dram("Rd2", [TEXT + TCTX, D], F32)
    NT_E = 21
    r_OF = [Res() for _ in range(NT_E + 2)]
    r_Rd = [Res() for _ in range(NT_E + 2)]
    r_Rd2 = [Res() for _ in range(NT_E + 2)]

    consts = T(p.sb("consts", [128, 640], F32))
    p.dma("sp", consts.ap[:], consts_d[:], writes=[consts.r])
    identf = consts.ap[:, 0:128]
    M_incl = [consts.ap[:, 128:256], consts.ap[:, 384:512]]
    M_excl = [consts.ap[:, 256:384], consts.ap[:, 512:640]]
    ident = T(p.sb("ident", [128, 128], BF16))
    p.op("dve", lambda: V.tensor_copy(out=ident.ap[:], in_=identf), reads=[consts.r], writes=[ident.r])
    ones = T(p.sb("ones", [128, 1], F32))
    p.op("dve", lambda: V.memset(ones.ap[:], 1.0), writes=[ones.r])
    sel = T(p.sb("sel", [128, 8], F32))
    p.dma("sp", sel.ap[:], sel_d[:], writes=[sel.r])

    pA = [T(p.ps("pA%d" % i, [128, 512])) for i in range(2)]
    pV = T(p.ps("pV", [128, 1024]))
    pO = T(p.ps("pO", [128, 1024]))
    pT = T(p.ps("pT", [128, 1024], BF16))
    pS = T(p.ps("pS", [128, 512]))
    pa_i = [0]

    def next_pA():
        pa_i[0] ^= 1
        return pA[pa_i[0]]

    modtmp = T(p.sb("modtmp", [128, D], F32))

    xt = [T(p.sb("xt%d" % i, [128, D], F32)) for i in range(2)]
    junk = T(p.sb("junk", [128, D], BF16))
    h32 = T(p.sb("h32", [128, D], F32))
    hb = T(p.sb("hb", [128, D], BF16))
    st = [T(p.sb("st%d" % i, [128, 8], F32)) for i in range(2)]
    st_i = [0]
    qsc = T(p.sb("qsc", [128, 1], F32))
    p.op("dve", lambda: V.memset(qsc.ap[:], float(np.log(128.0 ** -0.5))), writes=[qsc.r])
    eps_c = T(p.sb("eps_c", [128, 1], F32))
    p.op("dve", lambda: V.memset(eps_c.ap[:], EPS), writes=[eps_c.r])
    one_c = T(p.sb("one_c", [128, 1], F32))
    p.op("dve", lambda: V.memset(one_c.ap[:], 1.0), writes=[one_c.r])
    mkM = p.mark()
    cT = T(p.sb("cTs", [128, 16], F32))
    p.dma("sp", cT.ap[:], cT_d[:], writes=[cT.r])
    scT = T(p.sb("scT", [128, 16], F32))
    p.op("act", lambda: A.activation(out=scT.ap[:], in_=cT.ap[:], func=AF.Silu), reads=[cT.r], writes=[scT.r])
    adab = [T(p.sb("adab%d" % i, [2, 512], F32)) for i in range(2)]
    adaw = [T(p.sb("adaw%d" % i, [128, 8, 512], F32)) for i in range(6)]
    modsb = [T(p.sb("modsb%d" % i, [2, 512], F32)) for i in range(2)]
    it = 0
    for l in range(2):
        for n in range(12):
            wt = adaw[it % 6]
            ms = modsb[it % 2]
            ab = adab[it % 2]
            wq = ("sp", "act", "pool")[it % 3]
            it += 1
            for w in range(2):
                p.dma("sp", ab.ap[w:w + 1, :], ada_b[l:l + 1, n * 512:(n + 1) * 512], writes=[ab.r])
            p.dma(wq, wt.ap[:], ada_w[l, :, n * 512:(n + 1) * 512].rearrange("(k p) n -> p k n", p=128),
                  writes=[wt.r])
            pa = next_pA()
            for k in range(8):
                p.op("pe", lambda k=k, pa=pa, wt=wt: PE.matmul(pa.ap[0:2, :], lhsT=scT.ap[:, 2 * k:2 * k + 2], rhs=wt.ap[:, k, :],
                                                         start=(k == 0), stop=(k == 7)),
                     reads=[scT.r, wt.r], writes=[pa.r], inc=(k == 7))
            p.op("dve", lambda pa=pa, ms=ms, ab=ab: V.tensor_tensor(out=ms.ap[:], in0=pa.ap[0:2, :],
                                                                 in1=ab.ap[:, :], op=ALU.add),
                 reads=[pa.r, ab.r], writes=[ms.r])
            p.dma("sp", modd[l, :, n * 512:(n + 1) * 512], ms.ap[:], reads=[ms.r], writes=[r_modd])

    p.free_to(mkM)

    def bload(dst_ap, row_ap, n, writes, reads=()):
        p.dma("sp", dst_ap, row_ap.to_broadcast([128, n]), writes=writes, reads=reads)

    modbuf = {}

    def load_mod(tag, l, w, sub, normg_ap):
        if tag not in modbuf:
            modbuf[tag] = (T(p.sb("G_" + tag, [128, D], F32)), T(p.sb("SH_" + tag, [128, D], F32)),
                           T(p.sb("GT_" + tag, [128, D], F32)))
        Gt, SHt, GTt = modbuf[tag]
        base = sub * 3 * D
        tmp = T(p.sb("ngtmp_" + tag + str(l) + str(sub), [128, D], F32)) if False else modtmp
        bload(SHt.ap[:], modd[l, w:w + 1, base:base + D], D, writes=[SHt.r], reads=[r_modd])
        bload(Gt.ap[:], modd[l, w:w + 1, base + D:base + 2 * D], D, writes=[Gt.r], reads=[r_modd])
        bload(GTt.ap[:], modd[l, w:w + 1, base + 2 * D:base + 3 * D], D, writes=[GTt.r], reads=[r_modd])
        bload(tmp.ap[:], normg_ap, D, writes=[tmp.r])
        p.op("dve", lambda: V.scalar_tensor_tensor(out=Gt.ap[:], in0=Gt.ap[:], scalar=1.0, in1=tmp.ap[:],
                                                   op0=ALU.add, op1=ALU.mult),
             reads=[Gt.r, tmp.r], writes=[Gt.r])
        return Gt, SHt, GTt


    def norm_A1(xtile, n, Gt, SHt, hbuf=None):
        st_i[0] ^= 1
        s = st[st_i[0]]
        xa = xtile.ap
        p.op("act", lambda: A.activation(out=junk.ap[:n, :], in_=xa[:n, :], func=AF.Square, accum_out=s.ap[:n, 0:1]),
             reads=[xtile.r], writes=[junk.r, s.r])
        p.op("act", lambda: A.activation(out=s.ap[:n, 1:2], in_=s.ap[:n, 0:1], func=AF.Ln, scale=1.0 / D, bias=eps_c.ap[:n, 0:1]),
             reads=[s.r, eps_c.r], writes=[s.r])
        p.op("act", lambda: A.activation(out=s.ap[:n, 2:3], in_=s.ap[:n, 1:2], func=AF.Exp, scale=-0.5),
             reads=[s.r], writes=[s.r])
        p.op("dve", lambda: V.scalar_tensor_tensor(out=h32.ap[:n, :], in0=xa[:n, :], scalar=s.ap[:n, 2:3],
                                                   in1=Gt.ap[:n, :], op0=ALU.mult, op1=ALU.mult),
             reads=[xtile.r, s.r, Gt.r], writes=[h32.r])
        hb_ = hb if hbuf is None else hbuf
        p.op("pool", lambda: G_.tensor_tensor(out=hb_.ap[:n, :], in0=h32.ap[:n, :], in1=SHt.ap[:n, :], op=ALU.add),
             reads=[h32.r, SHt.r], writes=[hb_.r])

    def norm_A2(n, hT_ap, hT_r, col0=0, sub=None, hbuf=None):
        hb_ = hb if hbuf is None else hbuf
        for k in range(8):
            p.op("pe", lambda k=k: PE.transpose(out=pT.ap[:, k * 128:k * 128 + n], in_=hb_.ap[:n, k * 128:(k + 1) * 128],
                                                identity=ident.ap[:n, :n]),
                 reads=[hb_.r, ident.r], writes=[pT.r], inc=(k == 7))
        s0, sn = (0, n) if sub is None else sub
        p.op("act", lambda: A.copy(out=hT_ap[:, :, col0:col0 + sn],
                                   in_=pT.ap[:, :].rearrange("p (k t) -> p k t", k=8)[:, :, s0:s0 + sn]),
             reads=[pT.r], writes=[hT_r])

    def norm_mod_T(xtile, n, Gt, SHt, hT_ap, hT_r, col0=0, sub=None, xap=None):
        norm_A1(xtile, n, Gt, SHt)
        norm_A2(n, hT_ap, hT_r, col0, sub)

    mkG = p.mark()
    Wg = T(p.sb("Wg", [128, 8, 3 * D], BF16))
    for g in range(6):
        p.dma("pool", Wg.ap[:, :, g * 512:(g + 1) * 512],
              gla_w_in[:, g * 512:(g + 1) * 512].rearrange("(k p) n -> p k n", p=128), writes=[Wg.r])
    A1 = T(p.sb("A1", [128, 8, 32], BF16))
    p.dma("pool", A1.ap[:], gla_a1.rearrange("(k p) n -> p k n", p=128), writes=[A1.r])
    A2 = T(p.sb("A2", [33, 2, 512], F32))
    p.dma("sp", A2.ap[:], gla_a2.rearrange("d r n -> r d n"), writes=[A2.r])
    Wo = T(p.sb("Wo", [128, 8, D], BF16))
    p.dma("pool", Wo.ap[:], gla_w_out.rearrange("(k p) n -> p k n", p=128), writes=[Wo.r])
    NG = T(p.sb("NG", [128, D], F32))
    bload(NG.ap[:], gla_ng[0:1, :], D, writes=[NG.r])

    G1l, SH1l, GT1l = load_mod("a", 0, 0, 0, norm1_g[0:1, :])
    G1c, SH1c, GT1c = load_mod("b", 0, 1, 0, norm1_g[0:1, :])

    hT = T(p.sb("hT", [128, 8, 128], BF16))
    U = T(p.sb("U", [33, 128], F32))
    p.op("dve", lambda: V.memset(U.ap[:], 0.0), writes=[U.r])
    p.op("dve", lambda: V.memset(U.ap[32:33, :], 1.0), writes=[U.r])
    e1 = T(p.sb("e1", [128, 512], F32))
    sp_ = T(p.sb("sp", [128, 512], F32R))
    Mr = T(p.sb("Mr", [128, 512], F32R))
    p.op("dve", lambda: V.tensor_copy(out=Mr.ap[:, :], in_=consts.ap[:, 128:640]), reads=[consts.r], writes=[Mr.r])
    ones_r = T(p.sb("ones_r", [128, 2], F32R))
    p.op("dve", lambda: V.tensor_copy(out=ones_r.ap[:, :], in_=ones.ap[:, 0:1].to_broadcast([128, 2])), reads=[ones.r], writes=[ones_r.r])
    Mr_incl = [Mr.ap[:, 0:128], Mr.ap[:, 256:384]]
    Mr_excl = [Mr.ap[:, 128:256], Mr.ap[:, 384:512]]
    q32 = T(p.sb("q32", [128, 512], F32))
    k32 = T(p.sb("k32", [128, 512], F32))
    qr = T(p.sb("qr", [128, 512], F32))
    kr = T(p.sb("kr", [128, 512], F32))
    tA = T(p.sb("tA", [128, 256], F32))
    tB = T(p.sb("tB", [128, 256], F32))
    tC = T(p.sb("tC", [128, 256], F32))
    tD = T(p.sb("tD", [128, 256], F32))
    rope_t = [T(p.sb("ropet%d" % i, [128, 128], F32)) for i in range(2)]
    Eq = T(p.sb("Eq", [128, 512], F32))
    Ek = T(p.sb("Ek", [128, 512], F32))
    Dh = T(p.sb("Dh", [128, 512], F32))
    dec = T(p.sb("dec", [128, 4], F32))
    qt = T(p.sb("qt", [128, 512], BF16))
    kt = T(p.sb("kt", [128, 512], BF16))
    kh = T(p.sb("kh", [128, 512], BF16))
    qT = T(p.sb("qT", [128, 4, 128], BF16))
    kT = T(p.sb("kT", [128, 4, 128], BF16))
    vsb = T(p.sb("vsb", [128, D], BF16))
    attm = [T(p.sb("attm%d" % i, [128, 128], BF16)) for i in range(2)]
    S32 = [T(p.sb("S32_%d" % h, [128, 256], F32)) for h in range(4)]
    Sbf = [T(p.sb("Sbf_%d" % h, [128, 256], BF16)) for h in range(4)]
    acc = [[T(p.sb("acc%d_%d" % (d, h), [128, 256], F32)) for h in range(4)] for d in range(2)]
    oft = T(p.sb("oft", [128, D], F32))
    osum = T(p.sb("osum", [128, D], F32))
    sr = T(p.sb("sr", [128, D], F32))
    og = T(p.sb("og", [128, D], BF16))
    ogT = T(p.sb("ogT", [128, 8, 128], BF16))
    ytmp = T(p.sb("ytmp", [128, D], F32))
    xo = T(p.sb("xo", [128, D], F32))

    def load_x(buf, src_ap, n, reads=()):
        p.dma("sp", buf.ap[:n, :], src_ap, writes=[buf.r], reads=reads)

    def gla_tile(xtile, n, d, full, rope_src, Gt, SHt, of_ap=None, of_r=None, GTt=None, dst_ap=None, dst_r=None, first=True, nxt=None):
        if first:
            norm_mod_T(xtile, n, Gt, SHt, hT.ap, hT.r)
        rt = None
        if rope_src is not None:
            rope_t.reverse()
            rt = rope_t[0]
            p.dma("sp", rt.ap[:n, :], rope_src, writes=[rt.r])

        def proj(pa_ap, pa_r, c0, nc_):
            for k in range(8):
                p.op("pe", lambda k=k: PE.matmul(pa_ap[:n, 0:nc_], lhsT=hT.ap[:, k, :n], rhs=Wg.ap[:, k, c0:c0 + nc_],
                                                 start=(k == 0), stop=(k == 7)),
                     reads=[hT.r, Wg.r], writes=[pa_r], inc=(k == 7))

        pu = next_pA()
        for k in range(8):
            p.op("pe", lambda k=k: PE.matmul(pu.ap[0:16, :n], lhsT=A1.ap[:, k, d * 16:(d + 1) * 16], rhs=hT.ap[:, k, :n],
                                             start=(k == 0), stop=(k == 7)), reads=[A1.r, hT.r], writes=[pu.r], inc=(k == 7))
        p.op("act", lambda: A.copy(out=U.ap[0:16, :n], in_=pu.ap[0:16, :n]), reads=[pu.r], writes=[U.r])
        pz = next_pA()
        p.op("pe", lambda: PE.matmul(pz.ap[:n, :], lhsT=U.ap[:, :n], rhs=A2.ap[:, d, :], start=True, stop=True),
             reads=[U.r, A2.r], writes=[pz.r])
        p.op("act", lambda: A.activation(out=e1.ap[:n, :], in_=pz.ap[:n, :], func=AF.Exp, scale=-1.0),
             reads=[pz.r], writes=[e1.r])
        p.op("act", lambda: A.activation(out=sp_.ap[:n, :], in_=e1.ap[:n, :], func=AF.Ln, bias=one_c.ap[:n, 0:1]),
             reads=[e1.r, one_c.r], writes=[sp_.r])
        if nxt is not None:
            norm_A1(nxt[0], nxt[1], Gt, SHt)
        pk = next_pA()
        proj(pk.ap, pk.r, 512, 512)
        p.op("act", lambda: A.copy(out=k32.ap[:n, :], in_=pk.ap[:n, :]), reads=[pk.r], writes=[k32.r])

        def rope(src, dst):
            sv = src.ap[:n, :].rearrange("p (h c f e) -> p h c f e", h=4, c=2, f=2)
            dv = dst.ap[:n, :].rearrange("p (h c f e) -> p h c f e", h=4, c=2, f=2)
            x1, x2 = sv[:, :, :, 0, :], sv[:, :, :, 1, :]
            cs = rt.ap[:n, 0:64].rearrange("p (c e) -> p c e", c=2).unsqueeze(1).to_broadcast([n, 4, 2, 32])
            sn = rt.ap[:n, 64:128].rearrange("p (c e) -> p c e", c=2).unsqueeze(1).to_broadcast([n, 4, 2, 32])
            v4 = lambda t: t.ap[:n, :].rearrange("p (h c e) -> p h c e", h=4, c=2)
            p.op("pool", lambda: G_.tensor_tensor(out=v4(tA), in0=x1, in1=cs, op=ALU.mult), reads=[src.r, rt.r], writes=[tA.r])
            p.op("pool", lambda: G_.tensor_tensor(out=v4(tB), in0=x2, in1=sn, op=ALU.mult), reads=[src.r, rt.r], writes=[tB.r])
            p.op("dve", lambda: V.tensor_tensor(out=v4(tC), in0=x1, in1=sn, op=ALU.mult), reads=[src.r, rt.r], writes=[tC.r])
            p.op("dve", lambda: V.tensor_tensor(out=v4(tD), in0=x2, in1=cs, op=ALU.mult), reads=[src.r, rt.r], writes=[tD.r])
            p.op("pool", lambda: G_.tensor_tensor(out=dv[:, :, :, 0, :], in0=v4(tA), in1=v4(tB), op=ALU.subtract),
                 reads=[tA.r, tB.r], writes=[dst.r])
            p.op("dve", lambda: V.tensor_tensor(out=dv[:, :, :, 1, :], in0=v4(tC), in1=v4(tD), op=ALU.add),
                 reads=[tC.r, tD.r], writes=[dst.r])

        if rt is not None:
            rope(k32, kr)
            krr = kr
        else:
            krr = k32
        pc = next_pA()
        p.op("pe", lambda: PE.matmul(pc.ap[:n, :], lhsT=Mr_excl[d][:n, :n], rhs=sp_.ap[:n, :], start=True, stop=True),
             reads=[Mr.r, sp_.r], writes=[pc.r])
        p.op("act", lambda: A.activation(out=Dh.ap[:n, :], in_=pc.ap[:n, :], func=AF.Exp, scale=-1.0 / 16),
             reads=[pc.r], writes=[Dh.r])
        p.op("dve", lambda: V.tensor_tensor(out=kh.ap[:n, :], in0=krr.ap[:n, :], in1=Dh.ap[:n, :], op=ALU.mult),
             reads=[krr.r, Dh.r], writes=[kh.r])
        pd = next_pA()
        for h in range(4):
            p.op("pe", lambda h=h: PE.matmul(pd.ap[:, 2 * h:2 * h + 2], lhsT=sp_.ap[:n, h * 128:(h + 1) * 128], rhs=ones_r.ap[:n, 0:2],
                                             start=True, stop=True), reads=[sp_.r, ones_r.r], writes=[pd.r], inc=(h == 3))
        p.op("act", lambda: A.activation(out=dec.ap[:, :], in_=pd.ap[:, 0:8].rearrange("p (h t) -> p h t", t=2)[:, :, 0], func=AF.Exp, scale=-1.0 / 16),
             reads=[pd.r], writes=[dec.r])
        if full:
            pc2 = next_pA()
            p.op("pe", lambda: PE.matmul(pc2.ap[:n, :], lhsT=Mr_incl[d][:n, :n], rhs=sp_.ap[:n, :], start=True, stop=True),
                 reads=[Mr.r, sp_.r], writes=[pc2.r])
            p.op("act", lambda: A.activation(out=Eq.ap[:n, :], in_=pc2.ap[:n, :], func=AF.Exp, scale=-1.0 / 16, bias=qsc.ap[:n, 0:1]),
                 reads=[pc2.r, qsc.r], writes=[Eq.r])
            p.op("act", lambda: A.activation(out=Ek.ap[:n, :], in_=pc2.ap[:n, :], func=AF.Exp, scale=1.0 / 16),
                 reads=[pc2.r], writes=[Ek.r])
            pq = next_pA()
            proj(pq.ap, pq.r, 0, 512)
            p.op("act", lambda: A.copy(out=q32.ap[:n, :], in_=pq.ap[:n, :]), reads=[pq.r], writes=[q32.r])
            if rt is not None:
                rope(q32, qr)
                qrr = qr
            else:
                qrr = q32
            p.op("dve", lambda: V.tensor_tensor(out=qt.ap[:n, :], in0=qrr.ap[:n, :], in1=Eq.ap[:n, :], op=ALU.mult),
                 reads=[qrr.r, Eq.r], writes=[qt.r])
            p.op("pool", lambda: G_.tensor_tensor(out=kt.ap[:n, :], in0=krr.ap[:n, :], in1=Ek.ap[:n, :], op=ALU.mult),
                 reads=[krr.r, Ek.r], writes=[kt.r])
            for g in range(2):
                proj(pV.ap[:, g * 512:(g + 1) * 512], pV.r, 1024 + g * 512, 512)
            p.op("act", lambda: A.copy(out=vsb.ap[:n, :], in_=pV.ap[:n, :]), reads=[pV.r], writes=[vsb.r])
            if full and d == 1:
                for g in range(2):
                    proj(pV.ap[:, g * 512:(g + 1) * 512], pV.r, 2048 + g * 512, 512)
                p.op("act", lambda: A.activation(out=sr.ap[:n, :], in_=pV.ap[:n, :], func=AF.Silu), reads=[pV.r], writes=[sr.r])
                p.op("pool", lambda: G_.tensor_tensor(out=sr.ap[:n, :], in0=sr.ap[:n, :], in1=NG.ap[:n, :], op=ALU.mult),
                     reads=[sr.r, NG.r], writes=[sr.r])
            for h in range(4):
                p.op("pe", lambda h=h: PE.transpose(out=pT.ap[:, h * 128:h * 128 + n], in_=qt.ap[:n, h * 128:(h + 1) * 128],
                                                    identity=ident.ap[:n, :n]), reads=[qt.r, ident.r], writes=[pT.r], inc=False)
            for h in range(4):
                p.op("pe", lambda h=h: PE.transpose(out=pT.ap[:, 512 + h * 128:512 + h * 128 + n], in_=kt.ap[:n, h * 128:(h + 1) * 128],
                                                    identity=ident.ap[:n, :n]), reads=[kt.r, ident.r], writes=[pT.r], inc=(h == 3))
            pTv = pT.ap[:, :].rearrange("p (a h t) -> p a h t", a=2, h=4)
            p.op("act", lambda: A.copy(out=qT.ap[:, :, :n], in_=pTv[:, 0, :, :n]), reads=[pT.r], writes=[qT.r])
            p.op("act", lambda: A.copy(out=kT.ap[:, :, :n], in_=pTv[:, 1, :, :n]), reads=[pT.r], writes=[kT.r])
        if not full:
            for g in range(2):
                proj(pV.ap[:, g * 512:(g + 1) * 512], pV.r, 1024 + g * 512, 512)
            p.op("act", lambda: A.copy(out=vsb.ap[:n, :], in_=pV.ap[:n, :]), reads=[pV.r], writes=[vsb.r])
            if full and d == 1:
                for g in range(2):
                    proj(pV.ap[:, g * 512:(g + 1) * 512], pV.r, 2048 + g * 512, 512)
                p.op("act", lambda: A.activation(out=sr.ap[:n, :], in_=pV.ap[:n, :], func=AF.Silu), reads=[pV.r], writes=[sr.r])
                p.op("pool", lambda: G_.tensor_tensor(out=sr.ap[:n, :], in0=sr.ap[:n, :], in1=NG.ap[:n, :], op=ALU.mult),
                     reads=[sr.r, NG.r], writes=[sr.r])
        for h in range(4):
            hs = slice(h * 256, (h + 1) * 256)
            if full:
                am = attm[h % 2]
                psa = pS.ap[:, 256 + (h % 2) * 128:256 + (h % 2) * 128 + 128]
                p.op("pe", lambda h=h, psa=psa: PE.matmul(psa[:n, :n], lhsT=kT.ap[:, h, :n], rhs=qT.ap[:, h, :n], start=True, stop=True),
                     reads=[kT.r, qT.r], writes=[pS.r])
                p.op("dve", lambda am=am, psa=psa: V.tensor_tensor(out=am.ap[:n, :n], in0=psa[:n, :n], in1=M_incl[d][:n, :n], op=ALU.mult),
                     reads=[pS.r, consts.r], writes=[am.r])
                p.op("pe", lambda h=h, hs=hs: PE.matmul(pO.ap[:n, hs], lhsT=qT.ap[:, h, :n], rhs=Sbf[h].ap[:, :], start=True, stop=False),
                     reads=[qT.r, Sbf[h].r], writes=[pO.r], inc=False)
                p.op("pe", lambda h=h, hs=hs, am=am: PE.matmul(pO.ap[:n, hs], lhsT=am.ap[:n, :n], rhs=vsb.ap[:n, hs], start=False, stop=True),
                     reads=[am.r, vsb.r], writes=[pO.r])
            sn_bufs = [pA[0], pA[1]] if full else [pA[0], pA[1], pS]
            snb = sn_bufs[h % len(sn_bufs)]
            p.op("pe", lambda h=h, hs=hs: PE.matmul(snb.ap[:, 0:256], lhsT=kh.ap[:n, h * 128:(h + 1) * 128], rhs=vsb.ap[:n, hs], start=True, stop=True),
                 reads=[kh.r, vsb.r], writes=[snb.r])
            p.op("dve", lambda h=h: V.scalar_tensor_tensor(out=S32[h].ap[:, :], in0=S32[h].ap[:, :], scalar=dec.ap[:, h:h + 1],
                                                           in1=snb.ap[:, 0:256], op0=ALU.mult, op1=ALU.add),
                 reads=[S32[h].r, dec.r, snb.r], writes=[S32[h].r])
            if h == 1 and nxt is not None:
                norm_A2(nxt[1], hT.ap, hT.r)
            if full:
                p.op("act", lambda h=h: A.copy(out=Sbf[h].ap[:, :], in_=S32[h].ap[:, :]), reads=[S32[h].r], writes=[Sbf[h].r])
        if not full:
            return
        if d == 0:
            p.op("act", lambda: A.copy(out=oft.ap[:n, :], in_=pO.ap[:n, :]), reads=[pO.r], writes=[oft.r])
            p.dma("sp", of_ap, oft.ap[:n, :], reads=[oft.r], writes=[of_r])
            return
        p.dma("sp", oft.ap[:n, :], of_ap, reads=[of_r], writes=[oft.r])
        p.op("dve", lambda: V.tensor_tensor(out=osum.ap[:n, :], in0=pO.ap[:n, :], in1=oft.ap[:n, :], op=ALU.add),
             reads=[pO.r, oft.r], writes=[osum.r])
        st_i[0] ^= 1
        s = st[st_i[0]]
        for h in range(4):
            p.op("act", lambda h=h: A.activation(out=junk.ap[:n, 0:256], in_=osum.ap[:n, h * 256:(h + 1) * 256], func=AF.Square,
                                                 accum_out=s.ap[:n, h:h + 1]), reads=[osum.r], writes=[junk.r, s.r])
        p.op("act", lambda: A.activation(out=s.ap[:n, 0:4], in_=s.ap[:n, 0:4], func=AF.Ln, scale=1.0 / 256, bias=eps_c.ap[:n, 0:1]),
             reads=[s.r, eps_c.r], writes=[s.r])
        p.op("act", lambda: A.activation(out=s.ap[:n, 4:8], in_=s.ap[:n, 0:4], func=AF.Exp, scale=-0.5),
             reads=[s.r], writes=[s.r])
        for h in range(4):
            hs = slice(h * 256, (h + 1) * 256)
            p.op("dve", lambda h=h, hs=hs: V.scalar_tensor_tensor(out=og.ap[:n, hs], in0=osum.ap[:n, hs], scalar=s.ap[:n, 4 + h:5 + h],
                                                                  in1=sr.ap[:n, hs], op0=ALU.mult, op1=ALU.mult),
                 reads=[osum.r, s.r, sr.r], writes=[og.r])
        for k in range(8):
            p.op("pe", lambda k=k: PE.transpose(out=pT.ap[:, k * 128:k * 128 + n], in_=og.ap[:n, k * 128:(k + 1) * 128],
                                                identity=ident.ap[:n, :n]), reads=[og.r, ident.r], writes=[pT.r], inc=(k == 7))
        p.op("act", lambda: A.copy(out=ogT.ap[:, :, :n], in_=pT.ap[:, :].rearrange("p (k t) -> p k t", k=8)[:, :, :n]),
             reads=[pT.r], writes=[ogT.r])
        for g in range(2):
            for k in range(8):
                p.op("pe", lambda k=k, g=g: PE.matmul(pO.ap[:n, g * 512:(g + 1) * 512], lhsT=ogT.ap[:, k, :n],
                                                      rhs=Wo.ap[:, k, g * 512:(g + 1) * 512], start=(k == 0), stop=(k == 7)),
                     reads=[ogT.r, Wo.r], writes=[pO.r], inc=(k == 7 and g == 1))
        p.op("dve", lambda: V.tensor_tensor(out=ytmp.ap[:n, :], in0=pO.ap[:n, :], in1=GTt.ap[:n, :], op=ALU.mult),
             reads=[pO.r, GTt.r], writes=[ytmp.r])
        p.op("pool", lambda: G_.tensor_tensor(out=xo.ap[:n, :], in0=ytmp.ap[:n, :], in1=xtile.ap[:n, :], op=ALU.add),
             reads=[ytmp.r, xtile.r], writes=[xo.r])
        p.dma("sp", dst_ap, xo.ap[:n, :], reads=[xo.r], writes=[dst_r])

    def zero_state():
        for h in range(4):
            p.op("dve", lambda h=h: V.memset(S32[h].ap[:], 0.0), writes=[S32[h].r])
            p.op("pool", lambda h=h: G_.memset(Sbf[h].ap[:], 0.0), writes=[Sbf[h].r])

    def take_snap(d, i):
        sc = sel.ap[:, 4 * d + i:4 * d + i + 1]
        for h in range(4):
            if i == (0 if d == 0 else 3):
                p.op("dve", lambda h=h: V.tensor_scalar(out=acc[d][h].ap[:], in0=S32[h].ap[:], scalar1=sc, scalar2=None, op0=ALU.mult),
                     reads=[S32[h].r, sel.r], writes=[acc[d][h].r])
            else:
                p.op("dve", lambda h=h: V.scalar_tensor_tensor(out=acc[d][h].ap[:], in0=S32[h].ap[:], scalar=sc, in1=acc[d][h].ap[:],
                                                                 op0=ALU.mult, op1=ALU.add),
                     reads=[S32[h].r, sel.r, acc[d][h].r], writes=[acc[d][h].r])

    def select_state(d):
        for h in range(4):
            p.op("dve", lambda h=h: V.tensor_copy(out=S32[h].ap[:], in_=acc[d][h].ap[:]), reads=[acc[d][h].r], writes=[S32[h].r])
            p.op("act", lambda h=h: A.copy(out=Sbf[h].ap[:], in_=acc[d][h].ap[:]), reads=[acc[d][h].r], writes=[Sbf[h].r])

    def rows_to_tiles(r0, r1, descending=False):
        tiles = []
        if not descending:
            r = r0
            while r < r1:
                nr = 2 if r + 2 <= r1 else 1
                tiles.append((r * 64, nr * 64))
                r += nr
        else:
            r = r1
            while r > r0:
                nr = 2 if r - 2 >= r0 else 1
                tiles.append(((r - nr) * 64, nr * 64))
                r -= nr
        return tiles

    xi = [0]

    def sweep(tiles, fn, rlist=None, pipelined=False):
        bufs = []
        rd = (lambda i: [rlist[i]]) if rlist is not None else (lambda i: [])
        for idx, (src, n, extra) in enumerate(tiles):
            if idx == 0:
                xi[0] ^= 1
                b = xt[xi[0]]
                load_x(b, src, n, rd(0))
                bufs.append(b)
            if idx + 1 < len(tiles):
                xi[0] ^= 1
                b2 = xt[xi[0]]
                load_x(b2, tiles[idx + 1][0], tiles[idx + 1][1], rd(idx + 1))
                bufs.append(b2)
            nx = (bufs[idx + 1], tiles[idx + 1][1]) if idx + 1 < len(tiles) else None
            if pipelined:
                fn(bufs[idx], n, extra, idx == 0, nx)
            else:
                fn(bufs[idx], n, extra)

    ctx_tiles = [(xctx[i * 128:(i + 1) * 128, :], 128, i) for i in range(2)]
    ext_tiles = [(xext[i * 128:min((i + 1) * 128, TEXT), :], min(128, TEXT - i * 128), i) for i in range(NT_E)]

    zero_state()
    sweep(ctx_tiles, lambda b, n, i, f, nx: gla_tile(b, n, 0, True, None, G1c, SH1c,
                                              of_ap=OF[TEXT + i * 128:TEXT + i * 128 + n, :], of_r=r_OF[NT_E + i], first=f, nxt=nx), pipelined=True)
    take_snap(0, 0)
    for si, (r0, r1) in enumerate([(0, 27), (27, 59), (59, 87)]):
        tl = [(xfull[t0:t0 + n, :], n, t0) for (t0, n) in rows_to_tiles(r0, r1)]
        sweep(tl, lambda b, n, t0, f, nx: gla_tile(b, n, 0, False, ropef[t0:t0 + n, :], G1l, SH1l, first=f, nxt=nx), pipelined=True)
        take_snap(0, si + 1)
    zero_state()
    sweep(ctx_tiles[::-1], lambda b, n, i, f, nx: gla_tile(b, n, 1, True, None, G1c, SH1c,
                                                    of_ap=OF[TEXT + i * 128:TEXT + i * 128 + n, :], of_r=r_OF[NT_E + i], GTt=GT1c,
                                                    dst_ap=Rd[TEXT + i * 128:TEXT + i * 128 + n, :], dst_r=r_Rd[NT_E + i], first=f, nxt=nx), pipelined=True)
    take_snap(1, 3)
    for si, (r0, r1) in zip([2, 1, 0], [(100, 128), (68, 100), (41, 68)]):
        tl = [(xfull[t0:t0 + n, :], n, t0) for (t0, n) in rows_to_tiles(r0, r1, descending=True)]
        sweep(tl, lambda b, n, t0, f, nx: gla_tile(b, n, 1, False, ropef[t0:t0 + n, :], G1l, SH1l, first=f, nxt=nx), pipelined=True)
        take_snap(1, si)
    select_state(0)
    sweep(ext_tiles, lambda b, n, i, f, nx: gla_tile(b, n, 0, True, ropee[i * 128:i * 128 + n, :], G1l, SH1l,
                                              of_ap=OF[i * 128:i * 128 + n, :], of_r=r_OF[i], first=f, nxt=nx), pipelined=True)
    select_state(1)
    sweep(ext_tiles[::-1], lambda b, n, i, f, nx: gla_tile(b, n, 1, True, ropee[i * 128:i * 128 + n, :], G1l, SH1l,
                                                    of_ap=OF[i * 128:i * 128 + n, :], of_r=r_OF[i], GTt=GT1l,
                                                    dst_ap=Rd[i * 128:i * 128 + n, :], dst_r=r_Rd[i], first=f, nxt=nx), pipelined=True)

    def split_groups(tb):
        u = tb // 64
        ng = -(-u // 7)
        base, rem = divmod(u, ng)
        out, t = [], 0
        for i in range(ng):
            sz = (base + (1 if i < rem else 0)) * 64
            out.append((t, sz))
            t += sz
        return out

    SLICES = [(0, 4), (4, 4), (8, 4), (12, 4), (16, 3), (19, 3)]

    def ffn_phase(l, src, r_src, dst, r_dst, with_ctx, final):
        mk = p.mark()
        tagp = "f%d" % l
        G2l, SH2l, GT2l = load_mod(tagp + "a", l, 0, 1, norm2_g[l:l + 1, :])
        if with_ctx:
            G2c, SH2c, GT2c = load_mod(tagp + "b", l, 1, 1, norm2_g[l:l + 1, :])
        CW = T(p.sb(tagp + "CW", [128, NFC * 4], F32))
        p.dma("sp", CW.ap[:], ffn_cw[l], writes=[CW.r])
        if final:
            FG = T(p.sb(tagp + "FG", [128, D], F32))
            bload(FG.ap[:], final_g[0:1, :], D, writes=[FG.r])
        Wa = [T(p.sb(tagp + "Wa%d" % i, [128, 8, 512], BF16)) for i in range(2)]
        Wv = [T(p.sb(tagp + "Wv%d" % i, [128, 8, 512], BF16)) for i in range(2)]
        Wo_ = [T(p.sb(tagp + "Wo%d" % i, [128, 4, D], BF16)) for i in range(2)]
        NTB = 11
        Rb = T(p.sb(tagp + "Rb", [128, NTB, D], F32))
        r_Rb = [Res() for _ in range(NTB)]
        TBMAX = NTB * 128
        H2T = T(p.sb(tagp + "H2T", [128, 8, TBMAX + 2], BF16))
        gT = T(p.sb(tagp + "gT", [128, 4, TBMAX], BF16))
        if with_ctx:
            Rc = T(p.sb(tagp + "Rc", [128, 2, D], F32))
            r_Rc = [Res() for _ in range(2)]
            H2Tc = T(p.sb(tagp + "H2Tc", [128, 8, TCTX + 2], BF16))
            gTc = T(p.sb(tagp + "gTc", [128, 4, TCTX], BF16))
        c1 = [T(p.sb(tagp + "c1_%d" % i, [128, 448], F32)) for i in range(2)]
        c2 = [T(p.sb(tagp + "c2_%d" % i, [128, 448], F32)) for i in range(2)]
        ytm = T(p.sb(tagp + "ytm", [128, D], F32))
        hb2 = T(p.sb(tagp + "hb2", [128, D], BF16))
        pAV = [(pA[0], pA[1]), (T(pV.ap[:, 0:512]), T(pV.ap[:, 512:1024]))]
        pT32 = T(pT.ap.bitcast(F32))
        pT32.r = pT.r
        ybufs = [(T(pO.ap[:, 0:512]), T(pO.ap[:, 512:1024])), (pS, pT32)]
        ycnt = [0]
        wi = [0]

        def load_slice(si):
            f0, nf = SLICES[si]
            b = wi[0] % 2
            wi[0] += 1
            p.dma("pool", Wa[b].ap[:, :, 0:nf * 128], ffn_w_in[l, :, f0 * 128:(f0 + nf) * 128].rearrange("(k p) n -> p k n", p=128),
                  writes=[Wa[b].r])
            p.dma("pool", Wv[b].ap[:, :, 0:nf * 128],
                  ffn_w_in[l, :, DFF + f0 * 128:DFF + (f0 + nf) * 128].rearrange("(k p) n -> p k n", p=128), writes=[Wv[b].r])
            p.dma("pool", Wo_[b].ap[:, 0:nf, :], ffn_w_out[l, f0 * 128:(f0 + nf) * 128, :].rearrange("(c p) n -> p c n", p=128),
                  writes=[Wo_[b].r])
            return b

        blocks = [(0, 11), (11, 21)]
        cnt = [0]
        for bi, (tl0, tl1) in enumerate(blocks):
            tok0 = tl0 * 128
            tb = min(tl1 * 128, TEXT) - tok0
            do_ctx = with_ctx and bi == 0
            nxt = load_slice(0)
            ents = []
            n_pre = 0
            if tl0 == 0:
                p.op("pool", lambda: G_.memset(H2T.ap[:, :, 0:1], 0.0), writes=[H2T.r])
            else:
                xi[0] ^= 1
                hb_ = xt[xi[0]]
                p.dma("sp", hb_.ap[:, :], src[(tl0 - 1) * 128:tl0 * 128, :], reads=[r_src[tl0 - 1]], writes=[hb_.r])
                ents.append((hb_, 128, G2l, SH2l, H2T, 0, (127, 1)))
                n_pre = 1
            for ti in range(tl0, tl1):
                n = min(128, TEXT - ti * 128)
                sl = ti - tl0
                p.dma("sp", Rb.ap[:n, sl, :], src[ti * 128:ti * 128 + n, :], reads=[r_src[ti]], writes=[r_Rb[sl]])
                xtile = T(Rb.ap[:, sl, :])
                xtile.r = r_Rb[sl]
                ents.append((xtile, n, G2l, SH2l, H2T, 1 + sl * 128, None))
            n_tiles_e = len(ents)
            if tl1 == NT_E:
                p.op("pool", lambda: G_.memset(H2T.ap[:, :, 1 + tb:2 + tb], 0.0), writes=[H2T.r])
            else:
                xi[0] ^= 1
                hb_ = xt[xi[0]]
                nn = min(128, TEXT - tl1 * 128)
                p.dma("sp", hb_.ap[:nn, :], src[tl1 * 128:tl1 * 128 + nn, :], reads=[r_src[tl1]], writes=[hb_.r])
                ents.append((hb_, nn, G2l, SH2l, H2T, 1 + tb, (0, 1)))
            n_lat_e = len(ents)
            if do_ctx:
                for ci in range(2):
                    p.dma("sp", Rc.ap[:, ci, :], src[TEXT + ci * 128:TEXT + (ci + 1) * 128, :], reads=[r_src[NT_E + ci]], writes=[r_Rc[ci]])
                    xtile = T(Rc.ap[:, ci, :])
                    xtile.r = r_Rc[ci]
                    ents.append((xtile, 128, G2c, SH2c, H2Tc, 1 + ci * 128, None))
                p.op("pool", lambda: G_.memset(H2Tc.ap[:, :, 0:1], 0.0), writes=[H2Tc.r])
                p.op("pool", lambda: G_.memset(H2Tc.ap[:, :, TCTX + 1:TCTX + 2], 0.0), writes=[H2Tc.r])
            hbs = [hb, hb2]
            e0 = ents[0]
            norm_A1(e0[0], e0[1], e0[2], e0[3], hbuf=hbs[0])
            eptr = [0]
            h2t_res = [Res() for _ in ents]

            def emit_upto(kmax):
                while eptr[0] < kmax:
                    ei = eptr[0]
                    e = ents[ei]
                    if ei + 1 < len(ents):
                        e1 = ents[ei + 1]
                        norm_A1(e1[0], e1[1], e1[2], e1[3], hbuf=hbs[(ei + 1) % 2])
                    norm_A2(e[1], e[4].ap, h2t_res[ei], col0=e[5], sub=e[6], hbuf=hbs[ei % 2])
                    eptr[0] += 1

            lat_groups = split_groups(tb)
            groups = []
            for gi, (g0, ng) in enumerate(lat_groups):
                if gi == len(lat_groups) - 1:
                    need = n_lat_e
                else:
                    need = n_pre + min((g0 + ng) // 128, tl1 - tl0 - 1) + 1
                groups.append((H2T, gT, g0, ng, need))
            if do_ctx:
                groups.append((H2Tc, gTc, 0, TCTX, len(ents)))
            def item(b, fl, fc, Hs, gs, g0, ng, need):
                hres = [Hs.r] + h2t_res[:need]
                cwv = CW.ap[:, fc * 4:fc * 4 + 4]
                pa_, pv_ = pAV[cnt[0] % 2]
                cc1 = c1[cnt[0] % 2]
                cc2 = c2[cnt[0] % 2]
                cnt[0] += 1
                for k in range(8):
                    p.op("pe", lambda k=k, pa_=pa_, Hs=Hs, g0=g0, ng=ng, b=b, fl=fl: PE.matmul(
                        pa_.ap[:, 0:ng + 2], lhsT=Wa[b].ap[:, k, fl * 128:(fl + 1) * 128], rhs=Hs.ap[:, k, g0:g0 + ng + 2],
                        start=(k == 0), stop=(k == 7)), reads=[Wa[b].r] + hres, writes=[pa_.r], inc=(k == 7))
                for k in range(8):
                    p.op("pe", lambda k=k, pv_=pv_, Hs=Hs, g0=g0, ng=ng, b=b, fl=fl: PE.matmul(
                        pv_.ap[:, 0:ng], lhsT=Wv[b].ap[:, k, fl * 128:(fl + 1) * 128], rhs=Hs.ap[:, k, g0 + 1:g0 + 1 + ng],
                        start=(k == 0), stop=(k == 7)), reads=[Wv[b].r] + hres, writes=[pv_.r], inc=(k == 7))
                p.op("act", lambda pa_=pa_, cc1=cc1, ng=ng, cwv=cwv: A.activation(
                    out=cc1.ap[:, 0:ng], in_=pa_.ap[:, 1:ng + 1], func=AF.Identity, scale=cwv[:, 1:2], bias=cwv[:, 3:4]),
                    reads=[pa_.r, CW.r], writes=[cc1.r])
                p.op("dve", lambda pa_=pa_, cc1=cc1, cc2=cc2, ng=ng, cwv=cwv: V.scalar_tensor_tensor(
                    out=cc2.ap[:, 0:ng], in0=pa_.ap[:, 0:ng], scalar=cwv[:, 0:1], in1=cc1.ap[:, 0:ng], op0=ALU.mult, op1=ALU.add),
                    reads=[pa_.r, CW.r, cc1.r], writes=[cc2.r])
                p.op("dve", lambda pa_=pa_, cc1=cc1, cc2=cc2, ng=ng, cwv=cwv: V.scalar_tensor_tensor(
                    out=cc1.ap[:, 0:ng], in0=pa_.ap[:, 2:ng + 2], scalar=cwv[:, 2:3], in1=cc2.ap[:, 0:ng], op0=ALU.mult, op1=ALU.add),
                    reads=[pa_.r, CW.r, cc2.r], writes=[cc1.r])
                p.op("act", lambda cc1=cc1, cc2=cc2, ng=ng: A.activation(out=cc2.ap[:, 0:ng], in_=cc1.ap[:, 0:ng], func=AF.Silu),
                     reads=[cc1.r], writes=[cc2.r])
                p.op("dve", lambda pv_=pv_, cc2=cc2, gs=gs, g0=g0, ng=ng, fl=fl: V.tensor_tensor(
                    out=gs.ap[:, fl, g0:g0 + ng], in0=cc2.ap[:, 0:ng], in1=pv_.ap[:, 0:ng], op=ALU.mult),
                    reads=[cc2.r, pv_.r], writes=[gs.r])

            for si, (f0, nf) in enumerate(SLICES):
                b = nxt
                if si + 1 < len(SLICES):
                    nxt = load_slice(si + 1)
                if si == 0:
                    for (Hs, gs, g0, ng, need) in groups:
                        emit_upto(need)
                        for fl in range(nf):
                            item(b, fl, f0 + fl, Hs, gs, g0, ng, need)
                else:
                    for fl in range(nf):
                        for (Hs, gs, g0, ng, need) in groups:
                            item(b, fl, f0 + fl, Hs, gs, g0, ng, need)
                tiles = [(gT, Rb, r_Rb[ti - tl0], ti - tl0, (ti - tl0) * 128, min(128, TEXT - ti * 128), GT2l) for ti in range(tl0, tl1)]
                if do_ctx:
                    tiles += [(gTc, Rc, r_Rc[ci], ci, ci * 128, 128, GT2c) for ci in range(2)]
                for (gs, Rt, rr, sl, c0, n, GTt) in tiles:
                    yb = ybufs[ycnt[0] % 2]
                    ycnt[0] += 1
                    for g in range(2):
                        yt = yb[g]
                        for fl in range(nf):
                            p.op("pe", lambda: PE.matmul(
                                yt.ap[:n, 0:512], lhsT=gs.ap[:, fl, c0:c0 + n], rhs=Wo_[b].ap[:, fl, g * 512:(g + 1) * 512],
                                start=(fl == 0), stop=(fl == nf - 1)), reads=[gs.r, Wo_[b].r], writes=[yt.r], inc=(fl == nf - 1))
                    for g in range(2):
                        yt = yb[g]
                        p.op("dve", lambda: V.tensor_tensor(out=ytm.ap[:n, g * 512:(g + 1) * 512], in0=yt.ap[:n, 0:512],
                                                            in1=GTt.ap[:n, g * 512:(g + 1) * 512], op=ALU.mult),
                             reads=[yt.r, GTt.r], writes=[ytm.r])
                    p.op("dve", lambda Rt=Rt, sl=sl, n=n: V.tensor_tensor(out=Rt.ap[:n, sl, :], in0=Rt.ap[:n, sl, :], in1=ytm.ap[:n, :], op=ALU.add),
                         reads=[ytm.r, rr], writes=[rr])
            for ti in range(tl0, tl1):
                n = min(128, TEXT - ti * 128)
                sl = ti - tl0
                if not final:
                    p.dma("sp", dst[ti * 128:ti * 128 + n, :], Rb.ap[:n, sl, :], reads=[r_Rb[sl]], writes=[r_dst[ti]])
                else:
                    st_i[0] ^= 1
                    s_ = st[st_i[0]]
                    p.op("act", lambda s_=s_, n=n, sl=sl: A.activation(out=junk.ap[:n, :], in_=Rb.ap[:n, sl, :], func=AF.Square, accum_out=s_.ap[:n, 0:1]),
                         reads=[r_Rb[sl]], writes=[junk.r, s_.r])
                    p.op("act", lambda s_=s_, n=n: A.activation(out=s_.ap[:n, 1:2], in_=s_.ap[:n, 0:1], func=AF.Ln, scale=1.0 / D, bias=eps_c.ap[:n, 0:1]),
                         reads=[s_.r, eps_c.r], writes=[s_.r])
                    p.op("act", lambda s_=s_, n=n: A.activation(out=s_.ap[:n, 2:3], in_=s_.ap[:n, 1:2], func=AF.Exp, scale=-0.5),
                         reads=[s_.r], writes=[s_.r])
                    p.op("dve", lambda s_=s_, n=n, sl=sl: V.scalar_tensor_tensor(out=ytm.ap[:n, :], in0=Rb.ap[:n, sl, :], scalar=s_.ap[:n, 2:3],
                                                                           in1=FG.ap[:n, :], op0=ALU.mult, op1=ALU.mult),
                         reads=[r_Rb[sl], s_.r, FG.r], writes=[ytm.r])
                    p.dma("sp", out_d[ti * 128:ti * 128 + n, :], ytm.ap[:n, :], reads=[ytm.r], is_output=True)
            if do_ctx:
                for ci in range(2):
                    p.dma("sp", dst[TEXT + ci * 128:TEXT + (ci + 1) * 128, :], Rc.ap[:, ci, :], reads=[r_Rc[ci]], writes=[r_dst[NT_E + ci]])
        p.free_to(mk)

    def na_phase(src, r_src, dst, r_dst):
        l = 1
        for hh in range(2):
            mk = p.mark()
            tg = "n%d" % hh
            G1, SH1, GT1 = load_mod(tg + "a", l, 0, 0, norm1_g[l:l + 1, :])
            G1c_, SH1c_, _ = load_mod(tg + "b", l, 1, 0, norm1_g[l:l + 1, :])
            Wq = T(p.sb(tg + "Wq", [128, 8, 512], BF16))
            Wk = T(p.sb(tg + "Wk", [128, 8, 512], BF16))
            Wv2 = T(p.sb(tg + "Wv", [128, 8, 512], BF16))
            for (Wt, c0) in ((Wq, hh * 512), (Wk, D + hh * 512), (Wv2, 2 * D + hh * 512)):
                p.dma("pool", Wt.ap[:], na_w_qkv[:, c0:c0 + 512].rearrange("(k p) n -> p k n", p=128), writes=[Wt.r])
            WoN = T(p.sb(tg + "WoN", [128, 4, D], BF16))
            p.dma("pool", WoN.ap[:], na_w_out[hh * 512:(hh + 1) * 512, :].rearrange("(c p) n -> p c n", p=128), writes=[WoN.r])
            Tt = T(p.sb(tg + "Tt", [128, 4 * 960], BF16))
            naT4 = na_T.rearrange("c (pr hp x) -> c pr hp x", hp=2, x=960)
            for hp_ in range(2):
                p.dma("pool", Tt.ap[hp_ * 64:(hp_ + 1) * 64, :].rearrange("c (pr x) -> c pr x", x=960),
                      naT4[:, hh * 4:(hh + 1) * 4, hp_, :], writes=[Tt.r])
            QT = T(p.sb(tg + "QT", [128, 4, TEXT], BF16))
            KT = T(p.sb(tg + "KT", [128, 4, TEXT], BF16))
            Vt = T(p.sb(tg + "Vt", [128, NT_E, 512], BF16))
            KcT = T(p.sb(tg + "KcT", [128, 4, TCTX], BF16))
            Vc = T(p.sb(tg + "Vc", [128, 2, 512], BF16))
            hT1 = T(p.sb(tg + "hT1", [128, 8, 128], BF16))
            Pb = [T(p.sb(tg + "Pb%d" % i, [128, 896], BF16)) for i in range(2)]
            PTs = [T(p.sb(tg + "PTs%d" % i, [128, 896], BF16)) for i in range(2)]
            BD = [T(p.sb(tg + "BD%d" % i, [128, 128], BF16)) for i in range(2)]
            OT = T(p.sb(tg + "OT", [128, 4, 64], BF16))
            sm = [T(p.sb(tg + "sm%d" % i, [128, 8], F32)) for i in range(2)]
            xrow = [T(p.sb(tg + "xrow%d" % i, [64, D], F32)) for i in range(2)]
            yrow = T(p.sb(tg + "yrow", [64, D], F32))
            for i in range(2):
                p.op("pool", lambda i=i: G_.memset(Pb[i].ap[:], 0.0), writes=[Pb[i].r])
                p.op("pool", lambda i=i: G_.memset(BD[i].ap[:], 0.0), writes=[BD[i].r])
            pT2 = [T(pT.ap[:, 0:896]), T(pA[1].ap.bitcast(BF16)[:, 0:896])]
            pT2[0].r = pT.r
            pT2[1].r = pA[1].r
            Sps = [pV, pO]

            def kv_proj(n, tok0, QT_, KT_, Vdst_ap, Vdst_r, kcols):
                if QT_ is not None:
                    pq = pA[0]
                    for pr in range(4):
                        for k in range(8):
                            p.op("pe", lambda pr=pr, k=k: PE.matmul(pq.ap[:, pr * 128:pr * 128 + n], lhsT=Wq.ap[:, k, pr * 128:(pr + 1) * 128],
                                                                    rhs=hT1.ap[:, k, :n], start=(k == 0), stop=(k == 7)),
                                 reads=[Wq.r, hT1.r], writes=[pq.r], inc=(k == 7 and pr == 3))
                    p.op("act", lambda: A.mul(out=QT_.ap[:, :, tok0:tok0 + n],
                                              in_=pq.ap[:, :].rearrange("p (a t) -> p a t", a=4)[:, :, :n], mul=0.125),
                         reads=[pq.r], writes=[QT_.r])
                pk = pA[1]
                for pr in range(4):
                    for k in range(8):
                        p.op("pe", lambda pr=pr, k=k: PE.matmul(pk.ap[:, pr * 128:pr * 128 + n], lhsT=Wk.ap[:, k, pr * 128:(pr + 1) * 128],
                                                                rhs=hT1.ap[:, k, :n], start=(k == 0), stop=(k == 7)),
                             reads=[Wk.r, hT1.r], writes=[pk.r], inc=(k == 7 and pr == 3))
                p.op("act", lambda: A.copy(out=KT_.ap[:, :, kcols:kcols + n],
                                           in_=pk.ap[:, :].rearrange("p (a t) -> p a t", a=4)[:, :, :n]),
                     reads=[pk.r], writes=[KT_.r])
                for k in range(8):
                    p.op("pe", lambda k=k: PE.matmul(pS.ap[:n, :], lhsT=hT1.ap[:, k, :n], rhs=Wv2.ap[:, k, :], start=(k == 0), stop=(k == 7)),
                         reads=[hT1.r, Wv2.r], writes=[pS.r], inc=(k == 7))
                p.op("dve", lambda: V.tensor_copy(out=Vdst_ap, in_=pS.ap[:n, :]), reads=[pS.r], writes=[Vdst_r])

            def ctx_fn(b, n, ci, first, nx):
                if first:
                    norm_mod_T(b, n, G1c_, SH1c_, hT1.ap, hT1.r)
                if nx is not None:
                    norm_A1(nx[0], nx[1], G1c_, SH1c_)
                kv_proj(n, 0, None, KcT, Vc.ap[:n, ci, :], Vc.r, ci * 128)
                if nx is not None:
                    norm_A2(nx[1], hT1.ap, hT1.r)
            sweep([(src[TEXT + ci * 128:TEXT + (ci + 1) * 128, :], 128, ci) for ci in range(2)], ctx_fn,
                  rlist=[r_src[NT_E + ci] for ci in range(2)], pipelined=True)

            def ext_fn(b, n, ti, first, nx):
                if first:
                    norm_mod_T(b, n, G1, SH1, hT1.ap, hT1.r)
                if nx is not None:
                    norm_A1(nx[0], nx[1], G1, SH1)
                kv_proj(n, ti * 128, QT, KT, Vt.ap[:n, ti, :], Vt.r, ti * 128)
                if nx is not None:
                    norm_A2(nx[1], hT1.ap, hT1.r)
            sweep([(src[ti * 128:ti * 128 + min(128, TEXT - ti * 128), :], min(128, TEXT - ti * 128), ti) for ti in range(NT_E)], ext_fn,
                  rlist=[r_src[ti] for ti in range(NT_E)], pipelined=True)

            items = [(r, pr) for r in range(NROW) for pr in range(4)]

            def row_info(r):
                rs = min(max(r - 4, 0), NROW - 8)
                return rs, rs - r + 7

            def stage1(i):
                r, pr = items[i]
                rs, dr0 = row_info(r)
                S_, pb, smt, bd = Sps[i % 2], Pb[i % 2], sm[i % 2], BD[i % 2]
                if pr == 0:
                    xr = xrow[r % 2]
                    row_src = src if hh == 0 else dst
                    row_res = r_src[r // 2] if hh == 0 else r_dst[r // 2]
                    p.dma("sp", xr.ap[:, :], row_src[r * 64:(r + 1) * 64, :], reads=[row_res], writes=[xr.r])
                for hp in range(2):
                    p.op("pool", lambda: G_.tensor_copy(out=bd.ap[hp * 64:(hp + 1) * 64, hp * 64:(hp + 1) * 64],
                                                        in_=QT.ap[hp * 64:(hp + 1) * 64, pr, r * 64:(r + 1) * 64]),
                         reads=[QT.r], writes=[bd.r])
                p.op("pe", lambda: PE.matmul(S_.ap[:, 0:512], lhsT=bd.ap[:, :], rhs=KT.ap[:, pr, rs * 64:rs * 64 + 512],
                                             start=True, stop=False), reads=[bd.r, KT.r], writes=[S_.r], inc=False)
                tb0 = (pr * 15 + dr0) * 64
                p.op("pe", lambda: PE.matmul(S_.ap[:, 0:512], lhsT=ident.ap[:, :], rhs=Tt.ap[:, tb0:tb0 + 512],
                                             start=False, stop=True), reads=[ident.r, Tt.r], writes=[S_.r], inc=False)
                p.op("pe", lambda: PE.matmul(S_.ap[:, 512:768], lhsT=bd.ap[:, :], rhs=KcT.ap[:, pr, :],
                                             start=True, stop=True), reads=[bd.r, KcT.r], writes=[S_.r])
                p.op("dve", lambda: V.reduce_max(out=smt.ap[:, 0:1], in_=S_.ap[:, 0:768], axis=AX.X), reads=[S_.r], writes=[smt.r])
                p.op("dve", lambda: V.tensor_scalar(out=smt.ap[:, 1:2], in0=smt.ap[:, 0:1], scalar1=-1.0, scalar2=None, op0=ALU.mult),
                     reads=[smt.r], writes=[smt.r])
                p.op("act", lambda: A.activation(out=pb.ap[:, 64:576], in_=S_.ap[:, 0:512], func=AF.Exp, bias=smt.ap[:, 1:2],
                                                 accum_out=smt.ap[:, 2:3]), reads=[S_.r, smt.r], writes=[pb.r, smt.r])
                p.op("act", lambda: A.activation(out=pb.ap[:, 640:896], in_=S_.ap[:, 512:768], func=AF.Exp, bias=smt.ap[:, 1:2],
                                                 accum_out=smt.ap[:, 3:4]), reads=[S_.r, smt.r], writes=[pb.r, smt.r])

            ktab = {}
            pend = []

            def stage2pre(i):
                pb, smt = Pb[i % 2], sm[i % 2]
                p.op("dve", lambda: V.tensor_tensor(out=smt.ap[:, 4:5], in0=smt.ap[:, 2:3], in1=smt.ap[:, 3:4], op=ALU.add),
                     reads=[smt.r], writes=[smt.r])
                p.op("dve", lambda: V.reciprocal(out=smt.ap[:, 5:6], in_=smt.ap[:, 4:5]), reads=[smt.r], writes=[smt.r])
                p.op("dve", lambda: V.tensor_scalar(out=pb.ap[:, 64:896], in0=pb.ap[:, 64:896], scalar1=smt.ap[:, 5:6], scalar2=None, op0=ALU.mult),
                     reads=[pb.r, smt.r], writes=[pb.r])

            def stage2a(i):
                r, pr = items[i]
                rs, dr0 = row_info(r)
                pb, smt, pts, ptp = Pb[i % 2], sm[i % 2], PTs[i % 2], pT2[i % 2]
                kts = []
                if rs % 2 == 0:
                    for t in range(4):
                        kts.append((64 + 128 * t, 128, ("v", rs // 2 + t)))
                else:
                    for t in range(5):
                        vt = (rs - 1) // 2 + t
                        kk = 128 if vt * 128 + 128 <= TEXT else 64
                        kts.append((128 * t, kk, ("v", vt)))
                kts.append((640, 128, ("c", 0)))
                kts.append((768, 128, ("c", 1)))
                nt = len(kts)
                for t, (c0, kk, _) in enumerate(kts):
                    p.op("pe", lambda: PE.transpose(out=ptp.ap[:kk, t * 128:(t + 1) * 128], in_=pb.ap[:, c0:c0 + kk],
                                                    identity=ident.ap[:, :]),
                         reads=[pb.r, ident.r], writes=[ptp.r], inc=(t == nt - 1))
                p.op("dve", lambda: V.tensor_copy(out=pts.ap[:, 0:nt * 128], in_=ptp.ap[:, 0:nt * 128]), reads=[ptp.r], writes=[pts.r])
                ktab[i] = kts

            def stage2b(i):
                r, pr = items[i]
                pts = PTs[i % 2]
                kts = ktab.pop(i)
                nt = len(kts)
                if pend:
                    out_proj_row(pend.pop())
                po = pS.ap[:, pr * 128:(pr + 1) * 128]
                for t, (c0, kk, (kind, vi)) in enumerate(kts):
                    vl = Vt.ap[:kk, vi, pr * 128:(pr + 1) * 128] if kind == "v" else Vc.ap[:kk, vi, pr * 128:(pr + 1) * 128]
                    p.op("pe", lambda: PE.matmul(po, lhsT=vl, rhs=pts.ap[:kk, t * 128:(t + 1) * 128],
                                                 start=(t == 0), stop=(t == nt - 1)),
                         reads=[pts.r, Vt.r, Vc.r], writes=[pS.r], inc=(t == nt - 1))
                p.op("act", lambda: A.copy(out=OT.ap[0:64, pr, :], in_=pS.ap[0:64, pr * 128:pr * 128 + 64]), reads=[pS.r], writes=[OT.r])
                p.op("act", lambda: A.copy(out=OT.ap[64:128, pr, :], in_=pS.ap[64:128, pr * 128 + 64:pr * 128 + 128]), reads=[pS.r], writes=[OT.r])
                if pr == 3:
                    pend.append(r)

            def out_proj_row(r):
                xr = xrow[r % 2]
                for g in range(2):
                    for kk in range(4):
                        p.op("pe", lambda: PE.matmul(pA[g].ap[0:64, :], lhsT=OT.ap[:, kk, :], rhs=WoN.ap[:, kk, g * 512:(g + 1) * 512],
                                                     start=(kk == 0), stop=(kk == 3)), reads=[OT.r, WoN.r], writes=[pA[g].r], inc=(kk == 3))
                    p.op("dve", lambda: V.tensor_tensor(out=yrow.ap[:, g * 512:(g + 1) * 512], in0=pA[g].ap[0:64, :],
                                                        in1=GT1.ap[0:64, g * 512:(g + 1) * 512], op=ALU.mult),
                         reads=[pA[g].r, GT1.r], writes=[yrow.r])
                p.op("pool", lambda: G_.tensor_tensor(out=xr.ap[:, :], in0=xr.ap[:, :], in1=yrow.ap[:, :], op=ALU.add),
                     reads=[xr.r, yrow.r], writes=[xr.r])
                p.dma("sp", dst[r * 64:(r + 1) * 64, :], xr.ap[:, :], reads=[xr.r], writes=[r_dst[r // 2]])

            stage1(0)
            for i in range(len(items)):
                stage2pre(i)
                if i + 1 < len(items):
                    stage1(i + 1)
                if i >= 1:
                    stage2b(i - 1)
                stage2a(i)
            stage2b(len(items) - 1)
            out_proj_row(pend.pop())
            p.free_to(mk)

    def dump(src, r_src):
        for i in range(NT_E):
            n = min(128, TEXT - i * 128)
            b = xt[i % 2]
            p.dma("sp", b.ap[:n, :], src[i * 128:i * 128 + n, :], reads=[r_src[i]], writes=[b.r])
            p.dma("sp", out_d[i * 128:i * 128 + n, :], b.ap[:n, :], reads=[b.r], is_output=True)
        print("instructions", p.n_inst, "waits", p.n_wait)
        return p.finish()

    if stop == "gla":
        return dump(Rd, r_Rd)
    p.free_to(mkG)
    ffn_phase(0, Rd, r_Rd, Rd2, r_Rd2, True, False)
    if stop == "ffn0":
        return dump(Rd2, r_Rd2)
    if stop == "ffn0c":
        for i in range(2):
            b = xt[i % 2]
            p.dma("sp", b.ap[:, :], Rd2[TEXT + i * 128:TEXT + (i + 1) * 128, :], reads=[r_Rd2[NT_E + i]], writes=[b.r])
            p.dma("sp", out_d[i * 128:(i + 1) * 128, :], b.ap[:, :], reads=[b.r], is_output=True)
        return p.finish()

    na_phase(Rd2, r_Rd2, Rd, r_Rd)
    if stop == "na":
        return dump(Rd, r_Rd)
    ffn_phase(1, Rd, r_Rd, None, None, False, True)
    print("instructions", p.n_inst, "waits", p.n_wait)
    return p.finish()


def host_inputs(inputs):
    f = np.float32
    x = np.asarray(inputs["x"], f)
    ctx = np.asarray(inputs["ctx"], f)
    c = np.asarray(inputs["c"], f)
    c_ctx = np.asarray(inputs["c_ctx"], f)
    consts = np.zeros((128, 640), f)
    jj, ii = np.meshgrid(np.arange(128), np.arange(128), indexing="ij")
    consts[:, 0:128] = (jj == ii)
    consts[:, 128:256] = (jj <= ii)
    consts[:, 256:384] = (jj > ii)
    consts[:, 384:512] = (jj >= ii)
    consts[:, 512:640] = (jj < ii)
    half = 64
    inv = (1.0 / (10000.0 ** (np.arange(0, half, 2, dtype=f) / f(half)))).astype(f)
    pos = np.arange(8192)
    rows = (pos // 64).astype(f)
    cols = (pos % 64).astype(f)
    ar = (rows[:, None] * inv[None, :]).astype(f)
    ac = (cols[:, None] * inv[None, :]).astype(f)
    ropef = np.concatenate([np.cos(ar), np.cos(ac), np.sin(ar), np.sin(ac)], axis=1).astype(f)
    cw = np.zeros((2, 128, NFC * 4), f)
    for l in range(2):
        w4 = np.concatenate([np.asarray(inputs["ffn_conv_w"][l], f), np.asarray(inputs["ffn_conv_b"][l], f)[None]], axis=0)
        cw[l] = w4.reshape(4, NFC, 128).transpose(2, 1, 0).reshape(128, NFC * 4)
    a1 = np.concatenate([np.asarray(inputs["gla_a_w1"][0, 0], f), np.asarray(inputs["gla_a_w1"][0, 1], f)], axis=1)
    a2 = np.zeros((2, 33, 512), f)
    for d in range(2):
        a2[d, 0:16] = inputs["gla_a_w2"][0, d]
        a2[d, 32] = inputs["gla_a_b"][0, d]
    ng = np.tile(np.asarray(inputs["gla_norm_g"][0], f), 4)[None, :]
    rpb = np.asarray(inputs["na_rpb"][0], f)
    cpos = np.arange(64)
    cstart = np.clip(cpos - 8, 0, 48)
    kc = np.arange(64)
    inwin = (kc[None, :] >= cstart[:, None]) & (kc[None, :] < cstart[:, None] + 16)
    dc = np.clip(kc[None, :] - cpos[:, None] + 15, 0, 30)
    Tt = rpb[:, :, dc]
    Tt = np.where(inwin[None, None], Tt, f(-30000.0)).astype(f)
    Tt = np.ascontiguousarray(Tt.transpose(2, 0, 1, 3)).reshape(64, 16 * 15 * 64)
    shared = dict(
        consts=consts, ropef=ropef,
        ada_w=np.asarray(inputs["ada_w"], f), ada_b=np.asarray(inputs["ada_b"], f),
        norm1_g=np.asarray(inputs["norm1_g"], f), norm2_g=np.asarray(inputs["norm2_g"], f),
        ffn_w_in=np.asarray(inputs["ffn_w_in"], f), ffn_cw=cw, ffn_w_out=np.asarray(inputs["ffn_w_out"], f),
        gla_w_in=np.asarray(inputs["gla_w_in"][0], f), gla_a1=a1, gla_a2=a2, gla_ng=ng,
        gla_w_out=np.asarray(inputs["gla_w_out"][0], f),
        na_w_qkv=np.asarray(inputs["na_w_qkv"][0], f), na_T=Tt, na_w_out=np.asarray(inputs["na_w_out"][0], f),
        final_g=np.asarray(inputs["final_g"], f)[None, :],
    )
    maps = []
    for core in range(8):
        b, j = core // 4, core % 4
        t0 = ES[j] * 64
        cT = np.zeros((128, 16), f)
        cT[:, 0::2] = c[b].reshape(8, 128).T
        cT[:, 1::2] = c_ctx.reshape(8, 128).T
        sel = np.zeros((128, 8), f)
        sel[:, j] = 1.0
        sel[:, 4 + j] = 1.0
        m = dict(shared)
        m.update(xfull=x[b], xext=np.ascontiguousarray(x[b, t0:t0 + TEXT]), xctx=ctx[b], cT=cT, sel=sel,
                 ropee=np.ascontiguousarray(ropef[t0:t0 + TEXT]))
        maps.append(m)
    return maps


_NC_CACHE = {}


def run(inputs, stop="all"):
    if stop not in _NC_CACHE:
        _NC_CACHE[stop] = build(stop)
    nc = _NC_CACHE[stop]
    maps = host_inputs(inputs)
    res = run_bass_kernel_spmd(nc, maps, core_ids=list(range(8)))
    return [r["out"] for r in res.results]


def kernel(**inputs):
    outs = run(inputs, "all")
    full = np.zeros((2, 8192, D), np.float32)
    for core in range(8):
        b, j = core // 4, core % 4
        full[b, j * 2048:(j + 1) * 2048] = outs[core][OWN[j] * 64:OWN[j] * 64 + 2048]
    return full
```

```python
import numpy as np
import concourse.bass as bass
import concourse.mybir as mybir
from concourse.bass_utils import run_bass_kernel_spmd

F32 = mybir.dt.float32
BF16 = mybir.dt.bfloat16
AF = mybir.ActivationFunctionType
ALU = mybir.AluOpType
AX = mybir.AxisListType
F32R = mybir.dt.float32r

SAME_ENGINE_SYNC = True
N_DMA_SEMS = 40

D = 1024
NROW = 41
TEXT = NROW * 64
TCTX = 256
ES = [0, 27, 59, 87]
OWN = [0, 5, 5, 9]
EPS = 1e-6
DFF = 2816
NFC = 22


class Res:
    __slots__ = ("name", "w", "rs")

    def __init__(self, name=""):
        self.name = name
        self.w = None
        self.rs = []


class Prog:
    def __init__(self):
        self.nc = bass.Bass("TRN2", target_bir_lowering=False)
        nc = self.nc
        self.eng = {"pe": nc.tensor, "act": nc.scalar, "dve": nc.vector,
                    "pool": nc.gpsimd, "sp": nc.sync}
        self.sem = {e: nc.alloc_semaphore("s_" + e) for e in self.eng}
        self.cnt = {e: 0 for e in self.eng}
        self.dsem = [nc.alloc_semaphore("d%d" % i) for i in range(N_DMA_SEMS)]
        self.dcnt = [0] * N_DMA_SEMS
        self.dnext = 0
        self.seen = {}
        self.n_inst = 0
        self.n_wait = 0
        self.out_tickets = []
        self.stack = []

    def sb(self, name, shape, dt):
        g = self.nc.sbuf_tensor("sb_" + name, list(shape), dt)
        t = g.__enter__()
        self.stack.append(g)
        return t.ap() if hasattr(t, "ap") and callable(t.ap) else t

    def mark(self):
        return len(self.stack)

    def free_to(self, k):
        self.barrier()
        while len(self.stack) > k:
            self.stack.pop().__exit__(None, None, None)

    def barrier(self):
        for e in self.eng:
            for o in self.eng:
                if o != e and self.cnt[o] > 0:
                    self._wait(e, (o, self.cnt[o]))
            for i in range(N_DMA_SEMS):
                if self.dcnt[i] > 0:
                    self._wait(e, (i, self.dcnt[i]))

    def ps(self, name, shape, dt=F32):
        return self.nc.alloc_psum_tensor("ps_" + name, list(shape), dt).ap()

    def dram(self, name, shape, dt, kind="Internal"):
        return self.nc.dram_tensor(name, list(shape), dt, kind=kind).ap()

    def _wait(self, e, ticket, raw=True):
        if ticket is None:
            return
        key, val = ticket
        if isinstance(key, str):
            if key == e and (e == "pe" or not SAME_ENGINE_SYNC or not raw):
                return
            sem = self.sem[key]
        else:
            sem = self.dsem[key]
        k = (e, key)
        if self.seen.get(k, 0) >= val:
            return
        self.seen[k] = val
        self.eng[e].wait_ge(sem, val)
        self.n_wait += 1

    def _deps(self, e, reads, writes, is_dma=False):
        for r in reads:
            self._wait(e, r.w)
        for r in writes:
            self._wait(e, r.w, raw=is_dma)
            for t in r.rs:
                self._wait(e, t, raw=is_dma)

    def _commit(self, ticket, reads, writes):
        for r in reads:
            r.rs.append(ticket)
            if len(r.rs) > 48:
                best = {}
                for k, v in r.rs:
                    if best.get(k, 0) < v:
                        best[k] = v
                r.rs = list(best.items())
        for r in writes:
            r.w = ticket
            r.rs = []

    def op(self, e, fn, reads=(), writes=(), inc=True):
        self._deps(e, reads, writes)
        ins = fn()
        if inc:
            self.cnt[e] += 1
            ins.then_inc(self.sem[e], 1)
            t = (e, self.cnt[e])
        else:
            t = (e, self.cnt[e] + 1)
        self._commit(t, reads, writes)
        self.n_inst += 1
        return t

    def dma(self, q, out, in_, reads=(), writes=(), is_output=False, **kw):
        self._deps(q, reads, writes, is_dma=True)
        i = self.dnext
        self.dnext = (self.dnext + 1) % N_DMA_SEMS
        if self.dcnt[i] > 0:
            self._wait(q, (i, self.dcnt[i]))
        self.dcnt[i] += 16
        self.eng[q].dma_start(out=out, in_=in_, **kw).then_inc(self.dsem[i], 16)
        t = (i, self.dcnt[i])
        self._commit(t, reads, writes)
        if is_output:
            self.out_tickets.append(t)
        self.n_inst += 1
        return t

    def finish(self):
        for i in range(N_DMA_SEMS):
            if self.dcnt[i] > 0:
                self._wait("sp", (i, self.dcnt[i]))
        return self.nc


class T:
    __slots__ = ("ap", "r")

    def __init__(self, ap, name=""):
        self.ap = ap
        self.r = Res(name)


def build(stop="all"):
    p = Prog()
    nc = p.nc
    V, A, G_, PE = nc.vector, nc.scalar, nc.gpsimd, nc.tensor

    def din(name, shape):
        return p.dram(name, shape, F32, kind="ExternalInput")

    xfull = din("xfull", [8192, D])
    xext = din("xext", [TEXT, D])
    xctx = din("xctx", [TCTX, D])
    cT_d = din("cT", [128, 16])
    sel_d = din("sel", [128, 8])
    ropef = din("ropef", [8192, 128])
    ropee = din("ropee", [TEXT, 128])
    consts_d = din("consts", [128, 640])
    ada_w = din("ada_w", [2, D, 6 * D])
    ada_b = din("ada_b", [2, 6 * D])
    norm1_g = din("norm1_g", [2, D])
    norm2_g = din("norm2_g", [2, D])
    ffn_w_in = din("ffn_w_in", [2, D, 2 * DFF])
    ffn_cw = din("ffn_cw", [2, 128, NFC * 4])
    ffn_w_out = din("ffn_w_out", [2, DFF, D])
    gla_w_in = din("gla_w_in", [D, 3 * D])
    gla_a1 = din("gla_a1", [D, 32])
    gla_a2 = din("gla_a2", [2, 33, 512])
    gla_ng = din("gla_ng", [1, D])
    gla_w_out = din("gla_w_out", [D, D])
    na_w_qkv = din("na_w_qkv", [D, 3 * D])
    na_T = din("na_T", [64, 16 * 15 * 64])
    na_w_out = din("na_w_out", [D, D])
    final_g = din("final_g", [1, D])
    out_d = p.dram("out", [TEXT, D], F32, kind="ExternalOutput")

    modd = p.dram("modd", [2, 2, 6 * D], F32)
    r_modd = Res("modd")
    OF = p.dram("OF", [TEXT + TCTX, D], F32)
    Rd = p.dram("Rd", [TEXT + TCTX, D], F32)
    Rd2 = p.dram("Rd2", [TEXT + TCTX, D], F32)
    NT_E = 21
    r_OF = [Res() for _ in range(NT_E + 2)]
    r_Rd = [Res() for _ in range(NT_E + 2)]
    r_Rd2 = [Res() for _ in range(NT_E + 2)]

    consts = T(p.sb("consts", [128, 640], F32))
    p.dma("sp", consts.ap[:], consts_d[:], writes=[consts.r])
    identf = consts.ap[:, 0:128]
    M_incl = [consts.ap[:, 128:256], consts.ap[:, 384:512]]
    M_excl = [consts.ap[:, 256:384], consts.ap[:, 512:640]]
    ident = T(p.sb("ident", [128, 128], BF16))
    p.op("dve", lambda: V.tensor_copy(out=ident.ap[:], in_=identf), reads=[consts.r], writes=[ident.r])
    ones = T(p.sb("ones", [128, 1], F32))
    p.op("dve", lambda: V.memset(ones.ap[:], 1.0), writes=[ones.r])
    sel = T(p.sb("sel", [128, 8], F32))
    p.dma("sp", sel.ap[:], sel_d[:], writes=[sel.r])

    pA = [T(p.ps("pA%d" % i, [128, 512])) for i in range(2)]
    pV = T(p.ps("pV", [128, 1024]))
    pO = T(p.ps("pO", [128, 1024]))
    pT = T(p.ps("pT", [128, 1024], BF16))
    pS = T(p.ps("pS", [128, 512]))
    pa_i = [0]

    def next_pA():
        pa_i[0] ^= 1
        return pA[pa_i[0]]

    modtmp = T(p.sb("modtmp", [128, D], F32))

    xt = [T(p.sb("xt%d" % i, [128, D], F32)) for i in range(2)]
    junk = T(p.sb("junk", [128, D], BF16))
    h32 = T(p.sb("h32", [128, D], F32))
    hb = T(p.sb("hb", [128, D], BF16))
    st = [T(p.sb("st%d" % i, [128, 8], F32)) for i in range(2)]
    st_i = [0]
    qsc = T(p.sb("qsc", [128, 1], F32))
    p.op("dve", lambda: V.memset(qsc.ap[:], float(np.log(128.0 ** -0.5))), writes=[qsc.r])
    eps_c = T(p.sb("eps_c", [128, 1], F32))
    p.op("dve", lambda: V.memset(eps_c.ap[:], EPS), writes=[eps_c.r])
    one_c = T(p.sb("one_c", [128, 1], F32))
    p.op("dve", lambda: V.memset(one_c.ap[:], 1.0), writes=[one_c.r])
    mkM = p.mark()
    cT = T(p.sb("cTs", [128, 16], F32))
    p.dma("sp", cT.ap[:], cT_d[:], writes=[cT.r])
    scT = T(p.sb("scT", [128, 16], F32))
    p.op("act", lambda: A.activation(out=scT.ap[:], in_=cT.ap[:], func=AF.Silu), reads=[cT.r], writes=[scT.r])
    adab = [T(p.sb("adab%d" % i, [2, 512], F32)) for i in range(2)]
    adaw = [T(p.sb("adaw%d" % i, [128, 8, 512], F32)) for i in range(6)]
    modsb = [T(p.sb("modsb%d" % i, [2, 512], F32)) for i in range(2)]
    it = 0
    for l in range(2):
        for n in range(12):
            wt = adaw[it % 6]
            ms = modsb[it % 2]
            ab = adab[it % 2]
            wq = ("sp", "act", "pool")[it % 3]
            it += 1
            for w in range(2):
                p.dma("sp", ab.ap[w:w + 1, :], ada_b[l:l + 1, n * 512:(n + 1) * 512], writes=[ab.r])
            p.dma(wq, wt.ap[:], ada_w[l, :, n * 512:(n + 1) * 512].rearrange("(k p) n -> p k n", p=128),
                  writes=[wt.r])
            pa = next_pA()
            for k in range(8):
                p.op("pe", lambda k=k, pa=pa, wt=wt: PE.matmul(pa.ap[0:2, :], lhsT=scT.ap[:, 2 * k:2 * k + 2], rhs=wt.ap[:, k, :],
                                                         start=(k == 0), stop=(k == 7)),
                     reads=[scT.r, wt.r], writes=[pa.r], inc=(k == 7))
            p.op("dve", lambda pa=pa, ms=ms, ab=ab: V.tensor_tensor(out=ms.ap[:], in0=pa.ap[0:2, :],
                                                                 in1=ab.ap[:, :], op=ALU.add),
                 reads=[pa.r, ab.r], writes=[ms.r])
            p.dma("sp", modd[l, :, n * 512:(n + 1) * 512], ms.ap[:], reads=[ms.r], writes=[r_modd])

    p.free_to(mkM)

    def bload(dst_ap, row_ap, n, writes, reads=()):
        p.dma("sp", dst_ap, row_ap.to_broadcast([128, n]), writes=writes, reads=reads)

    modbuf = {}

    def load_mod(tag, l, w, sub, normg_ap):
        if tag not in modbuf:
            modbuf[tag] = (T(p.sb("G_" + tag, [128, D], F32)), T(p.sb("SH_" + tag, [128, D], F32)),
                           T(p.sb("GT_" + tag, [128, D], F32)))
        Gt, SHt, GTt = modbuf[tag]
        base = sub * 3 * D
        tmp = T(p.sb("ngtmp_" + tag + str(l) + str(sub), [128, D], F32)) if False else modtmp
        bload(SHt.ap[:], modd[l, w:w + 1, base:base + D], D, writes=[SHt.r], reads=[r_modd])
        bload(Gt.ap[:], modd[l, w:w + 1, base + D:base + 2 * D], D, writes=[Gt.r], reads=[r_modd])
        bload(GTt.ap[:], modd[l, w:w + 1, base + 2 * D:base + 3 * D], D, writes=[GTt.r], reads=[r_modd])
        bload(tmp.ap[:], normg_ap, D, writes=[tmp.r])
        p.op("dve", lambda: V.scalar_tensor_tensor(out=Gt.ap[:], in0=Gt.ap[:], scalar=1.0, in1=tmp.ap[:],
                                                   op0=ALU.add, op1=ALU.mult),
             reads=[Gt.r, tmp.r], writes=[Gt.r])
        return Gt, SHt, GTt


    def norm_A1(xtile, n, Gt, SHt, hbuf=None):
        st_i[0] ^= 1
        s = st[st_i[0]]
        xa = xtile.ap
        p.op("act", lambda: A.activation(out=junk.ap[:n, :], in_=xa[:n, :], func=AF.Square, accum_out=s.ap[:n, 0:1]),
             reads=[xtile.r], writes=[junk.r, s.r])
        p.op("act", lambda: A.activation(out=s.ap[:n, 1:2], in_=s.ap[:n, 0:1], func=AF.Ln, scale=1.0 / D, bias=eps_c.ap[:n, 0:1]),
             reads=[s.r, eps_c.r], writes=[s.r])
        p.op("act", lambda: A.activation(out=s.ap[:n, 2:3], in_=s.ap[:n, 1:2], func=AF.Exp, scale=-0.5),
             reads=[s.r], writes=[s.r])
        p.op("dve", lambda: V.scalar_tensor_tensor(out=h32.ap[:n, :], in0=xa[:n, :], scalar=s.ap[:n, 2:3],
                                                   in1=Gt.ap[:n, :], op0=ALU.mult, op1=ALU.mult),
             reads=[xtile.r, s.r, Gt.r], writes=[h32.r])
        hb_ = hb if hbuf is None else hbuf
        p.op("pool", lambda: G_.tensor_tensor(out=hb_.ap[:n, :], in0=h32.ap[:n, :], in1=SHt.ap[:n, :], op=ALU.add),
             reads=[h32.r, SHt.r], writes=[hb_.r])

    def norm_A2(n, hT_ap, hT_r, col0=0, sub=None, hbuf=None):
        hb_ = hb if hbuf is None else hbuf
        for k in range(8):
            p.op("pe", lambda k=k: PE.transpose(out=pT.ap[:, k * 128:k * 128 + n], in_=hb_.ap[:n, k * 128:(k + 1) * 128],
                                                identity=ident.ap[:n, :n]),
                 reads=[hb_.r, ident.r], writes=[pT.r], inc=(k == 7))
        s0, sn = (0, n) if sub is None else sub
        p.op("act", lambda: A.copy(out=hT_ap[:, :, col0:col0 + sn],
                                   in_=pT.ap[:, :].rearrange("p (k t) -> p k t", k=8)[:, :, s0:s0 + sn]),
             reads=[pT.r], writes=[hT_r])

    def norm_mod_T(xtile, n, Gt, SHt, hT_ap, hT_r, col0=0, sub=None, xap=None):
        norm_A1(xtile, n, Gt, SHt)
        norm_A2(n, hT_ap, hT_r, col0, sub)

    mkG = p.mark()
    Wg = T(p.sb("Wg", [128, 8, 3 * D], BF16))
    for g in range(6):
        p.dma("pool", Wg.ap[:, :, g * 512:(g + 1) * 512],
              gla_w_in[:, g * 512:(g + 1) * 512].rearrange("(k p) n -> p k n", p=128), writes=[Wg.r])
    A1 = T(p.sb("A1", [128, 8, 32], BF16))
    p.dma("pool", A1.ap[:], gla_a1.rearrange("(k p) n -> p k n", p=128), writes=[A1.r])
    A2 = T(p.sb("A2", [33, 2, 512], F32))
    p.dma("sp", A2.ap[:], gla_a2.rearrange("d r n -> r d n"), writes=[A2.r])
    Wo = T(p.sb("Wo", [128, 8, D], BF16))
    p.dma("pool", Wo.ap[:], gla_w_out.rearrange("(k p) n -> p k n", p=128), writes=[Wo.r])
    NG = T(p.sb("NG", [128, D], F32))
    bload(NG.ap[:], gla_ng[0:1, :], D, writes=[NG.r])

    G1l, SH1l, GT1l = load_mod("a", 0, 0, 0, norm1_g[0:1, :])
    G1c, SH1c, GT1c = load_mod("b", 0, 1, 0, norm1_g[0:1, :])

    hT = T(p.sb("hT", [128, 8, 128], BF16))
    Us = T(p.sb("Us", [33, 128], F32))
    p.op("dve", lambda: V.memset(Us.ap[:], 0.0), writes=[Us.r])
    p.op("dve", lambda: V.memset(Us.ap[32:33, :], 1.0), writes=[Us.r])
    U = T(p.sb("U", [33, 128], F32R))
    p.op("dve", lambda: V.tensor_copy(out=U.ap[:, :], in_=Us.ap[:, :]), reads=[Us.r], writes=[U.r])
    A2r = T(p.sb("A2r", [33, 2, 512], F32R))
    p.op("dve", lambda: V.tensor_copy(out=A2r.ap[:, :, :], in_=A2.ap[:, :, :]), reads=[A2.r], writes=[A2r.r])
    e1 = T(p.sb("e1", [128, 512], F32))
    sp_ = T(p.sb("sp", [128, 512], F32R))
    Mr = T(p.sb("Mr", [128, 512], F32R))
    p.op("dve", lambda: V.tensor_copy(out=Mr.ap[:, :], in_=consts.ap[:, 128:640]), reads=[consts.r], writes=[Mr.r])
    ones_r = T(p.sb("ones_r", [128, 2], F32R))
    p.op("dve", lambda: V.tensor_copy(out=ones_r.ap[:, :], in_=ones.ap[:, 0:1].to_broadcast([128, 2])), reads=[ones.r], writes=[ones_r.r])
    Mr_incl = [Mr.ap[:, 0:128], Mr.ap[:, 256:384]]
    Mr_excl = [Mr.ap[:, 128:256], Mr.ap[:, 384:512]]
    q32 = T(p.sb("q32", [128, 512], F32))
    k32 = T(p.sb("k32", [128, 512], F32))
    qr = T(p.sb("qr", [128, 512], F32))
    kr = T(p.sb("kr", [128, 512], F32))
    tA = T(p.sb("tA", [128, 256], F32))
    tB = T(p.sb("tB", [128, 256], F32))
    tC = T(p.sb("tC", [128, 256], F32))
    tD = T(p.sb("tD", [128, 256], F32))
    rope_t = [T(p.sb("ropet%d" % i, [128, 128], F32)) for i in range(2)]
    Eq = T(p.sb("Eq", [128, 512], F32))
    Ek = T(p.sb("Ek", [128, 512], F32))
    Dh = T(p.sb("Dh", [128, 512], F32))
    dec = T(p.sb("dec", [128, 4], F32))
    qt = T(p.sb("qt", [128, 512], BF16))
    kt = T(p.sb("kt", [128, 512], BF16))
    kh = T(p.sb("kh", [128, 512], BF16))
    qT = T(p.sb("qT", [128, 4, 128], BF16))
    kT = T(p.sb("kT", [128, 4, 128], BF16))
    vsb = T(p.sb("vsb", [128, D], BF16))
    attm = [T(p.sb("attm%d" % i, [128, 128], BF16)) for i in range(2)]
    S32 = [T(p.sb("S32_%d" % h, [128, 256], F32)) for h in range(4)]
    Sbf = [T(p.sb("Sbf_%d" % h, [128, 256], BF16)) for h in range(4)]
    acc = [[T(p.sb("acc%d_%d" % (d, h), [128, 256], F32)) for h in range(4)] for d in range(2)]
    oft = T(p.sb("oft", [128, D], F32))
    osum = T(p.sb("osum", [128, D], F32))
    sr = T(p.sb("sr", [128, D], F32))
    og = T(p.sb("og", [128, D], BF16))
    ogT = T(p.sb("ogT", [128, 8, 128], BF16))
    ytmp = T(p.sb("ytmp", [128, D], F32))
    xo = T(p.sb("xo", [128, D], F32))

    def load_x(buf, src_ap, n, reads=()):
        p.dma("sp", buf.ap[:n, :], src_ap, writes=[buf.r], reads=reads)

    def gla_tile(xtile, n, d, full, rope_src, Gt, SHt, of_ap=None, of_r=None, GTt=None, dst_ap=None, dst_r=None, first=True, nxt=None):
        if first:
            norm_mod_T(xtile, n, Gt, SHt, hT.ap, hT.r)
        rt = None
        if rope_src is not None:
            rope_t.reverse()
            rt = rope_t[0]
            p.dma("sp", rt.ap[:n, :], rope_src, writes=[rt.r])

        def proj(pa_ap, pa_r, c0, nc_):
            for k in range(8):
                p.op("pe", lambda k=k: PE.matmul(pa_ap[:n, 0:nc_], lhsT=hT.ap[:, k, :n], rhs=Wg.ap[:, k, c0:c0 + nc_],
                                                 start=(k == 0), stop=(k == 7)),
                     reads=[hT.r, Wg.r], writes=[pa_r], inc=(k == 7))

        pu = next_pA()
        for k in range(8):
            p.op("pe", lambda k=k: PE.matmul(pu.ap[0:16, :n], lhsT=A1.ap[:, k, d * 16:(d + 1) * 16], rhs=hT.ap[:, k, :n],
                                             start=(k == 0), stop=(k == 7)), reads=[A1.r, hT.r], writes=[pu.r], inc=(k == 7))
        p.op("act", lambda: A.copy(out=U.ap[0:16, :n], in_=pu.ap[0:16, :n]), reads=[pu.r], writes=[U.r])
        pz = next_pA()
        p.op("pe", lambda: PE.matmul(pz.ap[:n, :], lhsT=U.ap[:, :n], rhs=A2r.ap[:, d, :], start=True, stop=True),
             reads=[U.r, A2r.r], writes=[pz.r])
        p.op("act", lambda: A.activation(out=e1.ap[:n, :], in_=pz.ap[:n, :], func=AF.Exp, scale=-1.0),
             reads=[pz.r], writes=[e1.r])
        p.op("act", lambda: A.activation(out=sp_.ap[:n, :], in_=e1.ap[:n, :], func=AF.Ln, bias=one_c.ap[:n, 0:1]),
             reads=[e1.r, one_c.r], writes=[sp_.r])
        if nxt is not None:
            norm_A1(nxt[0], nxt[1], Gt, SHt)
        pk = next_pA()
        proj(pk.ap, pk.r, 512, 512)
        p.op("act", lambda: A.copy(out=k32.ap[:n, :], in_=pk.ap[:n, :]), reads=[pk.r], writes=[k32.r])

        def rope(src, dst):
            sv = src.ap[:n, :].rearrange("p (h c f e) -> p h c f e", h=4, c=2, f=2)
            dv = dst.ap[:n, :].rearrange("p (h c f e) -> p h c f e", h=4, c=2, f=2)
            x1, x2 = sv[:, :, :, 0, :], sv[:, :, :, 1, :]
            cs = rt.ap[:n, 0:64].rearrange("p (c e) -> p c e", c=2).unsqueeze(1).to_broadcast([n, 4, 2, 32])
            sn = rt.ap[:n, 64:128].rearrange("p (c e) -> p c e", c=2).unsqueeze(1).to_broadcast([n, 4, 2, 32])
            v4 = lambda t: t.ap[:n, :].rearrange("p (h c e) -> p h c e", h=4, c=2)
            p.op("pool", lambda: G_.tensor_tensor(out=v4(tA), in0=x1, in1=cs, op=ALU.mult), reads=[src.r, rt.r], writes=[tA.r])
            p.op("pool", lambda: G_.tensor_tensor(out=v4(tB), in0=x2, in1=sn, op=ALU.mult), reads=[src.r, rt.r], writes=[tB.r])
            p.op("dve", lambda: V.tensor_tensor(out=v4(tC), in0=x1, in1=sn, op=ALU.mult), reads=[src.r, rt.r], writes=[tC.r])
            p.op("dve", lambda: V.tensor_tensor(out=v4(tD), in0=x2, in1=cs, op=ALU.mult), reads=[src.r, rt.r], writes=[tD.r])
            p.op("pool", lambda: G_.tensor_tensor(out=dv[:, :, :, 0, :], in0=v4(tA), in1=v4(tB), op=ALU.subtract),
                 reads=[tA.r, tB.r], writes=[dst.r])
            p.op("dve", lambda: V.tensor_tensor(out=dv[:, :, :, 1, :], in0=v4(tC), in1=v4(tD), op=ALU.add),
                 reads=[tC.r, tD.r], writes=[dst.r])

        if rt is not None:
            rope(k32, kr)
            krr = kr
        else:
            krr = k32
        pc = next_pA()
        p.op("pe", lambda: PE.matmul(pc.ap[:n, :], lhsT=Mr_excl[d][:n, :n], rhs=sp_.ap[:n, :], start=True, stop=True),
             reads=[Mr.r, sp_.r], writes=[pc.r])
        p.op("act", lambda: A.activation(out=Dh.ap[:n, :], in_=pc.ap[:n, :], func=AF.Exp, scale=-1.0 / 16),
             reads=[pc.r], writes=[Dh.r])
        p.op("dve", lambda: V.tensor_tensor(out=kh.ap[:n, :], in0=krr.ap[:n, :], in1=Dh.ap[:n, :], op=ALU.mult),
             reads=[krr.r, Dh.r], writes=[kh.r])
        pd = next_pA()
        for h in range(4):
            p.op("pe", lambda h=h: PE.matmul(pd.ap[:, 2 * h:2 * h + 2], lhsT=sp_.ap[:n, h * 128:(h + 1) * 128], rhs=ones_r.ap[:n, 0:2],
                                             start=True, stop=True), reads=[sp_.r, ones_r.r], writes=[pd.r], inc=(h == 3))
        p.op("act", lambda: A.activation(out=dec.ap[:, :], in_=pd.ap[:, 0:8].rearrange("p (h t) -> p h t", t=2)[:, :, 0], func=AF.Exp, scale=-1.0 / 16),
             reads=[pd.r], writes=[dec.r])
        if full:
            pc2 = next_pA()
            p.op("pe", lambda: PE.matmul(pc2.ap[:n, :], lhsT=Mr_incl[d][:n, :n], rhs=sp_.ap[:n, :], start=True, stop=True),
                 reads=[Mr.r, sp_.r], writes=[pc2.r])
            p.op("act", lambda: A.activation(out=Eq.ap[:n, :], in_=pc2.ap[:n, :], func=AF.Exp, scale=-1.0 / 16, bias=qsc.ap[:n, 0:1]),
                 reads=[pc2.r, qsc.r], writes=[Eq.r])
            p.op("act", lambda: A.activation(out=Ek.ap[:n, :], in_=pc2.ap[:n, :], func=AF.Exp, scale=1.0 / 16),
                 reads=[pc2.r], writes=[Ek.r])
            pq = next_pA()
            proj(pq.ap, pq.r, 0, 512)
            p.op("act", lambda: A.copy(out=q32.ap[:n, :], in_=pq.ap[:n, :]), reads=[pq.r], writes=[q32.r])
            if rt is not None:
                rope(q32, qr)
                qrr = qr
            else:
                qrr = q32
            p.op("dve", lambda: V.tensor_tensor(out=qt.ap[:n, :], in0=qrr.ap[:n, :], in1=Eq.ap[:n, :], op=ALU.mult),
                 reads=[qrr.r, Eq.r], writes=[qt.r])
            p.op("pool", lambda: G_.tensor_tensor(out=kt.ap[:n, :], in0=krr.ap[:n, :], in1=Ek.ap[:n, :], op=ALU.mult),
                 reads=[krr.r, Ek.r], writes=[kt.r])
            for g in range(2):
                proj(pV.ap[:, g * 512:(g + 1) * 512], pV.r, 1024 + g * 512, 512)
            p.op("act", lambda: A.copy(out=vsb.ap[:n, :], in_=pV.ap[:n, :]), reads=[pV.r], writes=[vsb.r])
            if full and d == 1:
                for g in range(2):
                    proj(pV.ap[:, g * 512:(g + 1) * 512], pV.r, 2048 + g * 512, 512)
                p.op("act", lambda: A.activation(out=sr.ap[:n, :], in_=pV.ap[:n, :], func=AF.Silu), reads=[pV.r], writes=[sr.r])
                p.op("pool", lambda: G_.tensor_tensor(out=sr.ap[:n, :], in0=sr.ap[:n, :], in1=NG.ap[:n, :], op=ALU.mult),
                     reads=[sr.r, NG.r], writes=[sr.r])
            for h in range(4):
                p.op("pe", lambda h=h: PE.transpose(out=pT.ap[:, h * 128:h * 128 + n], in_=qt.ap[:n, h * 128:(h + 1) * 128],
                                                    identity=ident.ap[:n, :n]), reads=[qt.r, ident.r], writes=[pT.r], inc=False)
            for h in range(4):
                p.op("pe", lambda h=h: PE.transpose(out=pT.ap[:, 512 + h * 128:512 + h * 128 + n], in_=kt.ap[:n, h * 128:(h + 1) * 128],
                                                    identity=ident.ap[:n, :n]), reads=[kt.r, ident.r], writes=[pT.r], inc=(h == 3))
            pTv = pT.ap[:, :].rearrange("p (a h t) -> p a h t", a=2, h=4)
            p.op("act", lambda: A.copy(out=qT.ap[:, :, :n], in_=pTv[:, 0, :, :n]), reads=[pT.r], writes=[qT.r])
            p.op("act", lambda: A.copy(out=kT.ap[:, :, :n], in_=pTv[:, 1, :, :n]), reads=[pT.r], writes=[kT.r])
        if not full:
            for g in range(2):
                proj(pV.ap[:, g * 512:(g + 1) * 512], pV.r, 1024 + g * 512, 512)
            p.op("act", lambda: A.copy(out=vsb.ap[:n, :], in_=pV.ap[:n, :]), reads=[pV.r], writes=[vsb.r])
            if full and d == 1:
                for g in range(2):
                    proj(pV.ap[:, g * 512:(g + 1) * 512], pV.r, 2048 + g * 512, 512)
                p.op("act", lambda: A.activation(out=sr.ap[:n, :], in_=pV.ap[:n, :], func=AF.Silu), reads=[pV.r], writes=[sr.r])
                p.op("pool", lambda: G_.tensor_tensor(out=sr.ap[:n, :], in0=sr.ap[:n, :], in1=NG.ap[:n, :], op=ALU.mult),
                     reads=[sr.r, NG.r], writes=[sr.r])
        for h in range(4):
            hs = slice(h * 256, (h + 1) * 256)
            if full:
                am = attm[h % 2]
                psa = pS.ap[:, 256 + (h % 2) * 128:256 + (h % 2) * 128 + 128]
                p.op("pe", lambda h=h, psa=psa: PE.matmul(psa[:n, :n], lhsT=kT.ap[:, h, :n], rhs=qT.ap[:, h, :n], start=True, stop=True),
                     reads=[kT.r, qT.r], writes=[pS.r])
                p.op("dve", lambda am=am, psa=psa: V.tensor_tensor(out=am.ap[:n, :n], in0=psa[:n, :n], in1=M_incl[d][:n, :n], op=ALU.mult),
                     reads=[pS.r, consts.r], writes=[am.r])
                p.op("pe", lambda h=h, hs=hs: PE.matmul(pO.ap[:n, hs], lhsT=qT.ap[:, h, :n], rhs=Sbf[h].ap[:, :], start=True, stop=False),
                     reads=[qT.r, Sbf[h].r], writes=[pO.r], inc=False)
                p.op("pe", lambda h=h, hs=hs, am=am: PE.matmul(pO.ap[:n, hs], lhsT=am.ap[:n, :n], rhs=vsb.ap[:n, hs], start=False, stop=True),
                     reads=[am.r, vsb.r], writes=[pO.r])
            sn_bufs = [pA[0], pA[1]] if full else [pA[0], pA[1], pS]
            snb = sn_bufs[h % len(sn_bufs)]
            p.op("pe", lambda h=h, hs=hs: PE.matmul(snb.ap[:, 0:256], lhsT=kh.ap[:n, h * 128:(h + 1) * 128], rhs=vsb.ap[:n, hs], start=True, stop=True),
                 reads=[kh.r, vsb.r], writes=[snb.r])
            p.op("dve", lambda h=h: V.scalar_tensor_tensor(out=S32[h].ap[:, :], in0=S32[h].ap[:, :], scalar=dec.ap[:, h:h + 1],
                                                           in1=snb.ap[:, 0:256], op0=ALU.mult, op1=ALU.add),
                 reads=[S32[h].r, dec.r, snb.r], writes=[S32[h].r])
            if h == 1 and nxt is not None:
                norm_A2(nxt[1], hT.ap, hT.r)
            if full:
                p.op("act", lambda h=h: A.copy(out=Sbf[h].ap[:, :], in_=S32[h].ap[:, :]), reads=[S32[h].r], writes=[Sbf[h].r])
        if not full:
            return
        if d == 0:
            p.op("act", lambda: A.copy(out=oft.ap[:n, :], in_=pO.ap[:n, :]), reads=[pO.r], writes=[oft.r])
            p.dma("sp", of_ap, oft.ap[:n, :], reads=[oft.r], writes=[of_r])
            return
        p.dma("sp", oft.ap[:n, :], of_ap, reads=[of_r], writes=[oft.r])
        p.op("dve", lambda: V.tensor_tensor(out=osum.ap[:n, :], in0=pO.ap[:n, :], in1=oft.ap[:n, :], op=ALU.add),
             reads=[pO.r, oft.r], writes=[osum.r])
        st_i[0] ^= 1
        s = st[st_i[0]]
        for h in range(4):
            p.op("act", lambda h=h: A.activation(out=junk.ap[:n, 0:256], in_=osum.ap[:n, h * 256:(h + 1) * 256], func=AF.Square,
                                                 accum_out=s.ap[:n, h:h + 1]), reads=[osum.r], writes=[junk.r, s.r])
        p.op("act", lambda: A.activation(out=s.ap[:n, 0:4], in_=s.ap[:n, 0:4], func=AF.Ln, scale=1.0 / 256, bias=eps_c.ap[:n, 0:1]),
             reads=[s.r, eps_c.r], writes=[s.r])
        p.op("act", lambda: A.activation(out=s.ap[:n, 4:8], in_=s.ap[:n, 0:4], func=AF.Exp, scale=-0.5),
             reads=[s.r], writes=[s.r])
        for h in range(4):
            hs = slice(h * 256, (h + 1) * 256)
            p.op("dve", lambda h=h, hs=hs: V.scalar_tensor_tensor(out=og.ap[:n, hs], in0=osum.ap[:n, hs], scalar=s.ap[:n, 4 + h:5 + h],
                                                                  in1=sr.ap[:n, hs], op0=ALU.mult, op1=ALU.mult),
                 reads=[osum.r, s.r, sr.r], writes=[og.r])
        for k in range(8):
            p.op("pe", lambda k=k: PE.transpose(out=pT.ap[:, k * 128:k * 128 + n], in_=og.ap[:n, k * 128:(k + 1) * 128],
                                                identity=ident.ap[:n, :n]), reads=[og.r, ident.r], writes=[pT.r], inc=(k == 7))
        p.op("act", lambda: A.copy(out=ogT.ap[:, :, :n], in_=pT.ap[:, :].rearrange("p (k t) -> p k t", k=8)[:, :, :n]),
             reads=[pT.r], writes=[ogT.r])
        for g in range(2):
            for k in range(8):
                p.op("pe", lambda k=k, g=g: PE.matmul(pO.ap[:n, g * 512:(g + 1) * 512], lhsT=ogT.ap[:, k, :n],
                                                      rhs=Wo.ap[:, k, g * 512:(g + 1) * 512], start=(k == 0), stop=(k == 7)),
                     reads=[ogT.r, Wo.r], writes=[pO.r], inc=(k == 7 and g == 1))
        p.op("dve", lambda: V.tensor_tensor(out=ytmp.ap[:n, :], in0=pO.ap[:n, :], in1=GTt.ap[:n, :], op=ALU.mult),
             reads=[pO.r, GTt.r], writes=[ytmp.r])
        p.op("pool", lambda: G_.tensor_tensor(out=xo.ap[:n, :], in0=ytmp.ap[:n, :], in1=xtile.ap[:n, :], op=ALU.add),
             reads=[ytmp.r, xtile.r], writes=[xo.r])
        p.dma("sp", dst_ap, xo.ap[:n, :], reads=[xo.r], writes=[dst_r])

    def zero_state():
        for h in range(4):
            p.op("dve", lambda h=h: V.memset(S32[h].ap[:], 0.0), writes=[S32[h].r])
            p.op("pool", lambda h=h: G_.memset(Sbf[h].ap[:], 0.0), writes=[Sbf[h].r])

    def take_snap(d, i):
        sc = sel.ap[:, 4 * d + i:4 * d + i + 1]
        for h in range(4):
            if i == (0 if d == 0 else 3):
                p.op("dve", lambda h=h: V.tensor_scalar(out=acc[d][h].ap[:], in0=S32[h].ap[:], scalar1=sc, scalar2=None, op0=ALU.mult),
                     reads=[S32[h].r, sel.r], writes=[acc[d][h].r])
            else:
                p.op("dve", lambda h=h: V.scalar_tensor_tensor(out=acc[d][h].ap[:], in0=S32[h].ap[:], scalar=sc, in1=acc[d][h].ap[:],
                                                                 op0=ALU.mult, op1=ALU.add),
                     reads=[S32[h].r, sel.r, acc[d][h].r], writes=[acc[d][h].r])

    def select_state(d):
        for h in range(4):
            p.op("dve", lambda h=h: V.tensor_copy(out=S32[h].ap[:], in_=acc[d][h].ap[:]), reads=[acc[d][h].r], writes=[S32[h].r])
            p.op("act", lambda h=h: A.copy(out=Sbf[h].ap[:], in_=acc[d][h].ap[:]), reads=[acc[d][h].r], writes=[Sbf[h].r])

    def rows_to_tiles(r0, r1, descending=False):
        tiles = []
        if not descending:
            r = r0
            while r < r1:
                nr = 2 if r + 2 <= r1 else 1
                tiles.append((r * 64, nr * 64))
                r += nr
        else:
            r = r1
            while r > r0:
                nr = 2 if r - 2 >= r0 else 1
                tiles.append(((r - nr) * 64, nr * 64))
                r -= nr
        return tiles

    xi = [0]

    def sweep(tiles, fn, rlist=None, pipelined=False):
        bufs = []
        rd = (lambda i: [rlist[i]]) if rlist is not None else (lambda i: [])
        for idx, (src, n, extra) in enumerate(tiles):
            if idx == 0:
                xi[0] ^= 1
                b = xt[xi[0]]
                load_x(b, src, n, rd(0))
                bufs.append(b)
            if idx + 1 < len(tiles):
                xi[0] ^= 1
                b2 = xt[xi[0]]
                load_x(b2, tiles[idx + 1][0], tiles[idx + 1][1], rd(idx + 1))
                bufs.append(b2)
            nx = (bufs[idx + 1], tiles[idx + 1][1]) if idx + 1 < len(tiles) else None
            if pipelined:
                fn(bufs[idx], n, extra, idx == 0, nx)
            else:
                fn(bufs[idx], n, extra)

    ctx_tiles = [(xctx[i * 128:(i + 1) * 128, :], 128, i) for i in range(2)]
    ext_tiles = [(xext[i * 128:min((i + 1) * 128, TEXT), :], min(128, TEXT - i * 128), i) for i in range(NT_E)]

    zero_state()
    sweep(ctx_tiles, lambda b, n, i, f, nx: gla_tile(b, n, 0, True, None, G1c, SH1c,
                                              of_ap=OF[TEXT + i * 128:TEXT + i * 128 + n, :], of_r=r_OF[NT_E + i], first=f, nxt=nx), pipelined=True)
    take_snap(0, 0)
    for si, (r0, r1) in enumerate([(0, 27), (27, 59), (59, 87)]):
        tl = [(xfull[t0:t0 + n, :], n, t0) for (t0, n) in rows_to_tiles(r0, r1)]
        sweep(tl, lambda b, n, t0, f, nx: gla_tile(b, n, 0, False, ropef[t0:t0 + n, :], G1l, SH1l, first=f, nxt=nx), pipelined=True)
        take_snap(0, si + 1)
    zero_state()
    sweep(ctx_tiles[::-1], lambda b, n, i, f, nx: gla_tile(b, n, 1, True, None, G1c, SH1c,
                                                    of_ap=OF[TEXT + i * 128:TEXT + i * 128 + n, :], of_r=r_OF[NT_E + i], GTt=GT1c,
                                                    dst_ap=Rd[TEXT + i * 128:TEXT + i * 128 + n, :], dst_r=r_Rd[NT_E + i], first=f, nxt=nx), pipelined=True)
    take_snap(1, 3)
    for si, (r0, r1) in zip([2, 1, 0], [(100, 128), (68, 100), (41, 68)]):
        tl = [(xfull[t0:t0 + n, :], n, t0) for (t0, n) in rows_to_tiles(r0, r1, descending=True)]
        sweep(tl, lambda b, n, t0, f, nx: gla_tile(b, n, 1, False, ropef[t0:t0 + n, :], G1l, SH1l, first=f, nxt=nx), pipelined=True)
        take_snap(1, si)
    select_state(0)
    sweep(ext_tiles, lambda b, n, i, f, nx: gla_tile(b, n, 0, True, ropee[i * 128:i * 128 + n, :], G1l, SH1l,
                                              of_ap=OF[i * 128:i * 128 + n, :], of_r=r_OF[i], first=f, nxt=nx), pipelined=True)
    select_state(1)
    sweep(ext_tiles[::-1], lambda b, n, i, f, nx: gla_tile(b, n, 1, True, ropee[i * 128:i * 128 + n, :], G1l, SH1l,
                                                    of_ap=OF[i * 128:i * 128 + n, :], of_r=r_OF[i], GTt=GT1l,
                                                    dst_ap=Rd[i * 128:i * 128 + n, :], dst_r=r_Rd[i], first=f, nxt=nx), pipelined=True)

    def split_groups(tb):
        u = tb // 64
        ng = -(-u // 7)
        base, rem = divmod(u, ng)
        out, t = [], 0
        for i in range(ng):
            sz = (base + (1 if i < rem else 0)) * 64
            out.append((t, sz))
            t += sz
        return out

    SLICES = [(0, 4), (4, 4), (8, 4), (12, 4), (16, 3), (19, 3)]

    def ffn_phase(l, src, r_src, dst, r_dst, with_ctx, final):
        mk = p.mark()
        tagp = "f%d" % l
        G2l, SH2l, GT2l = load_mod(tagp + "a", l, 0, 1, norm2_g[l:l + 1, :])
        if with_ctx:
            G2c, SH2c, GT2c = load_mod(tagp + "b", l, 1, 1, norm2_g[l:l + 1, :])
        CW = T(p.sb(tagp + "CW", [128, NFC * 4], F32))
        p.dma("sp", CW.ap[:], ffn_cw[l], writes=[CW.r])
        if final:
            FG = T(p.sb(tagp + "FG", [128, D], F32))
            bload(FG.ap[:], final_g[0:1, :], D, writes=[FG.r])
        Wa = [T(p.sb(tagp + "Wa%d" % i, [128, 8, 512], BF16)) for i in range(2)]
        Wv = [T(p.sb(tagp + "Wv%d" % i, [128, 8, 512], BF16)) for i in range(2)]
        Wo_ = [T(p.sb(tagp + "Wo%d" % i, [128, 4, D], BF16)) for i in range(2)]
        NTB = 11
        Rb = T(p.sb(tagp + "Rb", [128, NTB, D], F32))
        r_Rb = [Res() for _ in range(NTB)]
        TBMAX = NTB * 128
        H2T = T(p.sb(tagp + "H2T", [128, 8, TBMAX + 2], BF16))
        gT = T(p.sb(tagp + "gT", [128, 4, TBMAX], BF16))
        if with_ctx:
            Rc = T(p.sb(tagp + "Rc", [128, 2, D], F32))
            r_Rc = [Res() for _ in range(2)]
            H2Tc = T(p.sb(tagp + "H2Tc", [128, 8, TCTX + 2], BF16))
            gTc = T(p.sb(tagp + "gTc", [128, 4, TCTX], BF16))
        c1 = [T(p.sb(tagp + "c1_%d" % i, [128, 448], F32)) for i in range(2)]
        c2 = [T(p.sb(tagp + "c2_%d" % i, [128, 448], F32)) for i in range(2)]
        ytm = T(p.sb(tagp + "ytm", [128, D], F32))
        hb2 = T(p.sb(tagp + "hb2", [128, D], BF16))
        pAV = [(pA[0], pA[1]), (T(pV.ap[:, 0:512]), T(pV.ap[:, 512:1024]))]
        pT32 = T(pT.ap.bitcast(F32))
        pT32.r = pT.r
        ybufs = [(T(pO.ap[:, 0:512]), T(pO.ap[:, 512:1024])), (pS, pT32)]
        ycnt = [0]
        wi = [0]

        def load_slice(si):
            f0, nf = SLICES[si]
            b = wi[0] % 2
            wi[0] += 1
            p.dma("pool", Wa[b].ap[:, :, 0:nf * 128], ffn_w_in[l, :, f0 * 128:(f0 + nf) * 128].rearrange("(k p) n -> p k n", p=128),
                  writes=[Wa[b].r])
            p.dma("pool", Wv[b].ap[:, :, 0:nf * 128],
                  ffn_w_in[l, :, DFF + f0 * 128:DFF + (f0 + nf) * 128].rearrange("(k p) n -> p k n", p=128), writes=[Wv[b].r])
            p.dma("pool", Wo_[b].ap[:, 0:nf, :], ffn_w_out[l, f0 * 128:(f0 + nf) * 128, :].rearrange("(c p) n -> p c n", p=128),
                  writes=[Wo_[b].r])
            return b

        blocks = [(0, 11), (11, 21)]
        cnt = [0]
        for bi, (tl0, tl1) in enumerate(blocks):
            tok0 = tl0 * 128
            tb = min(tl1 * 128, TEXT) - tok0
            do_ctx = with_ctx and bi == 0
            nxt = load_slice(0)
            ents = []
            n_pre = 0
            if tl0 == 0:
                p.op("pool", lambda: G_.memset(H2T.ap[:, :, 0:1], 0.0), writes=[H2T.r])
            else:
                xi[0] ^= 1
                hb_ = xt[xi[0]]
                p.dma("sp", hb_.ap[:, :], src[(tl0 - 1) * 128:tl0 * 128, :], reads=[r_src[tl0 - 1]], writes=[hb_.r])
                ents.append((hb_, 128, G2l, SH2l, H2T, 0, (127, 1)))
                n_pre = 1
            for ti in range(tl0, tl1):
                n = min(128, TEXT - ti * 128)
                sl = ti - tl0
                p.dma("sp", Rb.ap[:n, sl, :], src[ti * 128:ti * 128 + n, :], reads=[r_src[ti]], writes=[r_Rb[sl]])
                xtile = T(Rb.ap[:, sl, :])
                xtile.r = r_Rb[sl]
                ents.append((xtile, n, G2l, SH2l, H2T, 1 + sl * 128, None))
            n_tiles_e = len(ents)
            if tl1 == NT_E:
                p.op("pool", lambda: G_.memset(H2T.ap[:, :, 1 + tb:2 + tb], 0.0), writes=[H2T.r])
            else:
                xi[0] ^= 1
                hb_ = xt[xi[0]]
                nn = min(128, TEXT - tl1 * 128)
                p.dma("sp", hb_.ap[:nn, :], src[tl1 * 128:tl1 * 128 + nn, :], reads=[r_src[tl1]], writes=[hb_.r])
                ents.append((hb_, nn, G2l, SH2l, H2T, 1 + tb, (0, 1)))
            n_lat_e = len(ents)
            if do_ctx:
                for ci in range(2):
                    p.dma("sp", Rc.ap[:, ci, :], src[TEXT + ci * 128:TEXT + (ci + 1) * 128, :], reads=[r_src[NT_E + ci]], writes=[r_Rc[ci]])
                    xtile = T(Rc.ap[:, ci, :])
                    xtile.r = r_Rc[ci]
                    ents.append((xtile, 128, G2c, SH2c, H2Tc, 1 + ci * 128, None))
                p.op("pool", lambda: G_.memset(H2Tc.ap[:, :, 0:1], 0.0), writes=[H2Tc.r])
                p.op("pool", lambda: G_.memset(H2Tc.ap[:, :, TCTX + 1:TCTX + 2], 0.0), writes=[H2Tc.r])
            hbs = [hb, hb2]
            e0 = ents[0]
            norm_A1(e0[0], e0[1], e0[2], e0[3], hbuf=hbs[0])
            eptr = [0]
            h2t_res = [Res() for _ in ents]

            def emit_upto(kmax):
                while eptr[0] < kmax:
                    ei = eptr[0]
                    e = ents[ei]
                    if ei + 1 < len(ents):
                        e1 = ents[ei + 1]
                        norm_A1(e1[0], e1[1], e1[2], e1[3], hbuf=hbs[(ei + 1) % 2])
                    norm_A2(e[1], e[4].ap, h2t_res[ei], col0=e[5], sub=e[6], hbuf=hbs[ei % 2])
                    eptr[0] += 1

            lat_groups = split_groups(tb)
            groups = []
            for gi, (g0, ng) in enumerate(lat_groups):
                if gi == len(lat_groups) - 1:
                    need = n_lat_e
                else:
                    need = n_pre + min((g0 + ng) // 128, tl1 - tl0 - 1) + 1
                groups.append((H2T, gT, g0, ng, need))
            if do_ctx:
                groups.append((H2Tc, gTc, 0, TCTX, len(ents)))
            def item(b, fl, fc, Hs, gs, g0, ng, need):
                hres = [Hs.r] + h2t_res[:need]
                cwv = CW.ap[:, fc * 4:fc * 4 + 4]
                pa_, pv_ = pAV[cnt[0] % 2]
                cc1 = c1[cnt[0] % 2]
                cc2 = c2[cnt[0] % 2]
                cnt[0] += 1
                for k in range(8):
                    p.op("pe", lambda k=k, pa_=pa_, Hs=Hs, g0=g0, ng=ng, b=b, fl=fl: PE.matmul(
                        pa_.ap[:, 0:ng + 2], lhsT=Wa[b].ap[:, k, fl * 128:(fl + 1) * 128], rhs=Hs.ap[:, k, g0:g0 + ng + 2],
                        start=(k == 0), stop=(k == 7)), reads=[Wa[b].r] + hres, writes=[pa_.r], inc=(k == 7))
                for k in range(8):
                    p.op("pe", lambda k=k, pv_=pv_, Hs=Hs, g0=g0, ng=ng, b=b, fl=fl: PE.matmul(
                        pv_.ap[:, 0:ng], lhsT=Wv[b].ap[:, k, fl * 128:(fl + 1) * 128], rhs=Hs.ap[:, k, g0 + 1:g0 + 1 + ng],
                        start=(k == 0), stop=(k == 7)), reads=[Wv[b].r] + hres, writes=[pv_.r], inc=(k == 7))
                p.op("act", lambda pa_=pa_, cc1=cc1, ng=ng, cwv=cwv: A.activation(
                    out=cc1.ap[:, 0:ng], in_=pa_.ap[:, 1:ng + 1], func=AF.Identity, scale=cwv[:, 1:2], bias=cwv[:, 3:4]),
                    reads=[pa_.r, CW.r], writes=[cc1.r])
                p.op("dve", lambda pa_=pa_, cc1=cc1, cc2=cc2, ng=ng, cwv=cwv: V.scalar_tensor_tensor(
                    out=cc2.ap[:, 0:ng], in0=pa_.ap[:, 0:ng], scalar=cwv[:, 0:1], in1=cc1.ap[:, 0:ng], op0=ALU.mult, op1=ALU.add),
                    reads=[pa_.r, CW.r, cc1.r], writes=[cc2.r])
                p.op("dve", lambda pa_=pa_, cc1=cc1, cc2=cc2, ng=ng, cwv=cwv: V.scalar_tensor_tensor(
                    out=cc1.ap[:, 0:ng], in0=pa_.ap[:, 2:ng + 2], scalar=cwv[:, 2:3], in1=cc2.ap[:, 0:ng], op0=ALU.mult, op1=ALU.add),
                    reads=[pa_.r, CW.r, cc2.r], writes=[cc1.r])
                p.op("act", lambda cc1=cc1, cc2=cc2, ng=ng: A.activation(out=cc2.ap[:, 0:ng], in_=cc1.ap[:, 0:ng], func=AF.Silu),
                     reads=[cc1.r], writes=[cc2.r])
                p.op("dve", lambda pv_=pv_, cc2=cc2, gs=gs, g0=g0, ng=ng, fl=fl: V.tensor_tensor(
                    out=gs.ap[:, fl, g0:g0 + ng], in0=cc2.ap[:, 0:ng], in1=pv_.ap[:, 0:ng], op=ALU.mult),
                    reads=[cc2.r, pv_.r], writes=[gs.r])

            for si, (f0, nf) in enumerate(SLICES):
                b = nxt
                if si + 1 < len(SLICES):
                    nxt = load_slice(si + 1)
                if si == 0:
                    for (Hs, gs, g0, ng, need) in groups:
                        emit_upto(need)
                        for fl in range(nf):
                            item(b, fl, f0 + fl, Hs, gs, g0, ng, need)
                else:
                    for fl in range(nf):
                        for (Hs, gs, g0, ng, need) in groups:
                            item(b, fl, f0 + fl, Hs, gs, g0, ng, need)
                tiles = [(gT, Rb, r_Rb[ti - tl0], ti - tl0, (ti - tl0) * 128, min(128, TEXT - ti * 128), GT2l) for ti in range(tl0, tl1)]
                if do_ctx:
                    tiles += [(gTc, Rc, r_Rc[ci], ci, ci * 128, 128, GT2c) for ci in range(2)]
                for (gs, Rt, rr, sl, c0, n, GTt) in tiles:
                    yb = ybufs[ycnt[0] % 2]
                    ycnt[0] += 1
                    for g in range(2):
                        yt = yb[g]
                        for fl in range(nf):
                            p.op("pe", lambda: PE.matmul(
                                yt.ap[:n, 0:512], lhsT=gs.ap[:, fl, c0:c0 + n], rhs=Wo_[b].ap[:, fl, g * 512:(g + 1) * 512],
                                start=(fl == 0), stop=(fl == nf - 1)), reads=[gs.r, Wo_[b].r], writes=[yt.r], inc=(fl == nf - 1))
                    for g in range(2):
                        yt = yb[g]
                        p.op("dve", lambda: V.tensor_tensor(out=ytm.ap[:n, g * 512:(g + 1) * 512], in0=yt.ap[:n, 0:512],
                                                            in1=GTt.ap[:n, g * 512:(g + 1) * 512], op=ALU.mult),
                             reads=[yt.r, GTt.r], writes=[ytm.r])
                    p.op("dve", lambda Rt=Rt, sl=sl, n=n: V.tensor_tensor(out=Rt.ap[:n, sl, :], in0=Rt.ap[:n, sl, :], in1=ytm.ap[:n, :], op=ALU.add),
                         reads=[ytm.r, rr], writes=[rr])
            for ti in range(tl0, tl1):
                n = min(128, TEXT - ti * 128)
                sl = ti - tl0
                if not final:
                    p.dma("sp", dst[ti * 128:ti * 128 + n, :], Rb.ap[:n, sl, :], reads=[r_Rb[sl]], writes=[r_dst[ti]])
                else:
                    st_i[0] ^= 1
                    s_ = st[st_i[0]]
                    p.op("act", lambda s_=s_, n=n, sl=sl: A.activation(out=junk.ap[:n, :], in_=Rb.ap[:n, sl, :], func=AF.Square, accum_out=s_.ap[:n, 0:1]),
                         reads=[r_Rb[sl]], writes=[junk.r, s_.r])
                    p.op("act", lambda s_=s_, n=n: A.activation(out=s_.ap[:n, 1:2], in_=s_.ap[:n, 0:1], func=AF.Ln, scale=1.0 / D, bias=eps_c.ap[:n, 0:1]),
                         reads=[s_.r, eps_c.r], writes=[s_.r])
                    p.op("act", lambda s_=s_, n=n: A.activation(out=s_.ap[:n, 2:3], in_=s_.ap[:n, 1:2], func=AF.Exp, scale=-0.5),
                         reads=[s_.r], writes=[s_.r])
                    p.op("dve", lambda s_=s_, n=n, sl=sl: V.scalar_tensor_tensor(out=ytm.ap[:n, :], in0=Rb.ap[:n, sl, :], scalar=s_.ap[:n, 2:3],
                                                                           in1=FG.ap[:n, :], op0=ALU.mult, op1=ALU.mult),
                         reads=[r_Rb[sl], s_.r, FG.r], writes=[ytm.r])
                    p.dma("sp", out_d[ti * 128:ti * 128 + n, :], ytm.ap[:n, :], reads=[ytm.r], is_output=True)
            if do_ctx:
                for ci in range(2):
                    p.dma("sp", dst[TEXT + ci * 128:TEXT + (ci + 1) * 128, :], Rc.ap[:, ci, :], reads=[r_Rc[ci]], writes=[r_dst[NT_E + ci]])
        p.free_to(mk)

    def na_phase(src, r_src, dst, r_dst):
        l = 1
        for hh in range(2):
            mk = p.mark()
            tg = "n%d" % hh
            G1, SH1, GT1 = load_mod(tg + "a", l, 0, 0, norm1_g[l:l + 1, :])
            G1c_, SH1c_, _ = load_mod(tg + "b", l, 1, 0, norm1_g[l:l + 1, :])
            Wq = T(p.sb(tg + "Wq", [128, 8, 512], BF16))
            Wk = T(p.sb(tg + "Wk", [128, 8, 512], BF16))
            Wv2 = T(p.sb(tg + "Wv", [128, 8, 512], BF16))
            for (Wt, c0) in ((Wq, hh * 512), (Wk, D + hh * 512), (Wv2, 2 * D + hh * 512)):
                p.dma("pool", Wt.ap[:], na_w_qkv[:, c0:c0 + 512].rearrange("(k p) n -> p k n", p=128), writes=[Wt.r])
            WoN = T(p.sb(tg + "WoN", [128, 4, D], BF16))
            p.dma("pool", WoN.ap[:], na_w_out[hh * 512:(hh + 1) * 512, :].rearrange("(c p) n -> p c n", p=128), writes=[WoN.r])
            Tt = T(p.sb(tg + "Tt", [128, 4 * 960], BF16))
            naT4 = na_T.rearrange("c (pr hp x) -> c pr hp x", hp=2, x=960)
            for hp_ in range(2):
                p.dma("pool", Tt.ap[hp_ * 64:(hp_ + 1) * 64, :].rearrange("c (pr x) -> c pr x", x=960),
                      naT4[:, hh * 4:(hh + 1) * 4, hp_, :], writes=[Tt.r])
            QT = T(p.sb(tg + "QT", [128, 4, TEXT], BF16))
            KT = T(p.sb(tg + "KT", [128, 4, TEXT], BF16))
            Vt = T(p.sb(tg + "Vt", [128, NT_E, 512], BF16))
            KcT = T(p.sb(tg + "KcT", [128, 4, TCTX], BF16))
            Vc = T(p.sb(tg + "Vc", [128, 2, 512], BF16))
            hT1 = T(p.sb(tg + "hT1", [128, 8, 128], BF16))
            Pb = [T(p.sb(tg + "Pb%d" % i, [128, 896], BF16)) for i in range(2)]
            PTs = [T(p.sb(tg + "PTs%d" % i, [128, 896], BF16)) for i in range(2)]
            BD = [T(p.sb(tg + "BD%d" % i, [128, 128], BF16)) for i in range(2)]
            OT = T(p.sb(tg + "OT", [128, 4, 64], BF16))
            sm = [T(p.sb(tg + "sm%d" % i, [128, 8], F32)) for i in range(2)]
            xrow = [T(p.sb(tg + "xrow%d" % i, [64, D], F32)) for i in range(2)]
            yrow = T(p.sb(tg + "yrow", [64, D], F32))
            for i in range(2):
                p.op("pool", lambda i=i: G_.memset(Pb[i].ap[:], 0.0), writes=[Pb[i].r])
                p.op("pool", lambda i=i: G_.memset(BD[i].ap[:], 0.0), writes=[BD[i].r])
            pT2 = [T(pT.ap[:, 0:896]), T(pA[1].ap.bitcast(BF16)[:, 0:896])]
            pT2[0].r = pT.r
            pT2[1].r = pA[1].r
            Sps = [pV, pO]

            def kv_proj(n, tok0, QT_, KT_, Vdst_ap, Vdst_r, kcols):
                if QT_ is not None:
                    pq = pA[0]
                    for pr in range(4):
                        for k in range(8):
                            p.op("pe", lambda pr=pr, k=k: PE.matmul(pq.ap[:, pr * 128:pr * 128 + n], lhsT=Wq.ap[:, k, pr * 128:(pr + 1) * 128],
                                                                    rhs=hT1.ap[:, k, :n], start=(k == 0), stop=(k == 7)),
                                 reads=[Wq.r, hT1.r], writes=[pq.r], inc=(k == 7 and pr == 3))
                    p.op("act", lambda: A.mul(out=QT_.ap[:, :, tok0:tok0 + n],
                                              in_=pq.ap[:, :].rearrange("p (a t) -> p a t", a=4)[:, :, :n], mul=0.125),
                         reads=[pq.r], writes=[QT_.r])
                pk = pA[1]
                for pr in range(4):
                    for k in range(8):
                        p.op("pe", lambda pr=pr, k=k: PE.matmul(pk.ap[:, pr * 128:pr * 128 + n], lhsT=Wk.ap[:, k, pr * 128:(pr + 1) * 128],
                                                                rhs=hT1.ap[:, k, :n], start=(k == 0), stop=(k == 7)),
                             reads=[Wk.r, hT1.r], writes=[pk.r], inc=(k == 7 and pr == 3))
                p.op("act", lambda: A.copy(out=KT_.ap[:, :, kcols:kcols + n],
                                           in_=pk.ap[:, :].rearrange("p (a t) -> p a t", a=4)[:, :, :n]),
                     reads=[pk.r], writes=[KT_.r])
                for k in range(8):
                    p.op("pe", lambda k=k: PE.matmul(pS.ap[:n, :], lhsT=hT1.ap[:, k, :n], rhs=Wv2.ap[:, k, :], start=(k == 0), stop=(k == 7)),
                         reads=[hT1.r, Wv2.r], writes=[pS.r], inc=(k == 7))
                p.op("dve", lambda: V.tensor_copy(out=Vdst_ap, in_=pS.ap[:n, :]), reads=[pS.r], writes=[Vdst_r])

            def ctx_fn(b, n, ci, first, nx):
                if first:
                    norm_mod_T(b, n, G1c_, SH1c_, hT1.ap, hT1.r)
                if nx is not None:
                    norm_A1(nx[0], nx[1], G1c_, SH1c_)
                kv_proj(n, 0, None, KcT, Vc.ap[:n, ci, :], Vc.r, ci * 128)
                if nx is not None:
                    norm_A2(nx[1], hT1.ap, hT1.r)
            sweep([(src[TEXT + ci * 128:TEXT + (ci + 1) * 128, :], 128, ci) for ci in range(2)], ctx_fn,
                  rlist=[r_src[NT_E + ci] for ci in range(2)], pipelined=True)

            def ext_fn(b, n, ti, first, nx):
                if first:
                    norm_mod_T(b, n, G1, SH1, hT1.ap, hT1.r)
                if nx is not None:
                    norm_A1(nx[0], nx[1], G1, SH1)
                kv_proj(n, ti * 128, QT, KT, Vt.ap[:n, ti, :], Vt.r, ti * 128)
                if nx is not None:
                    norm_A2(nx[1], hT1.ap, hT1.r)
            sweep([(src[ti * 128:ti * 128 + min(128, TEXT - ti * 128), :], min(128, TEXT - ti * 128), ti) for ti in range(NT_E)], ext_fn,
                  rlist=[r_src[ti] for ti in range(NT_E)], pipelined=True)

            items = [(r, pr) for r in range(NROW) for pr in range(4)]

            def row_info(r):
                rs = min(max(r - 4, 0), NROW - 8)
                return rs, rs - r + 7

            def stage1(i):
                r, pr = items[i]
                rs, dr0 = row_info(r)
                S_, pb, smt, bd = Sps[i % 2], Pb[i % 2], sm[i % 2], BD[i % 2]
                if pr == 0:
                    xr = xrow[r % 2]
                    row_src = src if hh == 0 else dst
                    row_res = r_src[r // 2] if hh == 0 else r_dst[r // 2]
                    p.dma("sp", xr.ap[:, :], row_src[r * 64:(r + 1) * 64, :], reads=[row_res], writes=[xr.r])
                for hp in range(2):
                    p.op("pool", lambda: G_.tensor_copy(out=bd.ap[hp * 64:(hp + 1) * 64, hp * 64:(hp + 1) * 64],
                                                        in_=QT.ap[hp * 64:(hp + 1) * 64, pr, r * 64:(r + 1) * 64]),
                         reads=[QT.r], writes=[bd.r])
                p.op("pe", lambda: PE.matmul(S_.ap[:, 0:512], lhsT=bd.ap[:, :], rhs=KT.ap[:, pr, rs * 64:rs * 64 + 512],
                                             start=True, stop=False), reads=[bd.r, KT.r], writes=[S_.r], inc=False)
                tb0 = (pr * 15 + dr0) * 64
                p.op("pe", lambda: PE.matmul(S_.ap[:, 0:512], lhsT=ident.ap[:, :], rhs=Tt.ap[:, tb0:tb0 + 512],
                                             start=False, stop=True), reads=[ident.r, Tt.r], writes=[S_.r], inc=False)
                p.op("pe", lambda: PE.matmul(S_.ap[:, 512:768], lhsT=bd.ap[:, :], rhs=KcT.ap[:, pr, :],
                                             start=True, stop=True), reads=[bd.r, KcT.r], writes=[S_.r])
                p.op("dve", lambda: V.reduce_max(out=smt.ap[:, 0:1], in_=S_.ap[:, 0:768], axis=AX.X), reads=[S_.r], writes=[smt.r])
                p.op("dve", lambda: V.tensor_scalar(out=smt.ap[:, 1:2], in0=smt.ap[:, 0:1], scalar1=-1.0, scalar2=None, op0=ALU.mult),
                     reads=[smt.r], writes=[smt.r])
                p.op("act", lambda: A.activation(out=pb.ap[:, 64:576], in_=S_.ap[:, 0:512], func=AF.Exp, bias=smt.ap[:, 1:2],
                                                 accum_out=smt.ap[:, 2:3]), reads=[S_.r, smt.r], writes=[pb.r, smt.r])
                p.op("act", lambda: A.activation(out=pb.ap[:, 640:896], in_=S_.ap[:, 512:768], func=AF.Exp, bias=smt.ap[:, 1:2],
                                                 accum_out=smt.ap[:, 3:4]), reads=[S_.r, smt.r], writes=[pb.r, smt.r])

            ktab = {}
            pend = []

            def stage2pre(i):
                pb, smt = Pb[i % 2], sm[i % 2]
                p.op("dve", lambda: V.tensor_tensor(out=smt.ap[:, 4:5], in0=smt.ap[:, 2:3], in1=smt.ap[:, 3:4], op=ALU.add),
                     reads=[smt.r], writes=[smt.r])
                p.op("dve", lambda: V.reciprocal(out=smt.ap[:, 5:6], in_=smt.ap[:, 4:5]), reads=[smt.r], writes=[smt.r])
                p.op("dve", lambda: V.tensor_scalar(out=pb.ap[:, 64:896], in0=pb.ap[:, 64:896], scalar1=smt.ap[:, 5:6], scalar2=None, op0=ALU.mult),
                     reads=[pb.r, smt.r], writes=[pb.r])

            def stage2a(i):
                r, pr = items[i]
                rs, dr0 = row_info(r)
                pb, smt, pts, ptp = Pb[i % 2], sm[i % 2], PTs[i % 2], pT2[i % 2]
                kts = []
                if rs % 2 == 0:
                    for t in range(4):
                        kts.append((64 + 128 * t, 128, ("v", rs // 2 + t)))
                else:
                    for t in range(5):
                        vt = (rs - 1) // 2 + t
                        kk = 128 if vt * 128 + 128 <= TEXT else 64
                        kts.append((128 * t, kk, ("v", vt)))
                kts.append((640, 128, ("c", 0)))
                kts.append((768, 128, ("c", 1)))
                nt = len(kts)
                for t, (c0, kk, _) in enumerate(kts):
                    p.op("pe", lambda: PE.transpose(out=ptp.ap[:kk, t * 128:(t + 1) * 128], in_=pb.ap[:, c0:c0 + kk],
                                                    identity=ident.ap[:, :]),
                         reads=[pb.r, ident.r], writes=[ptp.r], inc=(t == nt - 1))
                p.op("dve", lambda: V.tensor_copy(out=pts.ap[:, 0:nt * 128], in_=ptp.ap[:, 0:nt * 128]), reads=[ptp.r], writes=[pts.r])
                ktab[i] = kts

            def stage2b(i):
                r, pr = items[i]
                pts = PTs[i % 2]
                kts = ktab.pop(i)
                nt = len(kts)
                if pend:
                    out_proj_row(pend.pop())
                po = pS.ap[:, pr * 128:(pr + 1) * 128]
                for t, (c0, kk, (kind, vi)) in enumerate(kts):
                    vl = Vt.ap[:kk, vi, pr * 128:(pr + 1) * 128] if kind == "v" else Vc.ap[:kk, vi, pr * 128:(pr + 1) * 128]
                    p.op("pe", lambda: PE.matmul(po, lhsT=vl, rhs=pts.ap[:kk, t * 128:(t + 1) * 128],
                                                 start=(t == 0), stop=(t == nt - 1)),
                         reads=[pts.r, Vt.r, Vc.r], writes=[pS.r], inc=(t == nt - 1))
                p.op("act", lambda: A.copy(out=OT.ap[0:64, pr, :], in_=pS.ap[0:64, pr * 128:pr * 128 + 64]), reads=[pS.r], writes=[OT.r])
                p.op("act", lambda: A.copy(out=OT.ap[64:128, pr, :], in_=pS.ap[64:128, pr * 128 + 64:pr * 128 + 128]), reads=[pS.r], writes=[OT.r])
                if pr == 3:
                    pend.append(r)

            def out_proj_row(r):
                xr = xrow[r % 2]
                for g in range(2):
                    for kk in range(4):
                        p.op("pe", lambda: PE.matmul(pA[g].ap[0:64, :], lhsT=OT.ap[:, kk, :], rhs=WoN.ap[:, kk, g * 512:(g + 1) * 512],
                                                     start=(kk == 0), stop=(kk == 3)), reads=[OT.r, WoN.r], writes=[pA[g].r], inc=(kk == 3))
                    p.op("dve", lambda: V.tensor_tensor(out=yrow.ap[:, g * 512:(g + 1) * 512], in0=pA[g].ap[0:64, :],
                                                        in1=GT1.ap[0:64, g * 512:(g + 1) * 512], op=ALU.mult),
                         reads=[pA[g].r, GT1.r], writes=[yrow.r])
                p.op("pool", lambda: G_.tensor_tensor(out=xr.ap[:, :], in0=xr.ap[:, :], in1=yrow.ap[:, :], op=ALU.add),
                     reads=[xr.r, yrow.r], writes=[xr.r])
                p.dma("sp", dst[r * 64:(r + 1) * 64, :], xr.ap[:, :], reads=[xr.r], writes=[r_dst[r // 2]])

            stage1(0)
            for i in range(len(items)):
                stage2pre(i)
                if i + 1 < len(items):
                    stage1(i + 1)
                if i >= 1:
                    stage2b(i - 1)
                stage2a(i)
            stage2b(len(items) - 1)
            out_proj_row(pend.pop())
            p.free_to(mk)

    def dump(src, r_src):
        for i in range(NT_E):
            n = min(128, TEXT - i * 128)
            b = xt[i % 2]
            p.dma("sp", b.ap[:n, :], src[i * 128:i * 128 + n, :], reads=[r_src[i]], writes=[b.r])
            p.dma("sp", out_d[i * 128:i * 128 + n, :], b.ap[:n, :], reads=[b.r], is_output=True)
        print("instructions", p.n_inst, "waits", p.n_wait)
        return p.finish()

    if stop == "gla":
        return dump(Rd, r_Rd)
    p.free_to(mkG)
    ffn_phase(0, Rd, r_Rd, Rd2, r_Rd2, True, False)
    if stop == "ffn0":
        return dump(Rd2, r_Rd2)
    if stop == "ffn0c":
        for i in range(2):
            b = xt[i % 2]
            p.dma("sp", b.ap[:, :], Rd2[TEXT + i * 128:TEXT + (i + 1) * 128, :], reads=[r_Rd2[NT_E + i]], writes=[b.r])
            p.dma("sp", out_d[i * 128:(i + 1) * 128, :], b.ap[:, :], reads=[b.r], is_output=True)
        return p.finish()

    na_phase(Rd2, r_Rd2, Rd, r_Rd)
    if stop == "na":
        return dump(Rd, r_Rd)
    ffn_phase(1, Rd, r_Rd, None, None, False, True)
    print("instructions", p.n_inst, "waits", p.n_wait)
    return p.finish()


def host_inputs(inputs):
    f = np.float32
    x = np.asarray(inputs["x"], f)
    ctx = np.asarray(inputs["ctx"], f)
    c = np.asarray(inputs["c"], f)
    c_ctx = np.asarray(inputs["c_ctx"], f)
    consts = np.zeros((128, 640), f)
    jj, ii = np.meshgrid(np.arange(128), np.arange(128), indexing="ij")
    consts[:, 0:128] = (jj == ii)
    consts[:, 128:256] = (jj <= ii)
    consts[:, 256:384] = (jj > ii)
    consts[:, 384:512] = (jj >= ii)
    consts[:, 512:640] = (jj < ii)
    half = 64
    inv = (1.0 / (10000.0 ** (np.arange(0, half, 2, dtype=f) / f(half)))).astype(f)
    pos = np.arange(8192)
    rows = (pos // 64).astype(f)
    cols = (pos % 64).astype(f)
    ar = (rows[:, None] * inv[None, :]).astype(f)
    ac = (cols[:, None] * inv[None, :]).astype(f)
    ropef = np.concatenate([np.cos(ar), np.cos(ac), np.sin(ar), np.sin(ac)], axis=1).astype(f)
    cw = np.zeros((2, 128, NFC * 4), f)
    for l in range(2):
        w4 = np.concatenate([np.asarray(inputs["ffn_conv_w"][l], f), np.asarray(inputs["ffn_conv_b"][l], f)[None]], axis=0)
        cw[l] = w4.reshape(4, NFC, 128).transpose(2, 1, 0).reshape(128, NFC * 4)
    a1 = np.concatenate([np.asarray(inputs["gla_a_w1"][0, 0], f), np.asarray(inputs["gla_a_w1"][0, 1], f)], axis=1)
    a2 = np.zeros((2, 33, 512), f)
    for d in range(2):
        a2[d, 0:16] = inputs["gla_a_w2"][0, d]
        a2[d, 32] = inputs["gla_a_b"][0, d]
    ng = np.tile(np.asarray(inputs["gla_norm_g"][0], f), 4)[None, :]
    rpb = np.asarray(inputs["na_rpb"][0], f)
    cpos = np.arange(64)
    cstart = np.clip(cpos - 8, 0, 48)
    kc = np.arange(64)
    inwin = (kc[None, :] >= cstart[:, None]) & (kc[None, :] < cstart[:, None] + 16)
    dc = np.clip(kc[None, :] - cpos[:, None] + 15, 0, 30)
    Tt = rpb[:, :, dc]
    Tt = np.where(inwin[None, None], Tt, f(-30000.0)).astype(f)
    Tt = np.ascontiguousarray(Tt.transpose(2, 0, 1, 3)).reshape(64, 16 * 15 * 64)
    shared = dict(
        consts=consts, ropef=ropef,
        ada_w=np.asarray(inputs["ada_w"], f), ada_b=np.asarray(inputs["ada_b"], f),
        norm1_g=np.asarray(inputs["norm1_g"], f), norm2_g=np.asarray(inputs["norm2_g"], f),
        ffn_w_in=np.asarray(inputs["ffn_w_in"], f), ffn_cw=cw, ffn_w_out=np.asarray(inputs["ffn_w_out"], f),
        gla_w_in=np.asarray(inputs["gla_w_in"][0], f), gla_a1=a1, gla_a2=a2, gla_ng=ng,
        gla_w_out=np.asarray(inputs["gla_w_out"][0], f),
        na_w_qkv=np.asarray(inputs["na_w_qkv"][0], f), na_T=Tt, na_w_out=np.asarray(inputs["na_w_out"][0], f),
        final_g=np.asarray(inputs["final_g"], f)[None, :],
    )
    maps = []
    for core in range(8):
        b, j = core // 4, core % 4
        t0 = ES[j] * 64
        cT = np.zeros((128, 16), f)
        cT[:, 0::2] = c[b].reshape(8, 128).T
        cT[:, 1::2] = c_ctx.reshape(8, 128).T
        sel = np.zeros((128, 8), f)
        sel[:, j] = 1.0
        sel[:, 4 + j] = 1.0
        m = dict(shared)
        m.update(xfull=x[b], xext=np.ascontiguousarray(x[b, t0:t0 + TEXT]), xctx=ctx[b], cT=cT, sel=sel,
                 ropee=np.ascontiguousarray(ropef[t0:t0 + TEXT]))
        maps.append(m)
    return maps


_NC_CACHE = {}


def run(inputs, stop="all"):
    if stop not in _NC_CACHE:
        _NC_CACHE[stop] = build(stop)
    nc = _NC_CACHE[stop]
    maps = host_inputs(inputs)
    res = run_bass_kernel_spmd(nc, maps, core_ids=list(range(8)))
    return [r["out"] for r in res.results]


def kernel(**inputs):
    outs = run(inputs, "all")
    full = np.zeros((2, 8192, D), np.float32)
    for core in range(8):
        b, j = core // 4, core % 4
        full[b, j * 2048:(j + 1) * 2048] = outs[core][OWN[j] * 64:OWN[j] * 64 + 2048]
    return full
```
